# Optimizing a Trainium2 kernel written in Bass

```python
import math
import jax
import jax.numpy as jnp
from jax import lax
import numpy as np

D_MODEL = 2048
BATCH = 32
SEQ = 256
DEPTH = 4
DEC_BATCH = 8
DEC_SEQ = 4096
PAST_LEN = 256

GRID_W = 64
N_EVEN = (DEPTH + 1) // 2
N_ODD = DEPTH // 2
EPS = 1e-6
A_HEADS = 8
QK_NOPE = 128
QK_ROPE = 64
V_HEAD = 128
D_CQ = 512
D_CKV = 512
ROPE_BASE = 10000.0
Q_BLOCK = 128
B_HEADS = 8
DK = 128
DV = 128
CONV_W = 3
CHUNK = 64
W_A = A_HEADS * V_HEAD
W_B = B_HEADS * DV
MIX_W = W_A + W_B
QKV_W = B_HEADS * (2 * DK + DV)
AB_W = 4 * B_HEADS
IN_EVEN = D_CQ + D_CKV + QK_ROPE + QKV_W + AB_W + MIX_W
EVEN_SPLITS = (D_CQ, D_CQ + D_CKV, D_CQ + D_CKV + QK_ROPE, D_CQ + D_CKV + QK_ROPE + QKV_W,
               D_CQ + D_CKV + QK_ROPE + QKV_W + AB_W)
POOL_W = D_MODEL
POOL_WINDOWS = (2, 4, 8, 16)
POOL_GROUPS = 4
POOL_G = POOL_W // POOL_GROUPS

kernel_name = 'hybrid_mla_gdn_pool_diffusion_step'


def rms_norm(x, g):
    xf = x.astype(jnp.float32)
    y = xf * lax.rsqrt(jnp.mean(xf * xf, axis=-1, keepdims=True) + EPS)
    return (y * g.astype(jnp.float32)).astype(x.dtype)


def l2_normalize(x):
    xf = x.astype(jnp.float32)
    return xf * lax.rsqrt(jnp.sum(xf * xf, axis=-1, keepdims=True) + EPS)


def modulation(cond, w, b):
    m = jnp.einsum('bd,de->be', jax.nn.silu(cond), w) + b
    return jnp.split(m[:, None, :], 3, axis=-1)


def axial_rope_tables(n_tokens):
    rows = n_tokens // GRID_W
    row_pos = jnp.repeat(jnp.arange(rows), GRID_W).astype(jnp.float32)
    col_pos = jnp.tile(jnp.arange(GRID_W), rows).astype(jnp.float32)
    half = QK_ROPE // 2
    inv_freq = ROPE_BASE ** (-jnp.arange(0, half, 2, dtype=jnp.float32) / half)
    ang = jnp.concatenate([row_pos[:, None] * inv_freq, col_pos[:, None] * inv_freq], axis=-1)
    return jnp.cos(ang), jnp.sin(ang)


def _rotate(x, cos, sin):
    x1, x2 = jnp.split(x, 2, axis=-1)
    return jnp.concatenate([x1 * cos - x2 * sin, x2 * cos + x1 * sin], axis=-1)


def apply_axial_rope(x, cos, sin):
    xr, xc = jnp.split(x.astype(jnp.float32), 2, axis=-1)
    cr, cc = jnp.split(cos, 2, axis=-1)
    sr, sc = jnp.split(sin, 2, axis=-1)
    return jnp.concatenate([_rotate(xr, cr, sr), _rotate(xc, cc, sc)], axis=-1).astype(x.dtype)


def blocked_attention(q, k, v):
    B, Lq, H, Dqk = q.shape
    nb = Lq // Q_BLOCK
    qb = jnp.moveaxis(q.reshape(B, nb, Q_BLOCK, H, Dqk), 1, 0)
    scale = Dqk ** -0.5

    def one_block(qi):
        s = jnp.einsum('bqhd,bkhd->bhqk', qi, k, preferred_element_type=jnp.float32) * scale
        p = jax.nn.softmax(s, axis=-1).astype(v.dtype)
        return jnp.einsum('bhqk,bkhd->bqhd', p, v)

    o = lax.map(one_block, qb)
    return jnp.moveaxis(o, 0, 1).reshape(B, Lq, H, v.shape[-1])


def short_conv(x, w):
    return lax.conv_general_dilated(x, w[:, None, :].astype(x.dtype), window_strides=(1,),
                                    padding=[(CONV_W // 2, CONV_W // 2)],
                                    dimension_numbers=('NWC', 'WIO', 'NWC'),
                                    feature_group_count=x.shape[-1])


def gated_delta_chunked(q, k, v, g, beta, s0):
    B, H, L, _ = q.shape
    n = L // CHUNK
    dv = v.shape[-1]

    def chunks(t):
        return t.reshape(B, H, n, CHUNK, *t.shape[3:])

    q, k, v, g, beta = chunks(q), chunks(k), chunks(v), chunks(g), chunks(beta)
    G = jnp.cumsum(g, axis=-1)
    lower = jnp.tril(jnp.ones((CHUNK, CHUNK), bool))
    strict = jnp.tril(jnp.ones((CHUNK, CHUNK), bool), -1)
    gamma = jnp.exp(jnp.where(lower, G[..., :, None] - G[..., None, :], -jnp.inf))
    kb = k * beta[..., None]
    a = jnp.einsum('bhncd,bhnsd->bhncs', kb, k) * jnp.where(strict, gamma, 0.0)
    eye = jnp.eye(CHUNK, dtype=jnp.float32)
    rhs = jnp.concatenate([v * beta[..., None], kb * jnp.exp(G)[..., None]], axis=-1)
    sol = lax.linalg.triangular_solve(eye + a, rhs, left_side=True, lower=True, unit_diagonal=True)
    u, w = sol[..., :dv], sol[..., dv:]
    intra = jnp.einsum('bhncd,bhnsd->bhncs', q, k) * gamma
    q_dec = q * jnp.exp(G)[..., None]
    k_dec = k * jnp.exp(G[..., -1:] - G)[..., None]
    chunk_decay = jnp.exp(G[..., -1])
    xs = (jnp.moveaxis(u, 2, 0), jnp.moveaxis(w, 2, 0), jnp.moveaxis(q_dec, 2, 0),
          jnp.moveaxis(k_dec, 2, 0), jnp.moveaxis(intra, 2, 0), jnp.moveaxis(chunk_decay, 2, 0))

    def step(S, inp):
        u_c, w_c, qd_c, kd_c, intra_c, dec_c = inp
        v_new = u_c - jnp.einsum('bhcd,bhde->bhce', w_c, S)
        o_c = jnp.einsum('bhcd,bhde->bhce', qd_c, S) + jnp.einsum('bhcs,bhse->bhce', intra_c, v_new)
        S = S * dec_c[..., None, None] + jnp.einsum('bhcd,bhce->bhde', kd_c, v_new)
        return S, o_c

    s_final, o = lax.scan(step, s0, xs)
    return jnp.moveaxis(o, 0, 2).reshape(B, H, L, dv), s_final


def gdn_bidirectional(q, k, v, g_f, beta_f, g_b, beta_b, s0_f, s0_b):
    o_f, s_f = gated_delta_chunked(q, k, v, g_f, beta_f, s0_f)
    flip = lambda t: jnp.flip(t, axis=2)
    o_b, s_b = gated_delta_chunked(flip(q), flip(k), flip(v), flip(g_b), flip(beta_b), s0_b)
    return o_f + flip(o_b), s_f, s_b


def even_layer_inputs(h, lw):
    B, L, _ = h.shape
    proj = jnp.einsum('bld,de->ble', h, lw['w_in'])
    c_q, c_kv, k_pe, qkv, ab, z = jnp.split(proj, EVEN_SPLITS, axis=-1)
    q = jnp.einsum('blc,ce->ble', rms_norm(c_q, lw['q_norm']), lw['w_uq']).reshape(B, L, A_HEADS, QK_NOPE + QK_ROPE)
    ckv = rms_norm(c_kv, lw['kv_norm'])
    qkv = jax.nn.silu(short_conv(qkv, lw['conv']))
    gq, gk, gv = jnp.split(qkv, [B_HEADS * DK, 2 * B_HEADS * DK], axis=-1)
    heads = lambda t, d: jnp.transpose(t.reshape(B, L, B_HEADS, d), (0, 2, 1, 3))
    gq = l2_normalize(heads(gq, DK)) * DK ** -0.5
    gk = l2_normalize(heads(gk, DK))
    gv = heads(gv, DV).astype(jnp.float32)
    ab = jnp.transpose(ab.astype(jnp.float32).reshape(B, L, 4, B_HEADS), (2, 0, 3, 1))
    a_log = lw['a_log'].astype(jnp.float32)
    dt_bias = lw['dt_bias'].astype(jnp.float32)
    g = -jnp.exp(a_log)[:, None, :, None] * jax.nn.softplus(ab[:2] + dt_bias[:, None, :, None])
    beta = jax.nn.sigmoid(ab[2:])
    return {'q_nope': q[..., :QK_NOPE], 'q_pe': q[..., QK_NOPE:], 'ckv': ckv, 'k_pe': k_pe,
            'gq': gq, 'gk': gk, 'gv': gv, 'g': g, 'beta': beta, 'z': z}


def mla_kv(ckv, k_pe, w_ukv):
    B, L, _ = ckv.shape
    kv = jnp.einsum('blc,ce->ble', ckv, w_ukv).reshape(B, L, A_HEADS, QK_NOPE + V_HEAD)
    k = jnp.concatenate([kv[..., :QK_NOPE], jnp.broadcast_to(k_pe[:, :, None, :], (B, L, A_HEADS, QK_ROPE))], axis=-1)
    return k, kv[..., QK_NOPE:]


def even_layer_output(attn, gdn_o, z, lw):
    B, L = z.shape[:2]
    gdn_o = rms_norm(jnp.transpose(gdn_o, (0, 2, 1, 3)), lw['o_norm']).reshape(B, L, W_B).astype(z.dtype)
    y = jnp.concatenate([attn.reshape(B, L, W_A).astype(z.dtype), gdn_o], axis=-1) * jax.nn.silu(z)
    return jnp.einsum('ble,ed->bld', y, lw['w_out'])


def even_context(h, lw):
    B = h.shape[0]
    p = even_layer_inputs(h, lw)
    q = jnp.concatenate([p['q_nope'], p['q_pe']], axis=-1)
    k, v = mla_kv(p['ckv'], p['k_pe'], lw['w_ukv'])
    attn = blocked_attention(q, k, v)
    s0 = jnp.zeros((B, B_HEADS, DK, DV), jnp.float32)
    o, s_f, s_b = gdn_bidirectional(p['gq'], p['gk'], p['gv'], p['g'][0], p['beta'][0], p['g'][1], p['beta'][1], s0, s0)
    return even_layer_output(attn, o, p['z'], lw), p['ckv'], p['k_pe'], s_f, s_b


def even_latent(h, ckv_ctx, kpe_ctx, sf_ctx, sb_ctx, rope_cos, rope_sin, lw):
    p = even_layer_inputs(h, lw)
    q_pe = apply_axial_rope(p['q_pe'], rope_cos[:, None, :], rope_sin[:, None, :])
    k_pe = apply_axial_rope(p['k_pe'], rope_cos, rope_sin)
    q = jnp.concatenate([p['q_nope'], q_pe], axis=-1)
    k_lat, v_lat = mla_kv(p['ckv'], k_pe, lw['w_ukv'])
    k_ctx, v_ctx = mla_kv(ckv_ctx.astype(h.dtype), kpe_ctx.astype(h.dtype), lw['w_ukv'])
    k = jnp.concatenate([k_ctx, k_lat], axis=1)
    v = jnp.concatenate([v_ctx, v_lat], axis=1)
    attn = blocked_attention(q, k, v)
    o, _, _ = gdn_bidirectional(p['gq'], p['gk'], p['gv'], p['g'][0], p['beta'][0], p['g'][1], p['beta'][1],
                                sf_ctx.astype(jnp.float32), sb_ctx.astype(jnp.float32))
    return even_layer_output(attn, o, p['z'], lw)


def multiscale_pool(x):
    B, L, _ = x.shape
    xg = x.astype(jnp.float32).reshape(B, L, POOL_GROUPS, POOL_G)
    cs = jnp.concatenate([jnp.zeros((B, 1, POOL_GROUPS, POOL_G), jnp.float32), jnp.cumsum(xg, axis=1)], axis=1)
    t = jnp.arange(L)
    outs = []
    for gi, w in enumerate(POOL_WINDOWS):
        lo = jnp.clip(t - w // 2, 0, L)
        hi = jnp.clip(t + (w - w // 2), 0, L)
        mean = (cs[:, hi, gi] - cs[:, lo, gi]) / (hi - lo).astype(jnp.float32)[:, None]
        outs.append(mean - xg[:, :, gi])
    return jnp.stack(outs, axis=2).astype(x.dtype)


def pool_mixer(h, w_in, w_pool, pool_scale, w_out):
    B, L, _ = h.shape
    pin, z = jnp.split(jnp.einsum('bld,de->ble', h, w_in), 2, axis=-1)
    pooled = multiscale_pool(pin)
    mixed = jnp.einsum('blgc,gce->blge', pooled, w_pool).reshape(B, L, POOL_W) * pool_scale
    return jnp.einsum('ble,ed->bld', mixed * jax.nn.silu(z), w_out)


def setup_inputs(seed: int = 0) -> dict:
    key = jax.random.key(seed)
    ks = iter(jax.random.split(key, 40))
    f32 = jnp.float32

    def nrm(shape, s):
        return jax.random.normal(next(ks), shape, f32) * s

    dt = jnp.exp(jax.random.uniform(next(ks), (N_EVEN, 2, B_HEADS), f32, math.log(1e-3), math.log(1e-1)))
    return {
        'x_prompt': nrm((BATCH, SEQ, D_MODEL), 1.0),
        'x_sample': nrm((DEC_BATCH, DEC_SEQ, D_MODEL), 1.0),
        'cache_ckv': nrm((DEC_BATCH, N_EVEN, PAST_LEN, D_CKV), 1.0),
        'cache_kpe': nrm((DEC_BATCH, N_EVEN, PAST_LEN, QK_ROPE), 1.0),
        'state_fwd': nrm((DEC_BATCH, N_EVEN, B_HEADS, DK, DV), 0.1),
        'state_bwd': nrm((DEC_BATCH, N_EVEN, B_HEADS, DK, DV), 0.1),
        'c': nrm((DEC_BATCH, D_MODEL), 1.0),
        'c_ctx': nrm((D_MODEL,), 1.0),
        'ln_e': 1.0 + nrm((N_EVEN, D_MODEL), 0.02),
        'mod_w_e': nrm((N_EVEN, D_MODEL, 3 * D_MODEL), 0.5 * D_MODEL ** -0.5),
        'mod_b_e': nrm((N_EVEN, 3 * D_MODEL), 0.02),
        'w_in_e': nrm((N_EVEN, D_MODEL, IN_EVEN), D_MODEL ** -0.5),
        'q_norm_e': 1.0 + nrm((N_EVEN, D_CQ), 0.02),
        'kv_norm_e': 1.0 + nrm((N_EVEN, D_CKV), 0.02),
        'w_uq_e': nrm((N_EVEN, D_CQ, A_HEADS * (QK_NOPE + QK_ROPE)), D_CQ ** -0.5),
        'w_ukv_e': nrm((N_EVEN, D_CKV, A_HEADS * (QK_NOPE + V_HEAD)), D_CKV ** -0.5),
        'conv_e': nrm((N_EVEN, CONV_W, QKV_W), CONV_W ** -0.5),
        'a_log_e': jnp.log(jax.random.uniform(next(ks), (N_EVEN, 2, B_HEADS), f32, 1.0, 16.0)),
        'dt_bias_e': dt + jnp.log(-jnp.expm1(-dt)),
        'o_norm_e': 1.0 + nrm((N_EVEN, DV), 0.02),
        'w_out_e': nrm((N_EVEN, MIX_W, D_MODEL), MIX_W ** -0.5),
        'ln_o': 1.0 + nrm((N_ODD, D_MODEL), 0.02),
        'mod_w_o': nrm((N_ODD, D_MODEL, 3 * D_MODEL), 0.5 * D_MODEL ** -0.5),
        'mod_b_o': nrm((N_ODD, 3 * D_MODEL), 0.02),
        'w_in_o': nrm((N_ODD, D_MODEL, 2 * POOL_W), D_MODEL ** -0.5),
        'w_pool_o': nrm((N_ODD, POOL_GROUPS, POOL_G, POOL_G), POOL_G ** -0.5),
        'pool_scale_o': 1.0 + nrm((N_ODD, POOL_W), 0.1),
        'w_out_o': nrm((N_ODD, POOL_W, D_MODEL), POOL_W ** -0.5),
        'final_norm': 1.0 + nrm((D_MODEL,), 0.02),
    }


def reference(x_prompt, x_sample, cache_ckv, cache_kpe, state_fwd, state_bwd, c, c_ctx,
              ln_e, mod_w_e, mod_b_e, w_in_e, q_norm_e, kv_norm_e, w_uq_e, w_ukv_e, conv_e,
              a_log_e, dt_bias_e, o_norm_e, w_out_e,
              ln_o, mod_w_o, mod_b_o, w_in_o, w_pool_o, pool_scale_o, w_out_o, final_norm):
    rope_cos, rope_sin = axial_rope_tables(x_sample.shape[1])
    cond_ctx = c_ctx[None, :]
    xp, xs = x_prompt, x_sample
    ckv_out, kpe_out, sf_out, sb_out = [], [], [], []
    for layer in range(DEPTH):
        i = layer // 2
        if layer % 2 == 0:
            lw = {'w_in': w_in_e[i], 'q_norm': q_norm_e[i], 'kv_norm': kv_norm_e[i], 'w_uq': w_uq_e[i],
                  'w_ukv': w_ukv_e[i], 'conv': conv_e[i], 'a_log': a_log_e[i], 'dt_bias': dt_bias_e[i],
                  'o_norm': o_norm_e[i], 'w_out': w_out_e[i]}
            sh_p, sc_p, gt_p = modulation(cond_ctx, mod_w_e[i], mod_b_e[i])
            sh_s, sc_s, gt_s = modulation(c, mod_w_e[i], mod_b_e[i])
            hp = rms_norm(xp, ln_e[i]) * (1.0 + sc_p) + sh_p
            hs = rms_norm(xs, ln_e[i]) * (1.0 + sc_s) + sh_s
            out_p, ckv, kpe, s_f, s_b = even_context(hp, lw)
            out_s = even_latent(hs, cache_ckv[:, i], cache_kpe[:, i], state_fwd[:, i], state_bwd[:, i],
                                rope_cos, rope_sin, lw)
            ckv_out.append(ckv)
            kpe_out.append(kpe)
            sf_out.append(s_f)
            sb_out.append(s_b)
        else:
            sh_p, sc_p, gt_p = modulation(cond_ctx, mod_w_o[i], mod_b_o[i])
            sh_s, sc_s, gt_s = modulation(c, mod_w_o[i], mod_b_o[i])
            hp = rms_norm(xp, ln_o[i]) * (1.0 + sc_p) + sh_p
            hs = rms_norm(xs, ln_o[i]) * (1.0 + sc_s) + sh_s
            out_p = pool_mixer(hp, w_in_o[i], w_pool_o[i], pool_scale_o[i], w_out_o[i])
            out_s = pool_mixer(hs, w_in_o[i], w_pool_o[i], pool_scale_o[i], w_out_o[i])
        xp = xp + gt_p * out_p
        xs = xs + gt_s * out_s
    y_prompt = rms_norm(xp, final_norm)
    y_sample = rms_norm(xs, final_norm)
    new_cache_ckv = jnp.stack(ckv_out, axis=1)
    new_cache_kpe = jnp.stack(kpe_out, axis=1)
    new_state_fwd = jnp.stack(sf_out, axis=1)
    new_state_bwd = jnp.stack(sb_out, axis=1)
    return (y_prompt, y_sample, new_cache_ckv, new_cache_kpe, new_state_fwd, new_state_bwd)
```

```python
import numpy as np
import ml_dtypes
from contextlib import ExitStack
import concourse.bass as bass
import concourse.mybir as mybir
from concourse.bass_utils import run_bass_kernel_spmd

F32 = mybir.dt.float32
BF16 = mybir.dt.bfloat16
AF = mybir.ActivationFunctionType
ALU = mybir.AluOpType
AX = mybir.AxisListType

D = 2048
NH = 8
IN_EVEN = 6240
EPS = 1e-6
NEG = -1.0e9


class Res:
    __slots__ = ("w", "r", "excl")

    def __init__(self, excl=False):
        self.w = None
        self.r = {}
        self.excl = excl


class Tile:
    def __init__(self, t, res=None):
        self.t = t
        self.res = res if res is not None else Res()

    def __getitem__(self, k):
        return self.t[k]


class Ctx:
    def __init__(self, nc):
        self.nc = nc
        self.es = ExitStack()
        self.eng = {"pe": nc.tensor, "act": nc.scalar, "dve": nc.vector, "pool": nc.gpsimd, "sp": nc.sync}
        self.sems = {}
        self.count = {}
        self.seen = {e: {} for e in self.eng}
        for e in ("pe", "act", "dve", "pool"):
            self._sem(e)
        self.resd = {}
        self.n_ins = 0
        self.dead = False
        self.ring_n = {}

    def _sem(self, name):
        if name not in self.sems:
            self.sems[name] = self.es.enter_context(self.nc.semaphore("s_" + name))
            self.count[name] = 0
        return self.sems[name]

    def res(self, key):
        r = self.resd.get(key)
        if r is None:
            r = self.resd[key] = Res()
        return r

    def sb(self, name, shape, dt, stack=None):
        self.n_sb = getattr(self, "n_sb", 0) + 1
        name = f"{name}_u{self.n_sb}"
        t = (stack or self.es).enter_context(self.nc.sbuf_tensor(name, list(shape), dt))
        return Tile(t)

    def _rs(self, x):
        return x.res if isinstance(x, Tile) else x

    def _waits(self, e, R, W):
        need = {}
        for r in R:
            r = self._rs(r)
            if r.w is not None:
                s, v = r.w
                if need.get(s, 0) < v:
                    need[s] = v
            if r.excl:
                for s, v in r.r.items():
                    if s != e and need.get(s, 0) < v:
                        need[s] = v
        for w in W:
            w = self._rs(w)
            if w.w is not None:
                s, v = w.w
                if need.get(s, 0) < v:
                    need[s] = v
            for s, v in w.r.items():
                if need.get(s, 0) < v:
                    need[s] = v
        seen = self.seen[e]
        for s, v in need.items():
            if e == "pe" and s == "pe":
                continue
            if seen.get(s, 0) < v:
                self.eng[e].wait_ge(self.sems[s], v)
                seen[s] = v

    def _mark(self, ticket, R, W):
        s, v = ticket
        for r in R:
            r = self._rs(r)
            if r.r.get(s, 0) < v:
                r.r[s] = v
        for w in W:
            w = self._rs(w)
            w.w = ticket
            w.r = {}

    def op(self, e, emit, R=(), W=()):
        if self.dead:
            return None
        self._waits(e, R, W)
        ins = emit(self.eng[e])
        self.count[e] += 1
        ins.then_inc(self.sems[e], 1)
        self._mark((e, self.count[e]), R, W)
        self.n_ins += 1
        return ins

    RING = 8

    def dma(self, q, stream, out, in_, R=(), W=(), **kw):
        if self.dead:
            return None
        n = self.ring_n.get(stream, 0)
        self.ring_n[stream] = n + 1
        sname = f"{stream}_{n % self.RING}"
        self._sem(sname)
        prev = self.count[sname]
        if prev > 0 and self.seen[q].get(sname, 0) < prev:
            self.eng[q].wait_ge(self.sems[sname], prev)
            self.seen[q][sname] = prev
        self._waits(q, R, W)
        ins = self.eng[q].dma_start(out=out, in_=in_, **kw)
        self.count[sname] += 16
        ins.then_inc(self.sems[sname], 16)
        self._mark((sname, self.count[sname]), R, W)
        self.n_ins += 1

    def barrier(self):
        if self.dead:
            self.dead = False
        for e in self.eng:
            seen = self.seen[e]
            for s, v in self.count.items():
                if e == "pe" and s == "pe":
                    continue
                if v > 0 and seen.get(s, 0) < v:
                    self.eng[e].wait_ge(self.sems[s], v)
                    seen[s] = v

    def mm(self, out, lhsT, rhs, start, stop, R, W):
        return self.op("pe", lambda e: e.matmul(out, lhsT=lhsT, rhs=rhs, start=start, stop=stop), R, W)

    def act(self, out, in_, func, R, W, bias=None, scale=None, accum=None, eng="act"):
        kw = {}
        if bias is not None:
            kw["bias"] = bias
        if scale is not None:
            kw["scale"] = scale
        if accum is not None:
            kw["accum_out"] = accum
        return self.op(eng, lambda e: e.activation(out=out, in_=in_, func=func, **kw), R, W)

    def copy(self, eng, out, in_, R, W):
        if eng == "pool":
            eng = "dve"
        if eng == "act":
            return self.op("act", lambda e: e.copy(out=out, in_=in_), R, W)
        return self.op(eng, lambda e: e.tensor_copy(out=out, in_=in_), R, W)

    def tt(self, eng, out, in0, in1, op, R, W):
        if eng == "pool":
            eng = "dve"
        return self.op(eng, lambda e: e.tensor_tensor(out=out, in0=in0, in1=in1, op=op), R, W)

    def ts(self, eng, out, in0, s1, op0, R, W, s2=None, op1=None):
        if eng == "pool":
            eng = "dve"
        if op1 is None:
            return self.op(eng, lambda e: e.tensor_scalar(out=out, in0=in0, scalar1=s1, scalar2=None, op0=op0), R, W)
        return self.op(eng, lambda e: e.tensor_scalar(out=out, in0=in0, scalar1=s1, scalar2=s2, op0=op0, op1=op1), R, W)

    def stt(self, eng, out, in0, scalar, in1, op0, op1, R, W):
        eng = "dve"
        return self.op(eng, lambda e: e.scalar_tensor_tensor(out=out, in0=in0, scalar=scalar, in1=in1, op0=op0, op1=op1), R, W)


class StopBuild(Exception):
    pass


class Cfg:
    def __init__(self, nps=4, lp=256, ls=4096, lc=256, depth=4, debug=False):
        self.nps, self.lp, self.ls, self.lc, self.depth, self.debug = nps, lp, ls, lc, depth, debug
        self.n_even = (depth + 1) // 2
        self.n_odd = depth // 2
        self.stop = None
        self.neu32 = True

    def chk(self, k):
        if self.stop == k:
            self.ctx.dead = True


def build(cfg):
    nc = bass.Bass("TRN2", target_bir_lowering=False)
    c = Ctx(nc)
    cfg.ctx = c
    NPS, LP, LS, LC = cfg.nps, cfg.lp, cfg.ls, cfg.lc
    NE, NO = cfg.n_even, cfg.n_odd
    NPT = NPS * LP

    def din(name, shape, dt=F32):
        return nc.dram_tensor(name, list(shape), dt, kind="ExternalInput").ap()

    def dout(name, shape, dt=F32):
        return nc.dram_tensor(name, list(shape), dt, kind="ExternalOutput").ap()

    def dscr(name, shape, dt=F32):
        kind = "ExternalOutput" if cfg.debug else "Internal"
        return nc.dram_tensor(name, list(shape), dt, kind=kind).ap()

    xp = din("xp", [NPT, D])
    xs = din("xs", [LS, D])
    cckv = din("cckv", [2, LC, 512])
    ckpe = din("ckpe", [2, LC, 64])
    sfw = din("sfw", [2, NH, 128, 128])
    sbw = din("sbw", [2, NH, 128, 128])
    cond = din("cond", [2, D])
    ln_e = din("ln_e", [2, D]); mod_w_e = din("mod_w_e", [2, D, 3 * D]); mod_b_e = din("mod_b_e", [2, 3 * D])
    w_in_e = din("w_in_e", [2, D, IN_EVEN]); q_norm_e = din("q_norm_e", [2, 512]); kv_norm_e = din("kv_norm_e", [2, 512])
    w_uq_e = din("w_uq_e", [2, 512, 1536]); w_ukv_e = din("w_ukv_e", [2, 512, 2048]); conv_e = din("conv_e", [2, 3, 3072])
    a_log_e = din("a_log_e", [2, 16]); dt_bias_e = din("dt_bias_e", [2, 16]); o_norm_e = din("o_norm_e", [2, 128])
    w_out_e = din("w_out_e", [2, D, D])
    ln_o = din("ln_o", [2, D]); mod_w_o = din("mod_w_o", [2, D, 3 * D]); mod_b_o = din("mod_b_o", [2, 3 * D])
    w_in_o = din("w_in_o", [2, D, 2 * D]); w_pool_o = din("w_pool_o", [2, 4, 512, 512]); pool_scale_o = din("pool_scale_o", [2, D])
    w_out_o = din("w_out_o", [2, D, D]); final_norm = din("final_norm", [D])
    k_ident = din("k_ident", [128, 128])
    k_masks = din("k_masks", [9, 128, 128])
    k_cum = din("k_cum", [2, 128, 128])
    k_rope = din("k_rope", [2, 64, LS])
    k_pinv_p = din("k_pinv_p", [4, LP]); k_pinv_s = din("k_pinv_s", [4, LS])

    yp = dout("yp", [NPT, D]); ys = dout("ys", [LS, D])
    nckv = dout("nckv", [NPS, 2, LP, 512]); nkpe = dout("nkpe", [NPS, 2, LP, 64])
    nsf = dout("nsf", [NPS, 2, NH, 128, 128]); nsb = dout("nsb", [NPS, 2, NH, 128, 128])

    xw_p = dscr("xw_p", [NPT, D]); xw_s = dscr("xw_s", [LS, D])
    modv = dscr("modv", [4, 2, 3 * D])
    wb_in_e = dscr("wb_in_e", [NE, D, IN_EVEN], BF16); wb_uq = dscr("wb_uq", [NE, 512, 1536], BF16)
    wb_ukv = dscr("wb_ukv", [NE, 512, 2048], BF16); wb_out_e = dscr("wb_out_e", [NE, D, D], BF16)
    wb_in_o = dscr("wb_in_o", [max(NO, 1), D, 2 * D], BF16); wb_pool = dscr("wb_pool", [max(NO, 1), 4, 512, 512], BF16)
    wb_out_o = dscr("wb_out_o", [max(NO, 1), D, D], BF16)
    LT = NPT + LS
    LKS = LC + LS
    qn_s = dscr("qn_s", [NH, 128, LT], BF16); qr_s = dscr("qr_s", [NH, 64, LT], BF16)
    kn_p = dscr("kn_p", [NH, 128, NPT], BF16); kr_p = dscr("kr_p", [64, NPT], BF16); v_p = dscr("v_p", [NPT, 1024], BF16)
    kn_x = dscr("kn_x", [NH, 128, LKS], BF16); kr_x = dscr("kr_x", [64, LKS], BF16); v_x = dscr("v_x", [LKS, 1024], BF16)
    qkv_s = dscr("qkv_s", [3072, LT]); ab_s = dscr("ab_s", [LT, 32]); abT_s = dscr("abT_s", [32, LT])
    sz_s = dscr("sz_s", [D, LT], BF16); yT_s = dscr("yT_s", [D, LT], BF16)
    gq_s = dscr("gq_s", [NH, 128, LT], BF16); gk_s = dscr("gk_s", [NH, 128, LT], BF16); gv_s = dscr("gv_s", [NH, 128, LT], BF16)
    gkb_s = dscr("gkb_s", [2, NH, 128, LT], BF16)
    og_s = dscr("og_s", [LT, 1024])
    pin_s = dscr("pin_s", [D, LT])

    seqs = [dict(t0=i * LP, L=LP, ci=0, kind="p", idx=i) for i in range(NPS)] + [dict(t0=NPT, L=LS, ci=1, kind="s", idx=0)]

    def xrows(src_first, t0, n):
        if t0 < NPT:
            return (xp if src_first else xw_p)[t0:t0 + n, :]
        return (xs if src_first else xw_s)[t0 - NPT:t0 - NPT + n, :]

    dbg_n = [0]
    layer_now = [0]

    def dbg(name, ap, shape, dt=F32, R=()):
        if not cfg.debug or layer_now[0] != 0:
            return
        t = nc.dram_tensor("dbg_" + name, list(shape), dt, kind="ExternalOutput").ap()
        c.dma("pool", "st2", t, ap, R=list(R))

    ident = c.sb("ident", [128, 128], F32)
    identb = c.sb("identb", [128, 128], BF16)
    ones32 = c.sb("ones32", [128, 128], F32)
    onesb = c.sb("onesb", [128, 128], BF16)
    epsT = c.sb("epsT", [128, 1], F32)
    oneT = c.sb("oneT", [128, 1], F32)
    c.dma("sp", "ld0", ident[:], k_ident, W=[ident])
    c.op("pool", lambda e: e.memset(ones32[:], 1.0), W=[ones32])
    c.op("pool", lambda e: e.memset(onesb[:], 1.0), W=[onesb])
    c.op("pool", lambda e: e.memset(epsT[:], EPS), W=[epsT])
    c.op("pool", lambda e: e.memset(oneT[:], 1.0), W=[oneT])
    c.copy("dve", identb[:], ident[:], R=[ident], W=[identb])

    banks = [c.es.enter_context(nc.psum_tensor(f"ps{i}", [128, 512], F32)) for i in range(8)]
    bank_res = [Res(excl=True) for _ in range(8)]

    class PS:
        def __init__(self, b, c0, n):
            self.b, self.c0, self.n = b, c0, n
            self.rs = [bank_res[b]]

        def ap(self, p=128, n=None, off=0):
            n = self.n if n is None else n
            return banks[self.b][0:p, self.c0 + off:self.c0 + off + n]

    st = {"slot": 0}

    def pbank(b=None):
        if b is None:
            s = (st["slot"] + 3) // 4 * 4 % 32
            st["slot"] = (s + 4) % 32
            b = s // 4
        return PS(b, 0, 512)

    def palign():
        st["slot"] = (st["slot"] + 3) // 4 * 4 % 32

    def pslots4():
        palign()
        return [pslot() for _ in range(4)]

    def pslot():
        s = st["slot"]
        st["slot"] = (s + 1) % 32
        return PS(s // 4, (s % 4) * 128, 128)

    r_w = c.res("wcast")

    def wcast(dst, src, rows_per=512):
        n = src.shape[0]
        for r0 in range(0, n, rows_per):
            r1 = min(n, r0 + rows_per)
            c.dma("pool", "wc", dst[r0:r1], src[r0:r1], W=[r_w])

    for i in range(NE):
        wcast(wb_in_e[i], w_in_e[i]); wcast(wb_uq[i], w_uq_e[i]); wcast(wb_ukv[i], w_ukv_e[i]); wcast(wb_out_e[i], w_out_e[i])
    for i in range(NO):
        wcast(wb_in_o[i], w_in_o[i]); wcast(wb_out_o[i], w_out_o[i])
        for g in range(4):
            wcast(wb_pool[i, g], w_pool_o[i, g])

    r_modv = c.res("modv")
    with ExitStack() as ph:
        condT = c.sb("condT", [128, 16, 2], F32, ph)
        scT = c.sb("scT", [128, 16, 2], F32, ph)
        sgT = c.sb("sgT", [128, 16, 2], F32, ph)
        for ci_ in range(2):
            c.dma("sp", "ld0", condT[:, :, ci_], cond[ci_].rearrange("(k p) -> p k", p=128), W=[condT], allow_slow_non_contiguous=True)
        c.act(scT[:], condT[:], AF.Exp, R=[condT], W=[scT], scale=-1.0)
        c.ts("dve", scT[:], scT[:], 1.0, ALU.add, R=[scT], W=[scT])
        c.op("dve", lambda e: e.reciprocal(out=scT[:], in_=scT[:]), R=[scT], W=[scT])
        c.tt("dve", sgT[:], condT[:], scT[:], ALU.mult, R=[condT, scT], W=[sgT])
        wts = [c.sb(f"mw{i}", [128, 4, 512], F32, ph) for i in range(3)]
        mb = c.sb("mb", [2, 3 * D], F32, ph)
        mo = c.sb("mo", [2, 3 * D], F32, ph)
        nld = 0
        for layer in range(cfg.depth):
            i = layer // 2
            mw = (mod_w_e if layer % 2 == 0 else mod_w_o)[i]
            mbv = (mod_b_e if layer % 2 == 0 else mod_b_o)[i]
            c.dma("sp", "ld0", mb[:], mbv.partition_broadcast(2), W=[mb], R=[])
            for n in range(12):
                pb = pbank()
                for kg in range(4):
                    wt = wts[nld % 3]; nld += 1
                    c.dma("sp", "ldw", wt[:], mw[kg * 512:(kg + 1) * 512, n * 512:(n + 1) * 512].rearrange("(k p) e -> p k e", p=128), W=[wt])
                    for kk in range(4):
                        k = kg * 4 + kk
                        c.mm(pb.ap(2), sgT[:, k, :], wt[:, kk, :], k == 0, k == 15, R=[sgT, wt], W=pb.rs)
                c.tt("dve", mo[:, n * 512:(n + 1) * 512], pb.ap(2), mb[:, n * 512:(n + 1) * 512], ALU.add, R=pb.rs + [mb], W=[mo])
            c.dma("sp", "st0", modv[layer], mo[:], R=[mo], W=[r_modv])
    c.barrier()

    def load_cols(ph, name, vec, nchunk, q="sp"):
        t = c.sb(name, [128, nchunk], F32, ph)
        c.dma(q, "ld0", t[:], vec.rearrange("(k p) -> p k", p=128), W=[t], R=[r_modv], allow_slow_non_contiguous=True)
        return t

    def norm_transpose(ph_tiles, src_first, t0, n, A, B, hT):
        xt, junk, ss = ph_tiles
        for j in range(n // 128):
            xn = xt[j % 2]
            c.dma("sp", "ldx", xn[:], xrows(src_first, t0 + j * 128, 128), R=[c.res(("x", (t0 + j * 128) // 128))], W=[xn])
            c.act(junk[:].rearrange("p a b -> p (a b)")[:, 0:D], xn[:], AF.Square, R=[xn], W=[junk, ss], accum=ss[:, 0:1])
            c.act(ss[:, 1:2], ss[:, 0:1], AF.Ln, R=[ss, epsT], W=[ss], bias=epsT[:, 0:1], scale=1.0 / D)
            c.act(ss[:, 2:3], ss[:, 1:2], AF.Exp, R=[ss], W=[ss], scale=-0.5)
            c.ts("dve", xn[:], xn[:], ss[:, 2:3], ALU.mult, R=[xn, ss], W=[xn])
            for kg in range(4):
                pb = pbank()
                for kk in range(4):
                    k = kg * 4 + kk
                    c.op("pe", lambda e: e.transpose(out=pb.ap(128, 128, kk * 128), in_=xn[:, k * 128:(k + 1) * 128], identity=ident[:]), R=[xn, ident], W=pb.rs)
                for kk in range(4):
                    k = kg * 4 + kk
                    eng = "dve" if kk % 2 == 0 else "pool"
                    if eng == "pool":
                        c.act(hT[:, k, j * 128:(j + 1) * 128], pb.ap(128, 128, kk * 128), AF.Identity, R=pb.rs + [A, B], W=[hT],
                              bias=B[:, k:k + 1], scale=A[:, k:k + 1])
                    else:
                        c.ts("dve", hT[:, k, j * 128:(j + 1) * 128], pb.ap(128, 128, kk * 128), A[:, k:k + 1], ALU.mult, R=pb.rs + [A, B], W=[hT],
                             s2=B[:, k:k + 1], op1=ALU.add)

    def mod_cols(ph, layer, lnw_vec):
        lnw = load_cols(ph, f"lnw{layer}", lnw_vec, 16)
        outs = []
        for ci in range(2):
            sh = load_cols(ph, f"sh{layer}_{ci}", modv[layer, ci, 0:D], 16)
            sc = load_cols(ph, f"sc{layer}_{ci}", modv[layer, ci, D:2 * D], 16)
            A = c.sb(f"A{layer}_{ci}", [128, 16], F32, ph)
            c.stt("dve", A[:], sc[:], 1.0, lnw[:], ALU.add, ALU.mult, R=[sc, lnw], W=[A])
            outs.append((A, sh))
        return outs

    def rstd_from_ps(ps_sum, n, inv_n, out_t, tmp_t):
        c.act(tmp_t[:, 0:n], ps_sum.ap(128, n), AF.Ln, R=ps_sum.rs + [epsT], W=[tmp_t], bias=epsT[:, 0:1], scale=inv_n)
        c.act(out_t[:, 0:n], tmp_t[:, 0:n], AF.Exp, R=[tmp_t], W=[out_t], scale=-0.5)

    def even_E1(layer):
        i = layer // 2
        first = layer == 0
        with ExitStack() as ph:
            mods = mod_cols(ph, layer, ln_e[i])
            qg = load_cols(ph, "qg", q_norm_e[i], 4)
            kg_ = load_cols(ph, "kvg", kv_norm_e[i], 4)
            wuq = c.sb("wuq", [128, 4, 1536], BF16, ph)
            wuqp = c.sb("wuqp", [128, 4, 8, 64], BF16, ph)
            wukk = c.sb("wukk", [128, 4, 8, 128], BF16, ph)
            wukv = c.sb("wukv", [128, 4, 8, 128], BF16, ph)
            c.dma("sp", "ld0", wuq[:], wb_uq[i].rearrange("(k p) e -> p k e", p=128), R=[r_w], W=[wuq])
            uqv = wb_uq[i].rearrange("(k p) (h e) -> p k h e", p=128, e=192)
            for blk in range(2):
                for half in range(2):
                    src = uqv[:, :, :, 128 + blk * 32 + (1 - half) * 16:128 + blk * 32 + (1 - half) * 16 + 16]
                    for k in range(4):
                        c.dma("sp", "ld0", wuqp[:, k, :, blk * 32 + half * 16:blk * 32 + half * 16 + 16], src[:, k], R=[r_w], W=[wuqp],
                              allow_slow_non_contiguous=True)
            ukvv = wb_ukv[i].rearrange("(k p) (h t e) -> p k h t e", p=128, t=2, e=128)
            for k in range(4):
                c.dma("sp", "ld0", wukk[:, k], ukvv[:, k, :, 0, :], R=[r_w], W=[wukk])
                c.dma("sp", "ld0", wukv[:, k], ukvv[:, k, :, 1, :], R=[r_w], W=[wukv])
            TB = 512
            hTs = [c.sb(f"hT{b}", [128, 16, TB], BF16, ph) for b in range(2)]
            xt = [c.sb(f"xt{b}", [128, D], F32, ph) for b in range(2)]
            ss = c.sb("ss", [128, 4], F32, ph)
            wg = [c.sb(f"wg{b}", [128, 16, 512], BF16, ph) for b in range(2)]
            wsm = c.sb("wsm", [128, 16, 128 + 32], BF16, ph)
            raw0 = c.sb("raw0", [128, 4, TB], F32, ph)
            raw = [raw0, raw0]
            sq = c.sb("sq", [128, 4, TB], BF16, ph)
            rstd = c.sb("rstd", [128, TB], F32, ph)
            tmpn = c.sb("tmpn", [128, TB], F32, ph)
            cqn = c.sb("cqn", [128, 4, TB], BF16, ph)
            ckn = c.sb("ckn", [128, 4, TB], BF16, ph)
            ckn32 = raw0
            stq = c.sb("stq", [128, 2, TB], BF16, ph)
            stqr = c.sb("stqr", [64, 2, TB], BF16, ph)
            stk = c.sb("stk", [128, 2, TB], BF16, ph)
            stv = [c.sb(f"stv{b}", [128, 1024], BF16, ph) for b in range(2)]
            stkr = c.sb("stkr", [64, TB], BF16, ph)
            kpe32 = c.sb("kpe32", [64, TB], F32, ph)
            st32 = [c.sb(f"st32_{b}", [128, TB], F32, ph) for b in range(3)]
            stz = [c.sb(f"stz{b}", [128, TB], BF16, ph) for b in range(2)]
            stab = c.sb("stab", [128, 32], F32, ph)
            otok = [c.sb(f"otok{b}", [128, 576], F32, ph) for b in range(2)]
            ropec = c.sb("ropec", [64, TB], F32, ph)
            ropes = c.sb("ropes", [64, TB], F32, ph)
            rt1 = c.sb("rt1", [64, TB], F32, ph)
            rt2 = c.sb("rt2", [64, TB], F32, ph)
            wv = wb_in_e[i].rearrange("(k p) e -> p k e", p=128)
            for k4 in range(4):
                c.dma("sp", "ld0", wsm[:, k4 * 4:k4 * 4 + 4, 0:64], wv[:, k4 * 4:k4 * 4 + 4, 1024:1088], R=[r_w], W=[wsm])
            for blk in range(2):
                for half in range(2):
                    c0 = 1024 + blk * 32 + (1 - half) * 16
                    for k4 in range(4):
                        c.dma("sp", "ld0", wsm[:, k4 * 4:k4 * 4 + 4, 64 + blk * 32 + half * 16:64 + blk * 32 + half * 16 + 16],
                              wv[:, k4 * 4:k4 * 4 + 4, c0:c0 + 16], R=[r_w], W=[wsm], allow_slow_non_contiguous=True)
            for k4 in range(4):
                c.dma("sp", "ld0", wsm[:, k4 * 4:k4 * 4 + 4, 128:160], wv[:, k4 * 4:k4 * 4 + 4, 4160:4192], R=[r_w], W=[wsm])
            groups = [(0, 512, "cq"), (512, 512, "ckv")] + [(1088 + g * 512, 512, "qkv") for g in range(6)] + \
                     [(4192 + g * 512, 512, "z") for g in range(4)]
            nwg = [0]
            nblk = 0
            nst = [0, 0, 0]

            def sumsq_norm(rawt, n, gcol, outb, out32):
                pb = pbank()
                for k in range(4):
                    c.mm(pb.ap(128, n), onesb[:], sq[:, k, 0:n], k == 0, k == 3, R=[onesb, sq], W=pb.rs)
                rstd_from_ps(pb, n, 1.0 / 512, rstd, tmpn)
                for k in range(4):
                    c.stt("dve", outb[:, k, 0:n], rawt[:, k, 0:n], gcol[:, k:k + 1], rstd[:, 0:n], ALU.mult, ALU.mult, R=[rawt, gcol, rstd], W=[outb])
                    if out32 is not None:
                        c.stt("pool", out32[:, k, 0:n], rawt[:, k, 0:n], gcol[:, k:k + 1], rstd[:, 0:n], ALU.mult, ALU.mult, R=[rawt, gcol, rstd, outb], W=[out32])

            def kv_up(src_bf, n, kn_dst, v_dst_rows):
                for h in range(NH):
                    pb = pbank()
                    for k in range(4):
                        c.mm(pb.ap(128, n), wukk[:, k, h, :], src_bf[:, k, 0:n], k == 0, k == 3, R=[wukk, src_bf], W=pb.rs)
                    if h % 2 == 0:
                        c.copy("dve", stk[:, h % 2, 0:n], pb.ap(128, n), R=pb.rs, W=[stk])
                    else:
                        c.copy("act", stk[:, h % 2, 0:n], pb.ap(128, n), R=pb.rs, W=[stk])
                    c.dma("pool", "st1", kn_dst(h), stk[:, h % 2, 0:n], R=[stk], W=[c.res("kn")])
                for j in range(n // 128):
                    sv = stv[nst[0] % 2]; nst[0] += 1
                    for half in range(2):
                        pb = pbank()
                        for k in range(4):
                            c.mm(pb.ap(128, 512), src_bf[:, k, j * 128:(j + 1) * 128], wukv[:, k, half * 4:(half + 1) * 4, :], k == 0, k == 3,
                                 R=[src_bf, wukv], W=pb.rs)
                        if half == 0:
                            c.copy("dve", sv[:, 0:512], pb.ap(128, 512), R=pb.rs, W=[sv])
                        else:
                            c.copy("act", sv[:, 512:1024], pb.ap(128, 512), R=pb.rs, W=[sv])
                    c.dma("pool", "st1", v_dst_rows(j), sv[:], R=[sv], W=[c.res("v")])

            for sq_ in seqs:
                A, Bv = mods[sq_["ci"]]
                is_s = sq_["kind"] == "s"
                L = sq_["L"]
                for b0 in range(0, L, TB):
                    n = min(TB, L - b0)
                    t0 = sq_["t0"] + b0
                    hT = hTs[nblk % 2]; nblk += 1
                    cfg.chk(1 + (10 if is_s else 0))
                    norm_transpose((xt, sq, ss), first, t0, n, A, Bv, hT)
                    cfg.chk(2 + (10 if is_s else 0))
                    if is_s:
                        c.dma("sp", "ld0", ropec[:, 0:n], k_rope[0, :, b0:b0 + n], W=[ropec])
                        c.dma("sp", "ld0", ropes[:, 0:n], k_rope[1, :, b0:b0 + n], W=[ropes])
                    for (c0, ncol, kind) in groups:
                        wt = wg[nwg[0] % 2]; nwg[0] += 1
                        for k4 in range(4):
                            c.dma("sp", "ldw", wt[:, k4 * 4:k4 * 4 + 4, 0:ncol], wv[:, k4 * 4:k4 * 4 + 4, c0:c0 + ncol], R=[r_w], W=[wt])
                        for fc in range(ncol // 128):
                            pb = pbank()
                            for k in range(16):
                                c.mm(pb.ap(128, n), wt[:, k, fc * 128:(fc + 1) * 128], hT[:, k, 0:n], k == 0, k == 15, R=[wt, hT], W=pb.rs)
                            if kind in ("cq", "ckv"):
                                rw = raw[0 if kind == "cq" else 1]
                                c.copy("dve", rw[:, fc, 0:n], pb.ap(128, n), R=pb.rs, W=[rw])
                                c.act(sq[:, fc, 0:n], rw[:, fc, 0:n], AF.Square, R=[rw], W=[sq])
                            elif kind == "qkv":
                                s32 = st32[nst[1] % 3]; nst[1] += 1
                                if fc % 2 == 0:
                                    c.copy("dve", s32[:, 0:n], pb.ap(128, n), R=pb.rs, W=[s32])
                                else:
                                    c.copy("act", s32[:, 0:n], pb.ap(128, n), R=pb.rs, W=[s32])
                                r0 = c0 - 1088 + fc * 128
                                c.dma("pool", "st1", qkv_s[r0:r0 + 128, t0:t0 + n], s32[:, 0:n], R=[s32], W=[c.res("qkv")])
                            else:
                                sz = stz[nst[2] % 2]; nst[2] += 1
                                c.act(sz[:, 0:n], pb.ap(128, n), AF.Silu, R=pb.rs, W=[sz])
                                r0 = c0 - 4192 + fc * 128
                                c.dma("pool", "st1", sz_s[r0:r0 + 128, t0:t0 + n], sz[:, 0:n], R=[sz], W=[c.res("sz")])
                        cfg.chk(3 + (10 if is_s else 0))
                        if kind == "cq":
                            sumsq_norm(raw[0], n, qg, cqn, None)
                            for h in range(NH):
                                pb = pbank()
                                for k in range(4):
                                    c.mm(pb.ap(128, n), wuq[:, k, h * 192:h * 192 + 128], cqn[:, k, 0:n], k == 0, k == 3, R=[wuq, cqn], W=pb.rs)
                                c.copy("act", stq[:, h % 2, 0:n], pb.ap(128, n), R=pb.rs, W=[stq])
                                c.dma("pool", "st1", qn_s[h, :, t0:t0 + n], stq[:, h % 2, 0:n], R=[stq], W=[c.res("qn")])
                                pr = pbank()
                                for k in range(4):
                                    c.mm(pr.ap(64, n), wuq[:, k, h * 192 + 128:h * 192 + 192], cqn[:, k, 0:n], k == 0, k == 3, R=[wuq, cqn], W=pr.rs)
                                if is_s:
                                    pr2 = pbank()
                                    for k in range(4):
                                        c.mm(pr2.ap(64, n), wuqp[:, k, h, :], cqn[:, k, 0:n], k == 0, k == 3, R=[wuqp, cqn], W=pr2.rs)
                                    c.tt("dve", rt1[:, 0:n], pr.ap(64, n), ropec[:, 0:n], ALU.mult, R=pr.rs + [ropec], W=[rt1])
                                    c.tt("dve", rt2[:, 0:n], pr2.ap(64, n), ropes[:, 0:n], ALU.mult, R=pr2.rs + [ropes], W=[rt2])
                                    c.tt("pool", stqr[:, h % 2, 0:n], rt1[:, 0:n], rt2[:, 0:n], ALU.add, R=[rt1, rt2], W=[stqr])
                                else:
                                    c.copy("dve", stqr[:, h % 2, 0:n], pr.ap(64, n), R=pr.rs, W=[stqr])
                                c.dma("pool", "st1", qr_s[h, :, t0:t0 + n], stqr[:, h % 2, 0:n], R=[stqr], W=[c.res("qr")])
                        if kind == "ckv":
                            cfg.chk(4 + (10 if is_s else 0))
                            sumsq_norm(raw[1], n, kg_, ckn, None if is_s else ckn32)
                            if is_s:
                                kv_up(ckn, n, lambda h: kn_x[h, :, LC + b0:LC + b0 + n],
                                      lambda j: v_x[LC + b0 + j * 128:LC + b0 + (j + 1) * 128, :])
                            else:
                                kv_up(ckn, n, lambda h: kn_p[h, :, t0:t0 + n],
                                      lambda j: v_p[t0 + j * 128:t0 + (j + 1) * 128, :])
                    cfg.chk(5 + (10 if is_s else 0))
                    pk = pbank()
                    for k in range(16):
                        c.mm(pk.ap(64, n), wsm[:, k, 0:64], hT[:, k, 0:n], k == 0, k == 15, R=[wsm, hT], W=pk.rs)
                    if is_s:
                        pk2 = pbank()
                        for k in range(16):
                            c.mm(pk2.ap(64, n), wsm[:, k, 64:128], hT[:, k, 0:n], k == 0, k == 15, R=[wsm, hT], W=pk2.rs)
                        c.tt("dve", rt1[:, 0:n], pk.ap(64, n), ropec[:, 0:n], ALU.mult, R=pk.rs + [ropec], W=[rt1])
                        c.tt("dve", rt2[:, 0:n], pk2.ap(64, n), ropes[:, 0:n], ALU.mult, R=pk2.rs + [ropes], W=[rt2])
                        c.tt("pool", stkr[:, 0:n], rt1[:, 0:n], rt2[:, 0:n], ALU.add, R=[rt1, rt2], W=[stkr])
                        c.dma("pool", "st1", kr_x[:, LC + b0:LC + b0 + n], stkr[:, 0:n], R=[stkr], W=[c.res("kr")])
                    else:
                        c.copy("dve", kpe32[:, 0:n], pk.ap(64, n), R=pk.rs, W=[kpe32])
                        c.copy("act", stkr[:, 0:n], kpe32[:, 0:n], R=[kpe32], W=[stkr])
                        c.dma("pool", "st1", kr_p[:, t0:t0 + n], stkr[:, 0:n], R=[stkr], W=[c.res("kr")])
                        for j in range(n // 128):
                            ot = otok[j % 2]
                            pb = pbank()
                            for k in range(4):
                                c.op("pe", lambda e: e.transpose(out=pb.ap(128, 128, k * 128), in_=ckn32[:, k, j * 128:(j + 1) * 128], identity=ident[:]),
                                     R=[ckn32, ident], W=pb.rs)
                            c.copy("dve", ot[:, 0:512], pb.ap(128, 512), R=pb.rs, W=[ot])
                            pb2 = pslot()
                            c.op("pe", lambda e: e.transpose(out=pb2.ap(128, 64), in_=kpe32[:, j * 128:(j + 1) * 128], identity=ident[0:64, 0:64]),
                                 R=[kpe32, ident], W=pb2.rs)
                            c.copy("act", ot[:, 512:576], pb2.ap(128, 64), R=pb2.rs, W=[ot])
                            l0 = b0 + j * 128
                            c.dma("pool", "st2", nckv[sq_["idx"], i, l0:l0 + 128, :], ot[:, 0:512], R=[ot], W=[c.res("nckv")])
                            c.dma("pool", "st2", nkpe[sq_["idx"], i, l0:l0 + 128, :], ot[:, 512:576], R=[ot], W=[c.res("nkpe")])
                    cfg.chk(6 + (10 if is_s else 0))
                    pa = pbank()
                    for k in range(16):
                        c.mm(pa.ap(32, n), wsm[:, k, 128:160], hT[:, k, 0:n], k == 0, k == 15, R=[wsm, hT], W=pa.rs)
                    s32 = st32[nst[1] % 3]; nst[1] += 1
                    c.copy("dve", s32[0:32, 0:n], pa.ap(32, n), R=pa.rs, W=[s32])
                    c.dma("pool", "st1", abT_s[:, t0:t0 + n], s32[0:32, 0:n], R=[s32], W=[c.res("abT")])
                    cfg.chk(8 + (10 if is_s else 0))
            cfg.chk(7)
            ctx32 = c.sb("ctx32", [128, 576], F32, ph)
            for j in range(LC // 128):
                c.dma("sp", "ld0", ctx32[:, 0:512], cckv[i, j * 128:(j + 1) * 128, :], W=[ctx32])
                c.dma("sp", "ld0", ctx32[:, 512:576], ckpe[i, j * 128:(j + 1) * 128, :], W=[ctx32])
                pb = pbank()
                for k in range(4):
                    c.op("pe", lambda e: e.transpose(out=pb.ap(128, 128, k * 128), in_=ctx32[:, k * 128:(k + 1) * 128], identity=ident[:]),
                         R=[ctx32, ident], W=pb.rs)
                for k in range(4):
                    c.copy("dve", ckn[:, k, j * 128:(j + 1) * 128], pb.ap(128, 128, k * 128), R=pb.rs, W=[ckn])
                pb2 = pslot()
                c.op("pe", lambda e: e.transpose(out=pb2.ap(64, 128), in_=ctx32[:, 512:576], identity=ident[:]), R=[ctx32, ident], W=pb2.rs)
                c.copy("act", stkr[:, j * 128:(j + 1) * 128], pb2.ap(64, 128), R=pb2.rs, W=[stkr])
            c.dma("pool", "st1", kr_x[:, 0:LC], stkr[:, 0:LC], R=[stkr], W=[c.res("kr")])
            kv_up(ckn, LC, lambda h: kn_x[h, :, 0:LC], lambda j: v_x[j * 128:(j + 1) * 128, :])
        c.barrier()

    def even_E3(layer):
        with ExitStack() as ph:
            LKMAX = LC + LS
            kn_t = [c.sb(f"kn{b}", [128, LKMAX], BF16, ph) for b in range(2)]
            v_t = [c.sb(f"vv{b}", [128, LKMAX // 128, 128], BF16, ph) for b in range(2)]
            kr_t = c.sb("krt", [64, LKMAX], BF16, ph)
            qn_t = [c.sb(f"qnt{b}", [128, 512], BF16, ph) for b in range(2)]
            qr_t = [c.sb(f"qrt{b}", [64, 512], BF16, ph) for b in range(2)]
            pT = [c.sb(f"pT{b}", [128, 512], BF16, ph) for b in range(4)]
            szt = [c.sb(f"szt{b}", [128, 512], BF16, ph) for b in range(2)]
            yst = [c.sb(f"yst{b}", [128, 512], BF16, ph) for b in range(2)]
            rden = c.sb("rden", [128, 512], F32, ph)
            tmpo = c.sb("tmpo", [128, 512], F32, ph)
            scale = 192.0 ** -0.5
            nh = 0; nq = 0; npt_box = [0]
            for sq_ in seqs:
                is_s = sq_["kind"] == "s"
                L = sq_["L"]; t0 = sq_["t0"]
                Lk = LC + L if is_s else L
                nkc = Lk // 128
                QB = min(512, L)
                c.dma("sp", "ld0", kr_t[:, 0:Lk], (kr_x[:, 0:Lk] if is_s else kr_p[:, t0:t0 + Lk]), R=[c.res("kr")], W=[kr_t])
                for h in range(NH):
                    knt = kn_t[nh % 2]; vt = v_t[nh % 2]; nh += 1
                    c.dma("sp", "ldk", knt[:, 0:Lk], (kn_x[h, :, 0:Lk] if is_s else kn_p[h, :, t0:t0 + Lk]), R=[c.res("kn")], W=[knt])
                    vsrc = (v_x[0:Lk, h * 128:(h + 1) * 128] if is_s else v_p[t0:t0 + Lk, h * 128:(h + 1) * 128])
                    c.dma("sp", "ldk", vt[:, 0:nkc, :], vsrc.rearrange("(n p) e -> p n e", p=128), R=[c.res("v")], W=[vt])
                    for q0 in range(0, L, QB):
                        qnt = qn_t[nq % 2]; qrt = qr_t[nq % 2]; szT = szt[nq % 2]; yT = yst[nq % 2]
                        a, b = t0 + q0, t0 + q0 + QB
                        c.dma("sp", "ldq", qnt[:, 0:QB], qn_s[h, :, a:b], R=[c.res("qn")], W=[qnt])
                        c.dma("sp", "ldq", qrt[:, 0:QB], qr_s[h, :, a:b], R=[c.res("qr")], W=[qrt])
                        c.dma("sp", "ldq", szT[:, 0:QB], sz_s[h * 128:(h + 1) * 128, a:b], R=[c.res("sz")], W=[szT])
                        po = pbank(4 + nq % 2); pd = pbank(6 + nq % 2); nq += 1
                        def scores(kc):
                            nonlocal_npt = npt_box[0]; npt_box[0] += 1
                            psb = pbank(nonlocal_npt % 4); p_t = pT[nonlocal_npt % 4]
                            c.mm(psb.ap(128, QB), knt[:, kc * 128:(kc + 1) * 128], qnt[:, 0:QB], True, False, R=[knt, qnt], W=psb.rs)
                            c.mm(psb.ap(128, QB), kr_t[:, kc * 128:(kc + 1) * 128], qrt[:, 0:QB], False, True, R=[kr_t, qrt], W=psb.rs)
                            c.act(p_t[:, 0:QB], psb.ap(128, QB), AF.Exp, R=psb.rs, W=[p_t], scale=scale)
                            return p_t
                        pend = [scores(0)]
                        if nkc > 1:
                            pend.append(scores(1))
                        for kc in range(nkc):
                            p_t = pend.pop(0)
                            if kc + 2 < nkc:
                                pend.append(scores(kc + 2))
                            c.mm(po.ap(128, QB), vt[:, kc, :], p_t[:, 0:QB], kc == 0, kc == nkc - 1, R=[vt, p_t], W=po.rs)
                            c.mm(pd.ap(128, QB), onesb[:], p_t[:, 0:QB], kc == 0, kc == nkc - 1, R=[onesb, p_t], W=pd.rs)
                        c.op("dve", lambda e: e.reciprocal(out=rden[:, 0:QB], in_=pd.ap(128, QB)), R=pd.rs, W=[rden])
                        c.tt("dve", tmpo[:, 0:QB], po.ap(128, QB), rden[:, 0:QB], ALU.mult, R=po.rs + [rden], W=[tmpo])
                        c.tt("pool", yT[:, 0:QB], tmpo[:, 0:QB], szT[:, 0:QB], ALU.mult, R=[tmpo, szT], W=[yT])
                        c.dma("pool", "st1", yT_s[h * 128:(h + 1) * 128, a:b], yT[:, 0:QB], R=[yT], W=[c.res("yT")])
        c.barrier()

    def out_proj(layer, wsrc):
        first = layer == 0
        last = layer == cfg.depth - 1
        with ExitStack() as ph:
            wo = c.sb("wo", [128, 16, D], BF16, ph)
            wov = wsrc.rearrange("(k p) e -> p k e", p=128)
            for k4 in range(8):
                c.dma("sp", "ld0", wo[:, k4 * 2:k4 * 2 + 2, :], wov[:, k4 * 2:k4 * 2 + 2, :], R=[r_w], W=[wo])
            gts = []
            for ci in range(2):
                g = c.sb(f"gt{ci}", [128, D], F32, ph)
                c.dma("sp", "ld0", g[:], modv[layer, ci, 2 * D:3 * D].partition_broadcast(128), R=[r_modv], W=[g])
                gts.append(g)
            fn = None
            if last:
                fn = c.sb("fnw", [128, D], F32, ph)
                c.dma("sp", "ld0", fn[:], final_norm.partition_broadcast(128), W=[fn])
            yts = [c.sb(f"yt{b}", [128, 16, 512], BF16, ph) for b in range(2)]
            xts = [c.sb(f"xo{b}", [128, D], F32, ph) for b in range(2)]
            tms = [c.sb(f"tm{b}", [128, 512], F32, ph) for b in range(2)]
            junk = c.sb("junkb", [128, D], BF16, ph)
            ss = c.sb("ss4", [128, 4], F32, ph)
            nb = 0; nt = 0; ntm = 0
            for sq_ in seqs:
                L = sq_["L"]; g = gts[sq_["ci"]]
                for b0 in range(0, L, 512):
                    n = min(512, L - b0)
                    t0 = sq_["t0"] + b0
                    yt = yts[nb % 2]; nb += 1
                    for k4 in range(4):
                        c.dma("sp", "ldk", yt[:, k4 * 4:k4 * 4 + 4, 0:n], yT_s.rearrange("(k p) l -> p k l", p=128)[:, k4 * 4:k4 * 4 + 4, t0:t0 + n],
                              R=[c.res("yT")], W=[yt])
                    for j in range(n // 128):
                        xt = xts[nt % 2]; nt += 1
                        tj = t0 + j * 128
                        xres = c.res(("x", tj // 128))
                        c.dma("sp", "ldx", xt[:], xrows(first, tj, 128), R=[xres], W=[xt])
                        for nn in range(4):
                            pb = pbank()
                            for k in range(16):
                                c.mm(pb.ap(128, 512), yt[:, k, j * 128:(j + 1) * 128], wo[:, k, nn * 512:(nn + 1) * 512], k == 0, k == 15, R=[yt, wo], W=pb.rs)
                            tm = tms[ntm % 2]; ntm += 1
                            c.tt("dve", tm[:], pb.ap(128, 512), g[:, nn * 512:(nn + 1) * 512], ALU.mult, R=pb.rs + [g], W=[tm])
                            c.tt("pool", xt[:, nn * 512:(nn + 1) * 512], xt[:, nn * 512:(nn + 1) * 512], tm[:], ALU.add, R=[tm, xt], W=[xt])
                        if last:
                            c.act(junk[:], xt[:], AF.Square, R=[xt], W=[junk, ss], accum=ss[:, 0:1])
                            c.act(ss[:, 1:2], ss[:, 0:1], AF.Ln, R=[ss, epsT], W=[ss], bias=epsT[:, 0:1], scale=1.0 / D)
                            c.act(ss[:, 2:3], ss[:, 1:2], AF.Exp, R=[ss], W=[ss], scale=-0.5)
                            c.stt("dve", xt[:], xt[:], ss[:, 2:3], fn[:], ALU.mult, ALU.mult, R=[xt, ss, fn], W=[xt])
                            dst = yp[tj:tj + 128, :] if tj < NPT else ys[tj - NPT:tj - NPT + 128, :]
                            c.dma("pool", "st2", dst, xt[:], R=[xt], W=[xres])
                        else:
                            dst = xw_p[tj:tj + 128, :] if tj < NPT else xw_s[tj - NPT:tj - NPT + 128, :]
                            c.dma("pool", "st2", dst, xt[:], R=[xt], W=[xres])
        c.barrier()

    def even_E2(layer):
        i = layer // 2
        with ExitStack() as ph:
            cw = c.sb("cw", [128, 3, 24], F32, ph)
            for t in range(3):
                c.dma("sp", "ld0", cw[:, t, :], conv_e[i, t].rearrange("(k p) -> p k", p=128), W=[cw], allow_slow_non_contiguous=True)
            LM = max(s_["L"] for s_ in seqs)
            xr = [c.sb(f"xr{b}", [128, LM + 2], F32, ph) for b in range(2)]
            accs = [c.sb(f"acc{b}", [128, LM], F32, ph) for b in range(2)]
            ysls = [c.sb(f"ysl{b}", [128, LM], F32, ph) for b in range(2)]
            sqb = c.sb("sqb", [128, LM], BF16, ph)
            rinv = c.sb("rinv", [128, LM], F32, ph)
            tln = c.sb("tln", [128, 512], F32, ph)
            outb = [c.sb(f"outb{b}", [128, LM], BF16, ph) for b in range(2)]
            bbc = c.sb("bbc", [128, LM], F32, ph)
            nx = 0; no = 0
            for sq_ in seqs:
                L = sq_["L"]; t0 = sq_["t0"]
                for h in range(NH):
                    kn_keep = None
                    for which in (1, 0, 2):
                        fcx = which * 8 + h
                        x = xr[nx % 2]; acc = accs[nx % 2]; ysl = ysls[nx % 2]; nx += 1
                        c.op("pool", lambda e: e.memset(x[:, 0:1], 0.0), W=[x])
                        c.op("pool", lambda e: e.memset(x[:, L + 1:L + 2], 0.0), W=[x])
                        c.dma("sp", "ldk", x[:, 1:L + 1], qkv_s[fcx * 128:(fcx + 1) * 128, t0:t0 + L], R=[c.res("qkv")], W=[x])
                        c.act(acc[:, 0:L], x[:, 1:L + 1], AF.Identity, R=[x, cw], W=[acc], scale=cw[:, 1, fcx:fcx + 1])
                        c.stt("dve", acc[:, 0:L], x[:, 0:L], cw[:, 0, fcx:fcx + 1], acc[:, 0:L], ALU.mult, ALU.add, R=[x, cw, acc], W=[acc])
                        c.stt("dve", acc[:, 0:L], x[:, 2:L + 2], cw[:, 2, fcx:fcx + 1], acc[:, 0:L], ALU.mult, ALU.add, R=[x, cw, acc], W=[acc])
                        c.act(ysl[:, 0:L], acc[:, 0:L], AF.Silu, R=[acc], W=[ysl])
                        ob = outb[no % 2]; no += 1
                        if which == 2:
                            c.copy("act", ob[:, 0:L], ysl[:, 0:L], R=[ysl], W=[ob])
                            c.dma("pool", "st1", gv_s[h, :, t0:t0 + L], ob[:, 0:L], R=[ob], W=[c.res("gv")])
                            continue
                        c.act(sqb[:, 0:L], ysl[:, 0:L], AF.Square, R=[ysl], W=[sqb])
                        for b0 in range(0, L, 512):
                            n = min(512, L - b0)
                            pb = pbank()
                            c.mm(pb.ap(128, n), onesb[:], sqb[:, b0:b0 + n], True, True, R=[onesb, sqb], W=pb.rs)
                            c.act(tln[:, 0:n], pb.ap(128, n), AF.Ln, R=pb.rs + [epsT], W=[tln], bias=epsT[:, 0:1], scale=1.0)
                            c.act(rinv[:, b0:b0 + n], tln[:, 0:n], AF.Exp, R=[tln], W=[rinv], scale=-0.5)
                        if which == 0:
                            c.stt("dve", ob[:, 0:L], ysl[:, 0:L], 128.0 ** -0.5, rinv[:, 0:L], ALU.mult, ALU.mult, R=[ysl, rinv], W=[ob])
                            c.dma("pool", "st1", gq_s[h, :, t0:t0 + L], ob[:, 0:L], R=[ob], W=[c.res("gq")])
                        else:
                            c.tt("dve", ysl[:, 0:L], ysl[:, 0:L], rinv[:, 0:L], ALU.mult, R=[ysl, rinv], W=[ysl])
                            c.copy("act", ob[:, 0:L], ysl[:, 0:L], R=[ysl], W=[ob])
                            c.dma("pool", "st1", gk_s[h, :, t0:t0 + L], ob[:, 0:L], R=[ob], W=[c.res("gk")])
                            for d in range(2):
                                c.dma("sp", "ldk", bbc[:, 0:L], abT_s[16 + d * 8 + h, t0:t0 + L].partition_broadcast(128), R=[c.res("abT")], W=[bbc])
                                c.act(bbc[:, 0:L], bbc[:, 0:L], AF.Sigmoid, R=[bbc], W=[bbc])
                                ob2 = outb[no % 2]; no += 1
                                c.tt("dve", ob2[:, 0:L], ysl[:, 0:L], bbc[:, 0:L], ALU.mult, R=[ysl, bbc], W=[ob2])
                                c.dma("pool", "st1", gkb_s[d, h, :, t0:t0 + L], ob2[:, 0:L], R=[ob2], W=[c.res("gkb")])
        cfg.chk(21)
        c.barrier()
        with ExitStack() as ph:
            msk = c.sb("msk", [128, 9, 128], F32, ph)
            cum = c.sb("cum", [128, 2, 128], F32, ph)
            for m_ in range(9):
                c.dma("sp", "ld0", msk[:, m_, :], k_masks[m_], W=[msk])
            for m_ in range(2):
                c.dma("sp", "ld0", cum[:, m_, :], k_cum[m_], W=[cum])
            dtb = c.sb("dtb", [128, 16], F32, ph)
            nega = c.sb("nega", [128, 16], F32, ph)
            c.dma("sp", "ld0", dtb[:], dt_bias_e[i].partition_broadcast(128), W=[dtb])
            c.dma("sp", "ld0", nega[:], a_log_e[i].partition_broadcast(128), W=[nega])
            c.act(nega[:], nega[:], AF.Exp, R=[nega], W=[nega])
            c.ts("dve", nega[:], nega[:], -1.0, ALU.mult, R=[nega], W=[nega])
            onw = load_cols(ph, "onw", o_norm_e[i], 1)
            NCM = max(max(s_["L"] for s_ in seqs) // 128, 8)
            abT_sb = c.sb("abT_sb", [32, NCM * 128], F32, ph)
            ab_tm = c.sb("ab_tm", [128, NCM, 32], F32, ph)
            T = lambda nm: c.sb(nm, [128, NCM, 16], F32, ph)
            xg, ta, tb_, gg, beta, Gc, expG, kdsc, dec, bexpG = [T(n_) for n_ in ("xg", "ta", "tb_", "gg", "beta", "Gc", "expG", "kdsc", "dec", "bexpG")]
            fl = lambda t_: t_[:].rearrange("p a b -> p (a b)")
            G2 = lambda nm, dt: [c.sb(f"{nm}{g}", [128, 4, 128], dt, ph) for g in range(2)]
            FL = lambda t_: t_[:].rearrange("p a b -> p (a b)")
            ld = [[c.sb(f"ld{nm}{b}", [128, NH, 128], BF16, ph) for b in range(2)] for nm in ("k", "q", "v", "kb")]
            NDT = F32 if cfg.neu32 else BF16
            gam1, gamT, gamTs, egrow, tmp32 = [G2(n_, F32) for n_ in ("gam1", "gamT", "gamTs", "egrow", "tmp32")]
            diag = tmp32
            IT8, QD8 = [G2(n_, BF16) for n_ in ("IT8", "QD8")]
            A8, AT8 = [G2(n_, NDT) for n_ in ("A8", "AT8")]
            Ad8, ATd8, AL8, Td8, Z8 = [G2(n_, NDT) for n_ in ("Ad8", "ATd8", "AL8", "Td8", "Z8")]
            PTf = G2("PTf", BF16)
            PTfb = PTf
            identn = ident if cfg.neu32 else identb
            Xa, Xb, XTa, XTb, PTa, PTb = [G2(n_, NDT) for n_ in ("Xa", "Xb", "XTa", "XTb", "PTa", "PTb")]
            KBG, KD, VB, WT, VN = [G2(n_, BF16) for n_ in ("KBG", "KD", "VB", "WT", "VN")]
            U32, S32, O32 = [G2(n_, F32) for n_ in ("U32", "S32", "O32")]
            ON32 = U32
            Sbf = G2("Sbf", BF16)
            YG = G2("YG", BF16)
            rep = {}
            for nm_, src_ in (("m01f", k_masks[4]), ("m01b", k_masks[5]), ("bd", k_masks[6]), ("ll", k_masks[7]), ("ur", k_masks[8]), ("id", k_ident)):
                t_ = c.sb("rep_" + nm_, [128, 4, 128], F32, ph)
                for hh in range(4):
                    c.dma("sp", "ld0", t_[:, hh, :], src_, W=[t_])
                rep[nm_] = t_
            OG = c.sb("OG", [128, NH, 128], F32, ph)
            szc = c.sb("szc", [128, NH, 128], BF16, ph)
            ssq = c.sb("ssq", [128, 3, NH], F32, ph)
            junk = c.sb("junk2", [128, 128], BF16, ph)
            nld = 0
            for sq_ in seqs:
                L = sq_["L"]; t0 = sq_["t0"]; is_s = sq_["kind"] == "s"
                nch = L // 128
                NP_ = max(nch, 8)
                c.dma("sp", "ld0", abT_sb[:, 0:L], abT_s[:, t0:t0 + L], R=[c.res("abT")], W=[abT_sb])
                for c4 in range(0, nch, 4):
                    ps4 = pbank()
                    nn_ = min(4, nch - c4)
                    for q_ in range(nn_):
                        c.op("pe", lambda e: e.transpose(out=ps4.ap(128, 32, q_ * 32), in_=abT_sb[:, (c4 + q_) * 128:(c4 + q_ + 1) * 128], identity=ident[0:32, 0:32]),
                             R=[abT_sb, ident], W=ps4.rs)
                    c.copy("dve", ab_tm[:, c4:c4 + nn_, :].rearrange("p a b -> p (a b)"), ps4.ap(128, nn_ * 32), R=ps4.rs, W=[ab_tm])
                cfg.chk(22)
                if NP_ > nch:
                    c.op("pool", lambda e: e.memset(fl(gg), 0.0), W=[gg])
                for j_ in range(16):
                    c.ts("dve", xg[:, 0:nch, j_], ab_tm[:, 0:nch, j_], dtb[:, j_:j_ + 1], ALU.add, R=[ab_tm, dtb], W=[xg])
                xf, taf, tbf = xg[:, 0:nch, :], ta[:, 0:nch, :], tb_[:, 0:nch, :]
                c.ts("dve", taf, xf, -1.0, ALU.mult, R=[xg], W=[ta])
                c.tt("dve", taf, taf, xf, ALU.max, R=[ta, xg], W=[ta])
                c.act(taf, taf, AF.Exp, R=[ta], W=[ta], scale=-1.0)
                c.act(taf, taf, AF.Ln, R=[ta, oneT], W=[ta], bias=oneT[:, 0:1], scale=1.0)
                c.ts("dve", tbf, xf, 0.0, ALU.max, R=[xg], W=[tb_])
                c.tt("dve", taf, taf, tbf, ALU.add, R=[ta, tb_], W=[ta])
                for j_ in range(16):
                    c.ts("dve", gg[:, 0:nch, j_], ta[:, 0:nch, j_], nega[:, j_:j_ + 1], ALU.mult, R=[ta, nega], W=[gg])
                c.act(beta[:, 0:nch, :], ab_tm[:, 0:nch, 16:32], AF.Exp, R=[ab_tm], W=[beta], scale=-1.0)
                c.ts("dve", beta[:, 0:nch, :], beta[:, 0:nch, :], 1.0, ALU.add, R=[beta], W=[beta])
                c.op("dve", lambda e: e.reciprocal(out=beta[:, 0:nch, :], in_=beta[:, 0:nch, :]), R=[beta], W=[beta])
                pF = pbank(); pB = pbank(); pT_ = pbank()
                W_ = NP_ * 16
                c.mm(pF.ap(128, W_), cum[:, 0, :], fl(gg)[:, 0:W_], True, True, R=[cum, gg], W=pF.rs)
                c.mm(pB.ap(128, W_), cum[:, 1, :], fl(gg)[:, 0:W_], True, True, R=[cum, gg], W=pB.rs)
                c.mm(pT_.ap(128, W_), ones32[:], fl(gg)[:, 0:W_], True, True, R=[ones32, gg], W=pT_.rs)
                v3 = lambda ps_: ps_.ap(128, W_).rearrange("p (a b) -> p a b", b=16)
                c.copy("dve", Gc[:, 0:NP_, 0:8], v3(pF)[:, :, 0:8], R=pF.rs, W=[Gc])
                c.copy("dve", Gc[:, 0:NP_, 8:16], v3(pB)[:, :, 8:16], R=pB.rs, W=[Gc])
                c.act(fl(expG)[:, 0:W_], fl(Gc)[:, 0:W_], AF.Exp, R=[Gc], W=[expG])
                c.tt("dve", fl(kdsc)[:, 0:W_], pT_.ap(128, W_), fl(Gc)[:, 0:W_], ALU.subtract, R=pT_.rs + [Gc], W=[kdsc])
                c.act(fl(kdsc)[:, 0:W_], fl(kdsc)[:, 0:W_], AF.Exp, R=[kdsc], W=[kdsc])
                c.act(fl(dec)[:, 0:W_], pT_.ap(128, W_), AF.Exp, R=pT_.rs, W=[dec])
                c.tt("dve", fl(bexpG)[:, 0:W_], fl(beta)[:, 0:W_], fl(expG)[:, 0:W_], ALU.mult, R=[beta, expG], W=[bexpG])
                if sq_ is seqs[0]:
                    for nm_, t_ in (("Gc", Gc), ("expG", expG), ("kdsc", kdsc), ("dec", dec), ("bexpG", bexpG), ("beta", beta), ("gg", gg)):
                        dbg(nm_, t_[:], [128, NCM, 16], R=[t_])
                    dbg("ab_tm", ab_tm[:], [128, NCM, 32], R=[ab_tm])
                cfg.chk(23)
                for d in range(2):
                    mA, mB, m01 = (0, 1, 4) if d == 0 else (2, 3, 5)
                    for gi in range(2):
                        for hh in range(4):
                            h = gi * 4 + hh
                            if is_s:
                                c.dma("sp", "ld0", S32[gi][:, hh, :], (sfw if d == 0 else sbw)[i, h], W=[S32[gi]])
                        if not is_s:
                            c.op("pool", lambda e: e.memset(FL(S32[gi]), 0.0), W=[S32[gi]])
                        c.copy("act", FL(Sbf[gi]), FL(S32[gi]), R=[S32[gi]], W=[Sbf[gi]])
                    order = range(nch) if d == 0 else range(nch - 1, -1, -1)
                    for ci in order:
                        a_, b_ = t0 + ci * 128, t0 + (ci + 1) * 128
                        lk, lq, lv, lkb = [ld[x_][nld % 2] for x_ in range(4)]; nld += 1
                        c.dma("sp", "ldq", lk[:], gk_s.rearrange("h p l -> p h l")[:, :, a_:b_], R=[c.res("gk")], W=[lk])
                        c.dma("sp", "ldq", lq[:], gq_s.rearrange("h p l -> p h l")[:, :, a_:b_], R=[c.res("gq")], W=[lq])
                        c.dma("sp", "ldq", lv[:], gv_s.rearrange("h p l -> p h l")[:, :, a_:b_], R=[c.res("gv")], W=[lv])
                        c.dma("sp", "ldq", lkb[:], gkb_s[d].rearrange("h p l -> p h l")[:, :, a_:b_], R=[c.res("gkb")], W=[lkb])
                        if d == 1:
                            c.dma("sp", "ldq", OG[:].rearrange("p h e -> p (h e)"), og_s[a_:b_, :], R=[c.res(("og", a_))], W=[OG])
                            c.dma("sp", "ldq", szc[:], sz_s[1024:2048, a_:b_].rearrange("(h p) l -> p h l", p=128), R=[c.res("sz")], W=[szc])
                        col = lambda t_, h: t_[:, ci, d * 8 + h:d * 8 + h + 1]
                        m01r = rep["m01f"] if d == 0 else rep["m01b"]
                        offr = rep["ll"] if d == 0 else rep["ur"]
                        GI = (0, 1)
                        hsl = lambda gi: slice(gi * 4, gi * 4 + 4)
                        def mm4(pb, lhs, rhs, R):
                            for hh in range(4):
                                c.mm(pb.ap(128, 128, hh * 128), lhs(hh), rhs(hh), True, True, R=R, W=pb.rs)
                        pg = {}
                        for gi in GI:
                            for hh in range(4):
                                c.act(diag[gi][:, hh, :], ident[:], AF.Identity, R=[ident, Gc], W=[diag[gi]], scale=col(Gc, gi * 4 + hh))
                            pg[gi] = pbank()
                            mm4(pg[gi], lambda hh: ones32[:], lambda hh: diag[gi][:, hh, :], [ones32, diag[gi]])
                        for gi in GI:
                            for hh in range(4):
                                h = gi * 4 + hh
                                c.stt("dve", gam1[gi][:, hh, :], pg[gi].ap(128, 128, hh * 128), col(Gc, h), msk[:, mA, :], ALU.subtract, ALU.subtract,
                                      R=pg[gi].rs + [Gc, msk], W=[gam1[gi]])
                                c.stt("dve", gamT[gi][:, hh, :], pg[gi].ap(128, 128, hh * 128), col(Gc, h), msk[:, mB, :], ALU.subtract, ALU.add,
                                      R=pg[gi].rs + [Gc, msk], W=[gamT[gi]])
                            c.act(FL(egrow[gi]), pg[gi].ap(128, 512), AF.Exp, R=pg[gi].rs, W=[egrow[gi]])
                            c.act(FL(gam1[gi]), FL(gam1[gi]), AF.Exp, R=[gam1[gi]], W=[gam1[gi]], scale=-1.0)
                            c.act(FL(gamT[gi]), FL(gamT[gi]), AF.Exp, R=[gamT[gi]], W=[gamT[gi]])
                            c.tt("pool", FL(gamTs[gi]), FL(gamT[gi]), FL(m01r), ALU.mult, R=[gamT[gi], m01r], W=[gamTs[gi]])
                        cfg.chk(24)
                        for gi in GI:
                            o4 = gi * 4
                            p1 = pbank(); mm4(p1, lambda hh: lkb[:, o4 + hh, :], lambda hh: lk[:, o4 + hh, :], [lkb, lk])
                            p2 = pbank(); mm4(p2, lambda hh: lk[:, o4 + hh, :], lambda hh: lkb[:, o4 + hh, :], [lkb, lk])
                            p3 = pbank(); mm4(p3, lambda hh: lk[:, o4 + hh, :], lambda hh: lq[:, o4 + hh, :], [lk, lq])
                            c.tt("dve", FL(A8[gi]), p1.ap(128, 512), FL(gam1[gi]), ALU.mult, R=p1.rs + [gam1[gi]], W=[A8[gi]])
                            c.tt("dve", FL(AT8[gi]), p2.ap(128, 512), FL(gamTs[gi]), ALU.mult, R=p2.rs + [gamTs[gi]], W=[AT8[gi]])
                            c.tt("dve", FL(IT8[gi]), p3.ap(128, 512), FL(gamT[gi]), ALU.mult, R=p3.rs + [gamT[gi]], W=[IT8[gi]])
                            c.tt("pool", FL(QD8[gi]), lq[:, hsl(gi), :].rearrange("p a b -> p (a b)"), FL(egrow[gi]), ALU.mult, R=[lq, egrow[gi]], W=[QD8[gi]])
                            c.tt("pool", FL(Ad8[gi]), FL(A8[gi]), FL(rep["bd"]), ALU.mult, R=[A8[gi], rep["bd"]], W=[Ad8[gi]])
                            c.tt("pool", FL(ATd8[gi]), FL(AT8[gi]), FL(rep["bd"]), ALU.mult, R=[AT8[gi], rep["bd"]], W=[ATd8[gi]])
                            c.tt("pool", FL(AL8[gi]), FL(A8[gi]), FL(offr), ALU.mult, R=[A8[gi], offr], W=[AL8[gi]])
                            c.tt("pool", FL(PTa[gi]), FL(rep["id"]), FL(ATd8[gi]), ALU.subtract, R=[rep["id"], ATd8[gi]], W=[PTa[gi]])
                        cfg.chk(25)
                        NL = 5
                        X, XT, PT = Ad8, ATd8, PTa
                        Xn, XTn, PTn = Xa, XTa, PTb
                        for lvl in range(1, NL + 1):
                            sA, sB, sC = {}, {}, {}
                            for gi in GI:
                                sA[gi] = pbank(); mm4(sA[gi], lambda hh: XT[gi][:, hh, :], lambda hh: X[gi][:, hh, :], [XT[gi], X[gi]])
                                if lvl < NL:
                                    sB[gi] = pbank(); mm4(sB[gi], lambda hh: X[gi][:, hh, :], lambda hh: XT[gi][:, hh, :], [XT[gi], X[gi]])
                            for gi in GI:
                                c.copy("act", FL(Xn[gi]), sA[gi].ap(128, 512), R=sA[gi].rs, W=[Xn[gi]])
                                if lvl < NL:
                                    c.copy("act" if lvl % 2 == 0 else "dve", FL(XTn[gi]), sB[gi].ap(128, 512), R=sB[gi].rs, W=[XTn[gi]])
                            for gi in GI:
                                sC[gi] = pbank(); mm4(sC[gi], lambda hh: Xn[gi][:, hh, :], lambda hh: PT[gi][:, hh, :], [Xn[gi], PT[gi]])
                            for gi in GI:
                                c.tt("dve", FL(PTn[gi]), sC[gi].ap(128, 512), FL(PT[gi]), ALU.add, R=sC[gi].rs + [PT[gi]], W=[PTn[gi]])
                            X, XT, PT = Xn, XTn, PTn
                            Xn = Xb if X is Xa else Xa
                            XTn = XTb if XT is XTa else XTa
                            PTn = PTa if PT is PTb else PTb
                        TdT = PT
                        sT, sZ, sR = {}, {}, {}
                        for gi in GI:
                            sT[gi] = pbank(); mm4(sT[gi], lambda hh: TdT[gi][:, hh, :], lambda hh: identn[:], [TdT[gi], identn])
                            sZ[gi] = pbank(); mm4(sZ[gi], lambda hh: AL8[gi][:, hh, :], lambda hh: TdT[gi][:, hh, :], [TdT[gi], AL8[gi]])
                        for gi in GI:
                            c.copy("act", FL(Td8[gi]), sT[gi].ap(128, 512), R=sT[gi].rs, W=[Td8[gi]])
                            c.copy("dve", FL(Z8[gi]), sZ[gi].ap(128, 512), R=sZ[gi].rs, W=[Z8[gi]])
                        for gi in GI:
                            sR[gi] = pbank(); mm4(sR[gi], lambda hh: Td8[gi][:, hh, :], lambda hh: Z8[gi][:, hh, :], [Td8[gi], Z8[gi]])
                        for gi in GI:
                            c.tt("dve", FL(PTf[gi]), FL(TdT[gi]), sR[gi].ap(128, 512), ALU.subtract, R=sR[gi].rs + [TdT[gi]], W=[PTf[gi]])
                        PT = PTfb
                        cfg.chk(26)
                        for gi in GI:
                            o4 = gi * 4
                            p1 = pbank(); mm4(p1, lambda hh: lk[:, o4 + hh, :], lambda hh: identb[:], [lk, identb])
                            p2 = pbank(); mm4(p2, lambda hh: lv[:, o4 + hh, :], lambda hh: identb[:], [lv, identb])
                            for hh in range(4):
                                c.act(KBG[gi][:, hh, :], p1.ap(128, 128, hh * 128), AF.Identity, R=p1.rs + [bexpG], W=[KBG[gi]], scale=col(bexpG, o4 + hh))
                            for hh in range(4):
                                c.ts("dve", VB[gi][:, hh, :], p2.ap(128, 128, hh * 128), col(beta, o4 + hh), ALU.mult, R=p2.rs + [beta], W=[VB[gi]])
                            for hh in range(4):
                                c.ts("dve", KD[gi][:, hh, :], p1.ap(128, 128, hh * 128), col(kdsc, o4 + hh), ALU.mult, R=p1.rs + [kdsc], W=[KD[gi]])
                        for gi in GI:
                            p1 = pbank(); mm4(p1, lambda hh: PT[gi][:, hh, :], lambda hh: VB[gi][:, hh, :], [PT[gi], VB[gi]])
                            p2 = pbank(); mm4(p2, lambda hh: KBG[gi][:, hh, :], lambda hh: PT[gi][:, hh, :], [PT[gi], KBG[gi]])
                            c.copy("act", FL(U32[gi]), p1.ap(128, 512), R=p1.rs, W=[U32[gi]])
                            c.copy("dve", FL(WT[gi]), p2.ap(128, 512), R=p2.rs, W=[WT[gi]])
                        cfg.chk(27)
                        for gi in GI:
                            p1 = pbank(); mm4(p1, lambda hh: WT[gi][:, hh, :], lambda hh: Sbf[gi][:, hh, :], [WT[gi], Sbf[gi]])
                            c.tt("dve", FL(VN[gi]), FL(U32[gi]), p1.ap(128, 512), ALU.subtract, R=p1.rs + [U32[gi]], W=[VN[gi]])
                        for gi in GI:
                            o4 = gi * 4
                            p2 = pbank()
                            for hh in range(4):
                                c.mm(p2.ap(128, 128, hh * 128), QD8[gi][:, hh, :], Sbf[gi][:, hh, :], True, False, R=[QD8[gi], Sbf[gi]], W=p2.rs)
                                c.mm(p2.ap(128, 128, hh * 128), IT8[gi][:, hh, :], VN[gi][:, hh, :], False, True, R=[IT8[gi], VN[gi]], W=p2.rs)
                            p3 = pbank(); mm4(p3, lambda hh: KD[gi][:, hh, :], lambda hh: VN[gi][:, hh, :], [KD[gi], VN[gi]])
                            for hh in range(4):
                                c.stt("dve", S32[gi][:, hh, :], S32[gi][:, hh, :], col(dec, o4 + hh), p3.ap(128, 128, hh * 128), ALU.mult, ALU.add,
                                      R=p3.rs + [S32[gi], dec], W=[S32[gi]])
                            c.copy("act", FL(Sbf[gi]), FL(S32[gi]), R=[S32[gi]], W=[Sbf[gi]])
                            if d == 0:
                                c.copy("act", FL(O32[gi]), p2.ap(128, 512), R=p2.rs, W=[O32[gi]])
                                c.dma("pool", "st1", og_s[a_:b_, o4 * 128:(o4 + 4) * 128], FL(O32[gi]), R=[O32[gi]], W=[c.res(("og", a_))])
                            else:
                                c.tt("dve", FL(O32[gi]), p2.ap(128, 512), OG[:, hsl(gi), :].rearrange("p a b -> p (a b)"), ALU.add, R=p2.rs + [OG], W=[O32[gi]])
                                for hh in range(4):
                                    c.act(junk[:], O32[gi][:, hh, :], AF.Square, R=[O32[gi]], W=[junk, ssq], accum=ssq[:, 0, o4 + hh:o4 + hh + 1])
                        if d == 1:
                            c.act(ssq[:, 1, :], ssq[:, 0, :], AF.Ln, R=[ssq, epsT], W=[ssq], bias=epsT[:, 0:1], scale=1.0 / 128)
                            c.act(ssq[:, 2, :], ssq[:, 1, :], AF.Exp, R=[ssq], W=[ssq], scale=-0.5)
                            for gi in GI:
                                o4 = gi * 4
                                for hh in range(4):
                                    c.act(ON32[gi][:, hh, :], O32[gi][:, hh, :], AF.Identity, R=[O32[gi], ssq], W=[ON32[gi]], scale=ssq[:, 2, o4 + hh:o4 + hh + 1])
                                p1 = pbank()
                                for hh in range(4):
                                    c.op("pe", lambda e: e.transpose(out=p1.ap(128, 128, hh * 128), in_=ON32[gi][:, hh, :], identity=ident[:]), R=[ON32[gi], ident], W=p1.rs)
                                c.stt("dve", FL(YG[gi]), p1.ap(128, 512), onw[:, 0:1], szc[:, hsl(gi), :].rearrange("p a b -> p (a b)"), ALU.mult, ALU.mult,
                                      R=p1.rs + [onw, szc], W=[YG[gi]])
                                c.dma("pool", "st1", yT_s[(8 + o4) * 128:(12 + o4) * 128, a_:b_].rearrange("(h p) l -> p h l", p=128), YG[gi][:], R=[YG[gi]], W=[c.res("yT")])
                    if not is_s:
                        for h in range(NH):
                            dst = (nsf if d == 0 else nsb)[sq_["idx"], i, h]
                            c.dma("pool", "st2", dst, S32[h // 4][:, h % 4, :], R=[S32[h // 4]], W=[c.res("ns")])
        c.barrier()

    def odd_O1(layer):
        i = layer // 2
        first = False
        with ExitStack() as ph:
            mods = mod_cols(ph, layer, ln_o[i])
            TB = 512
            hTs = [c.sb(f"hT{b}", [128, 16, TB], BF16, ph) for b in range(2)]
            xt = [c.sb(f"xt{b}", [128, D], F32, ph) for b in range(2)]
            ss = c.sb("ss", [128, 4], F32, ph)
            junk = c.sb("junk3", [128, 4, TB], BF16, ph)
            wg = [c.sb(f"wg{b}", [128, 16, 512], BF16, ph) for b in range(2)]
            st32 = [c.sb(f"st32_{b}", [128, TB], F32, ph) for b in range(3)]
            stz = [c.sb(f"stz{b}", [128, TB], BF16, ph) for b in range(2)]
            wv = wb_in_o[i].rearrange("(k p) e -> p k e", p=128)
            nwg = 0; nblk = 0; n32 = 0; nz = 0
            for sq_ in seqs:
                A, Bv = mods[sq_["ci"]]
                L = sq_["L"]
                for b0 in range(0, L, TB):
                    n = min(TB, L - b0)
                    t0 = sq_["t0"] + b0
                    hT = hTs[nblk % 2]; nblk += 1
                    norm_transpose((xt, junk, ss), first, t0, n, A, Bv, hT)
                    for gidx in range(8):
                        c0 = gidx * 512
                        wt = wg[nwg % 2]; nwg += 1
                        for k4 in range(4):
                            c.dma("sp", "ldw", wt[:, k4 * 4:k4 * 4 + 4, :], wv[:, k4 * 4:k4 * 4 + 4, c0:c0 + 512], R=[r_w], W=[wt])
                        for fc in range(4):
                            pb = pbank()
                            for k in range(16):
                                c.mm(pb.ap(128, n), wt[:, k, fc * 128:(fc + 1) * 128], hT[:, k, 0:n], k == 0, k == 15, R=[wt, hT], W=pb.rs)
                            s32 = st32[n32 % 3]; n32 += 1
                            if gidx < 4:
                                if fc % 2 == 0:
                                    c.copy("dve", s32[:, 0:n], pb.ap(128, n), R=pb.rs, W=[s32])
                                else:
                                    c.copy("act", s32[:, 0:n], pb.ap(128, n), R=pb.rs, W=[s32])
                                r0 = c0 + fc * 128
                                c.dma("pool", "st1", pin_s[r0:r0 + 128, t0:t0 + n], s32[:, 0:n], R=[s32], W=[c.res("pin")])
                            else:
                                sz = stz[nz % 2]; nz += 1
                                c.act(sz[:, 0:n], pb.ap(128, n), AF.Silu, R=pb.rs, W=[sz])
                                r0 = c0 - 2048 + fc * 128
                                c.dma("pool", "st1", sz_s[r0:r0 + 128, t0:t0 + n], sz[:, 0:n], R=[sz], W=[c.res("sz")])
        c.barrier()

    def odd_O2(layer):
        i = layer // 2
        with ExitStack() as ph:
            wp = c.sb("wp", [128, 4, 4, 512], BF16, ph)
            for g in range(4):
                c.dma("sp", "ld0", wp[:, g], wb_pool[i, g].rearrange("(k p) e -> p k e", p=128), R=[r_w], W=[wp])
            psc = load_cols(ph, "psc", pool_scale_o[i], 16)
            TB = 512
            HW = 8
            xr = [c.sb(f"pxr{b}", [128, TB + 2 * HW], F32, ph) for b in range(3)]
            sa = [c.sb(f"psa{b}", [128, TB + 2 * HW], F32, ph) for b in range(2)]
            pooled = [[c.sb(f"ppl{b}_{k}", [128, TB], BF16, ph) for k in range(4)] for b in range(2)]
            inv = [c.sb(f"pinv{b}", [128, TB], F32, ph) for b in range(2)]
            szt = [c.sb(f"pszt{b}", [128, TB], BF16, ph) for b in range(2)]
            yst = [c.sb(f"pyst{b}", [128, TB], BF16, ph) for b in range(2)]
            nx = 0; npl = 0; ni = 0; nz = 0
            for sq_ in seqs:
                L = sq_["L"]; ts0 = sq_["t0"]
                pinv_src = k_pinv_s if sq_["kind"] == "s" else k_pinv_p
                for b0 in range(0, L, TB):
                    n = min(TB, L - b0)
                    for g, w in enumerate((2, 4, 8, 16)):
                        iv = inv[ni % 2]; ni += 1
                        c.dma("sp", "ld0", iv[:, 0:n], pinv_src[g, b0:b0 + n].partition_broadcast(128), W=[iv])
                        pl = pooled[npl % 2]; npl += 1
                        for c4 in range(4):
                            x = xr[nx % 3]; nx += 1
                            lo = max(b0 - HW, 0); hi = min(b0 + n + HW, L)
                            if lo > b0 - HW:
                                c.op("pool", lambda e: e.memset(x[:, 0:HW], 0.0), W=[x])
                            if hi < b0 + n + HW:
                                c.op("pool", lambda e: e.memset(x[:, HW + n:HW + n + HW], 0.0), W=[x])
                            r0 = (g * 4 + c4) * 128
                            c.dma("sp", "ldk", x[:, lo - (b0 - HW):hi - (b0 - HW)], pin_s[r0:r0 + 128, ts0 + lo:ts0 + hi], R=[c.res("pin")], W=[x])
                            W_ = n + 2 * HW
                            cur = x; step = 1; length = W_
                            k_ = 0
                            while step < w:
                                dst = sa[k_ % 2]; k_ += 1
                                length -= step
                                c.tt("pool", dst[:, 0:length], cur[:, 0:length], cur[:, step:step + length], ALU.add, R=[cur], W=[dst])
                                cur = dst; step *= 2
                            o0 = HW - w // 2
                            tmpd = sa[k_ % 2]
                            c.tt("dve", tmpd[:, 0:n], cur[:, o0:o0 + n], iv[:, 0:n], ALU.mult, R=[cur, iv], W=[tmpd])
                            c.tt("dve", pl[c4][:, 0:n], tmpd[:, 0:n], x[:, HW:HW + n], ALU.subtract, R=[tmpd, x], W=[pl[c4]])
                        for e_ in range(4):
                            pb = pbank()
                            for c4 in range(4):
                                c.mm(pb.ap(128, n), wp[:, g, c4, e_ * 128:(e_ + 1) * 128], pl[c4][:, 0:n], c4 == 0, c4 == 3, R=[wp, pl[c4]], W=pb.rs)
                            fcx = g * 4 + e_
                            sz = szt[nz % 2]; ys_ = yst[nz % 2]; nz += 1
                            c.dma("sp", "ldq", sz[:, 0:n], sz_s[fcx * 128:(fcx + 1) * 128, ts0 + b0:ts0 + b0 + n], R=[c.res("sz")], W=[sz])
                            c.stt("dve", ys_[:, 0:n], pb.ap(128, n), psc[:, fcx:fcx + 1], sz[:, 0:n], ALU.mult, ALU.mult, R=pb.rs + [psc, sz], W=[ys_])
                            c.dma("pool", "st1", yT_s[fcx * 128:(fcx + 1) * 128, ts0 + b0:ts0 + b0 + n], ys_[:, 0:n], R=[ys_], W=[c.res("yT")])
        c.barrier()

    PH = {}
    PH["even_E1"] = even_E1
    PH["odd_O1"] = odd_O1
    PH["odd_O2"] = odd_O2
    PH["even_E2"] = even_E2
    PH["out_proj"] = out_proj
    PH["even_E3"] = even_E3
    def run_all():
        for layer in range(cfg.depth):
            i = layer // 2
            layer_now[0] = layer
            if layer % 2 == 0:
                even_E1(layer)
                even_E2(layer)
                even_E3(layer)
                out_proj(layer, wb_out_e[i])
            else:
                odd_O1(layer)
                odd_O2(layer)
                out_proj(layer, wb_out_o[i])
        c.barrier()
        nc.all_engine_barrier()
        c.es.close()

    PH["run_all"] = run_all
    return nc, c, PH, locals()


def make_consts(cfg):
    p = np.arange(128)[:, None]
    f = np.arange(128)[None, :]
    neg = lambda m: np.where(m, 0.0, NEG).astype(np.float32)
    bd = ((p // 64) == (f // 64)).astype(np.float32)
    ll = ((p >= 64) & (f < 64)).astype(np.float32)
    ur = ((p < 64) & (f >= 64)).astype(np.float32)
    masks = np.stack([neg(p > f), neg(f >= p), neg(f > p), neg(p >= f), (f > p).astype(np.float32), (p > f).astype(np.float32), bd, ll, ur])
    cum = np.stack([(p <= f).astype(np.float32), (p >= f).astype(np.float32)])
    LS = cfg.ls
    t = np.arange(LS)
    row = (t // 64).astype(np.float32)
    col = (t % 64).astype(np.float32)
    inv_freq = (10000.0 ** (-np.arange(0, 32, 2, dtype=np.float32) / 32)).astype(np.float32)
    cosT = np.zeros((64, LS), np.float32)
    sinS = np.zeros((64, LS), np.float32)
    for q in range(64):
        blk, half, j = q // 32, (q % 32) // 16, q % 16
        ang = (row if blk == 0 else col) * inv_freq[j]
        cosT[q] = np.cos(ang)
        sinS[q] = np.sin(ang) * (-1.0 if half == 0 else 1.0)

    def pinv(L):
        out = np.zeros((4, L), np.float32)
        for gi, w in enumerate((2, 4, 8, 16)):
            lo = np.clip(t[:L] - w // 2, 0, L) if L <= LS else None
            tt_ = np.arange(L)
            lo = np.clip(tt_ - w // 2, 0, L)
            hi = np.clip(tt_ + (w - w // 2), 0, L)
            out[gi] = 1.0 / (hi - lo).astype(np.float32)
        return out

    return dict(k_ident=np.eye(128, dtype=np.float32), k_masks=masks, k_cum=cum, k_rope=np.stack([cosT, sinS]),
                k_pinv_p=pinv(cfg.lp), k_pinv_s=pinv(cfg.ls))


WNAMES = ["ln_e", "mod_w_e", "mod_b_e", "w_in_e", "q_norm_e", "kv_norm_e", "w_uq_e", "w_ukv_e", "conv_e", "o_norm_e", "w_out_e",
          "ln_o", "mod_w_o", "mod_b_o", "w_in_o", "w_pool_o", "pool_scale_o", "w_out_o", "final_norm"]


def core_inputs(cfg, inp, j, consts):
    f = lambda a: np.ascontiguousarray(np.asarray(a, dtype=np.float32))
    m = dict(consts)
    m["xp"] = f(inp["x_prompt"][j * cfg.nps:(j + 1) * cfg.nps]).reshape(cfg.nps * cfg.lp, D)
    m["xs"] = f(inp["x_sample"][j])
    m["cckv"] = f(inp["cache_ckv"][j]); m["ckpe"] = f(inp["cache_kpe"][j])
    m["sfw"] = f(inp["state_fwd"][j]); m["sbw"] = f(inp["state_bwd"][j])
    m["cond"] = f(np.stack([np.asarray(inp["c_ctx"]), np.asarray(inp["c"][j])]))
    for k in WNAMES:
        m[k] = f(inp[k])
    m["a_log_e"] = f(inp["a_log_e"]).reshape(2, 16)
    m["dt_bias_e"] = f(inp["dt_bias_e"]).reshape(2, 16)
    return m


N_CORES = 8


def kernel(**inputs):
    cfg = Cfg(nps=4, lp=256, ls=4096, lc=256, depth=4)
    nc, c, PH, _ = build(cfg)
    PH["run_all"]()
    consts = make_consts(cfg)
    in_maps = [core_inputs(cfg, inputs, j, consts) for j in range(N_CORES)]
    res = run_bass_kernel_spmd(nc, in_maps, core_ids=list(range(N_CORES)))
    rs = res.results
    f = lambda k, shp: np.concatenate([np.asarray(r[k], dtype=np.float32).reshape(shp) for r in rs], axis=0)
    y_prompt = f("yp", (4, 256, D))
    y_sample = f("ys", (1, 4096, D))
    nckv = f("nckv", (4, 2, 256, 512))
    nkpe = f("nkpe", (4, 2, 256, 64))
    nsf = f("nsf", (4, 2, NH, 128, 128))
    nsb = f("nsb", (4, 2, NH, 128, 128))
    return (y_prompt, y_sample, nckv, nkpe, nsf, nsb)
```

```python
import numpy as np
import ml_dtypes
from contextlib import ExitStack
import concourse.bass as bass
import concourse.mybir as mybir
from concourse.bass_utils import run_bass_kernel_spmd

F32 = mybir.dt.float32
BF16 = mybir.dt.bfloat16
AF = mybir.ActivationFunctionType
ALU = mybir.AluOpType
AX = mybir.AxisListType

D = 2048
NH = 8
IN_EVEN = 6240
EPS = 1e-6
NEG = -1.0e9


class Res:
    __slots__ = ("w", "r", "excl")

    def __init__(self, excl=False):
        self.w = None
        self.r = {}
        self.excl = excl


class Tile:
    def __init__(self, t, res=None):
        self.t = t
        self.res = res if res is not None else Res()

    def __getitem__(self, k):
        return self.t[k]


class Ctx:
    def __init__(self, nc):
        self.nc = nc
        self.es = ExitStack()
        self.eng = {"pe": nc.tensor, "act": nc.scalar, "dve": nc.vector, "pool": nc.gpsimd, "sp": nc.sync}
        self.sems = {}
        self.count = {}
        self.seen = {e: {} for e in self.eng}
        for e in ("pe", "act", "dve", "pool"):
            self._sem(e)
        self.resd = {}
        self.n_ins = 0
        self.dead = False
        self.ring_n = {}

    def _sem(self, name):
        if name not in self.sems:
            self.sems[name] = self.es.enter_context(self.nc.semaphore("s_" + name))
            self.count[name] = 0
        return self.sems[name]

    def res(self, key):
        r = self.resd.get(key)
        if r is None:
            r = self.resd[key] = Res()
        return r

    def sb(self, name, shape, dt, stack=None):
        self.n_sb = getattr(self, "n_sb", 0) + 1
        name = f"{name}_u{self.n_sb}"
        t = (stack or self.es).enter_context(self.nc.sbuf_tensor(name, list(shape), dt))
        return Tile(t)

    def _rs(self, x):
        return x.res if isinstance(x, Tile) else x

    def _waits(self, e, R, W):
        need = {}
        for r in R:
            r = self._rs(r)
            if r.w is not None:
                s, v = r.w
                if need.get(s, 0) < v:
                    need[s] = v
            if r.excl:
                for s, v in r.r.items():
                    if s != e and need.get(s, 0) < v:
                        need[s] = v
        for w in W:
            w = self._rs(w)
            if w.w is not None:
                s, v = w.w
                if need.get(s, 0) < v:
                    need[s] = v
            for s, v in w.r.items():
                if need.get(s, 0) < v:
                    need[s] = v
        seen = self.seen[e]
        for s, v in need.items():
            if e == "pe" and s == "pe":
                continue
            if seen.get(s, 0) < v:
                self.eng[e].wait_ge(self.sems[s], v)
                seen[s] = v

    def _mark(self, ticket, R, W):
        s, v = ticket
        for r in R:
            r = self._rs(r)
            if r.r.get(s, 0) < v:
                r.r[s] = v
        for w in W:
            w = self._rs(w)
            w.w = ticket
            w.r = {}

    def op(self, e, emit, R=(), W=()):
        if self.dead:
            return None
        self._waits(e, R, W)
        ins = emit(self.eng[e])
        self.count[e] += 1
        ins.then_inc(self.sems[e], 1)
        self._mark((e, self.count[e]), R, W)
        self.n_ins += 1
        return ins

    RING = 8

    def dma(self, q, stream, out, in_, R=(), W=(), **kw):
        if self.dead:
            return None
        n = self.ring_n.get(stream, 0)
        self.ring_n[stream] = n + 1
        sname = f"{stream}_{n % self.RING}"
        self._sem(sname)
        prev = self.count[sname]
        if prev > 0 and self.seen[q].get(sname, 0) < prev:
            self.eng[q].wait_ge(self.sems[sname], prev)
            self.seen[q][sname] = prev
        self._waits(q, R, W)
        ins = self.eng[q].dma_start(out=out, in_=in_, **kw)
        self.count[sname] += 16
        ins.then_inc(self.sems[sname], 16)
        self._mark((sname, self.count[sname]), R, W)
        self.n_ins += 1

    def barrier(self):
        if self.dead:
            self.dead = False
        for e in self.eng:
            seen = self.seen[e]
            for s, v in self.count.items():
                if e == "pe" and s == "pe":
                    continue
                if v > 0 and seen.get(s, 0) < v:
                    self.eng[e].wait_ge(self.sems[s], v)
                    seen[s] = v

    def mm(self, out, lhsT, rhs, start, stop, R, W):
        return self.op("pe", lambda e: e.matmul(out, lhsT=lhsT, rhs=rhs, start=start, stop=stop), R, W)

    def act(self, out, in_, func, R, W, bias=None, scale=None, accum=None, eng="act"):
        kw = {}
        if bias is not None:
            kw["bias"] = bias
        if scale is not None:
            kw["scale"] = scale
        if accum is not None:
            kw["accum_out"] = accum
        return self.op(eng, lambda e: e.activation(out=out, in_=in_, func=func, **kw), R, W)

    def copy(self, eng, out, in_, R, W):
        if eng == "pool":
            eng = "dve"
        if eng == "act":
            return self.op("act", lambda e: e.copy(out=out, in_=in_), R, W)
        return self.op(eng, lambda e: e.tensor_copy(out=out, in_=in_), R, W)

    def tt(self, eng, out, in0, in1, op, R, W):
        if eng == "pool":
            eng = "dve"
        return self.op(eng, lambda e: e.tensor_tensor(out=out, in0=in0, in1=in1, op=op), R, W)

    def ts(self, eng, out, in0, s1, op0, R, W, s2=None, op1=None):
        if eng == "pool":
            eng = "dve"
        if op1 is None:
            return self.op(eng, lambda e: e.tensor_scalar(out=out, in0=in0, scalar1=s1, scalar2=None, op0=op0), R, W)
        return self.op(eng, lambda e: e.tensor_scalar(out=out, in0=in0, scalar1=s1, scalar2=s2, op0=op0, op1=op1), R, W)

    def stt(self, eng, out, in0, scalar, in1, op0, op1, R, W):
        eng = "dve"
        return self.op(eng, lambda e: e.scalar_tensor_tensor(out=out, in0=in0, scalar=scalar, in1=in1, op0=op0, op1=op1), R, W)


class StopBuild(Exception):
    pass


class Cfg:
    def __init__(self, nps=4, lp=256, ls=4096, lc=256, depth=4, debug=False):
        self.nps, self.lp, self.ls, self.lc, self.depth, self.debug = nps, lp, ls, lc, depth, debug
        self.n_even = (depth + 1) // 2
        self.n_odd = depth // 2
        self.stop = None
        self.neu32 = True

    def chk(self, k):
        if self.stop == k:
            self.ctx.dead = True


def build(cfg):
    nc = bass.Bass("TRN2", target_bir_lowering=False)
    c = Ctx(nc)
    cfg.ctx = c
    NPS, LP, LS, LC = cfg.nps, cfg.lp, cfg.ls, cfg.lc
    NE, NO = cfg.n_even, cfg.n_odd
    NPT = NPS * LP

    def din(name, shape, dt=F32):
        return nc.dram_tensor(name, list(shape), dt, kind="ExternalInput").ap()

    def dout(name, shape, dt=F32):
        return nc.dram_tensor(name, list(shape), dt, kind="ExternalOutput").ap()

    def dscr(name, shape, dt=F32):
        kind = "ExternalOutput" if cfg.debug else "Internal"
        return nc.dram_tensor(name, list(shape), dt, kind=kind).ap()

    xp = din("xp", [NPT, D])
    xs = din("xs", [LS, D])
    cckv = din("cckv", [2, LC, 512])
    ckpe = din("ckpe", [2, LC, 64])
    sfw = din("sfw", [2, NH, 128, 128])
    sbw = din("sbw", [2, NH, 128, 128])
    cond = din("cond", [2, D])
    ln_e = din("ln_e", [2, D]); mod_w_e = din("mod_w_e", [2, D, 3 * D]); mod_b_e = din("mod_b_e", [2, 3 * D])
    w_in_e = din("w_in_e", [2, D, IN_EVEN]); q_norm_e = din("q_norm_e", [2, 512]); kv_norm_e = din("kv_norm_e", [2, 512])
    w_uq_e = din("w_uq_e", [2, 512, 1536]); w_ukv_e = din("w_ukv_e", [2, 512, 2048]); conv_e = din("conv_e", [2, 3, 3072])
    a_log_e = din("a_log_e", [2, 16]); dt_bias_e = din("dt_bias_e", [2, 16]); o_norm_e = din("o_norm_e", [2, 128])
    w_out_e = din("w_out_e", [2, D, D])
    ln_o = din("ln_o", [2, D]); mod_w_o = din("mod_w_o", [2, D, 3 * D]); mod_b_o = din("mod_b_o", [2, 3 * D])
    w_in_o = din("w_in_o", [2, D, 2 * D]); w_pool_o = din("w_pool_o", [2, 4, 512, 512]); pool_scale_o = din("pool_scale_o", [2, D])
    w_out_o = din("w_out_o", [2, D, D]); final_norm = din("final_norm", [D])
    k_ident = din("k_ident", [128, 128])
    k_masks = din("k_masks", [9, 128, 128])
    k_cum = din("k_cum", [2, 128, 128])
    k_rope = din("k_rope", [2, 64, LS])
    k_pinv_p = din("k_pinv_p", [4, LP]); k_pinv_s = din("k_pinv_s", [4, LS])

    yp = dout("yp", [NPT, D]); ys = dout("ys", [LS, D])
    nckv = dout("nckv", [NPS, 2, LP, 512]); nkpe = dout("nkpe", [NPS, 2, LP, 64])
    nsf = dout("nsf", [NPS, 2, NH, 128, 128]); nsb = dout("nsb", [NPS, 2, NH, 128, 128])

    xw_p = dscr("xw_p", [NPT, D]); xw_s = dscr("xw_s", [LS, D])
    modv = dscr("modv", [4, 2, 3 * D])
    wb_in_e = dscr("wb_in_e", [NE, D, IN_EVEN], BF16); wb_uq = dscr("wb_uq", [NE, 512, 1536], BF16)
    wb_ukv = dscr("wb_ukv", [NE, 512, 2048], BF16); wb_out_e = dscr("wb_out_e", [NE, D, D], BF16)
    wb_in_o = dscr("wb_in_o", [max(NO, 1), D, 2 * D], BF16); wb_pool = dscr("wb_pool", [max(NO, 1), 4, 512, 512], BF16)
    wb_out_o = dscr("wb_out_o", [max(NO, 1), D, D], BF16)
    LT = NPT + LS
    LKS = LC + LS
    qn_s = dscr("qn_s", [NH, 128, LT], BF16); qr_s = dscr("qr_s", [NH, 64, LT], BF16)
    kn_p = dscr("kn_p", [NH, 128, NPT], BF16); kr_p = dscr("kr_p", [64, NPT], BF16); v_p = dscr("v_p", [NPT, 1024], BF16)
    kn_x = dscr("kn_x", [NH, 128, LKS], BF16); kr_x = dscr("kr_x", [64, LKS], BF16); v_x = dscr("v_x", [LKS, 1024], BF16)
    qkv_s = dscr("qkv_s", [3072, LT]); ab_s = dscr("ab_s", [LT, 32]); abT_s = dscr("abT_s", [32, LT])
    sz_s = dscr("sz_s", [D, LT], BF16); yT_s = dscr("yT_s", [D, LT], BF16)
    gq_s = dscr("gq_s", [NH, 128, LT], BF16); gk_s = dscr("gk_s", [NH, 128, LT], BF16); gv_s = dscr("gv_s", [NH, 128, LT], BF16)
    gkb_s = dscr("gkb_s", [2, NH, 128, LT], BF16)
    og_s = dscr("og_s", [LT, 1024])
    pin_s = dscr("pin_s", [D, LT])

    seqs = [dict(t0=i * LP, L=LP, ci=0, kind="p", idx=i) for i in range(NPS)] + [dict(t0=NPT, L=LS, ci=1, kind="s", idx=0)]

    def xrows(src_first, t0, n):
        if t0 < NPT:
            return (xp if src_first else xw_p)[t0:t0 + n, :]
        return (xs if src_first else xw_s)[t0 - NPT:t0 - NPT + n, :]

    dbg_n = [0]
    layer_now = [0]

    def dbg(name, ap, shape, dt=F32, R=()):
        if not cfg.debug or layer_now[0] != 0:
            return
        t = nc.dram_tensor("dbg_" + name, list(shape), dt, kind="ExternalOutput").ap()
        c.dma("pool", "st2", t, ap, R=list(R))

    ident = c.sb("ident", [128, 128], F32)
    identb = c.sb("identb", [128, 128], BF16)
    ones32 = c.sb("ones32", [128, 128], F32)
    onesb = c.sb("onesb", [128, 128], BF16)
    epsT = c.sb("epsT", [128, 1], F32)
    oneT = c.sb("oneT", [128, 1], F32)
    c.dma("sp", "ld0", ident[:], k_ident, W=[ident])
    c.op("pool", lambda e: e.memset(ones32[:], 1.0), W=[ones32])
    c.op("pool", lambda e: e.memset(onesb[:], 1.0), W=[onesb])
    c.op("pool", lambda e: e.memset(epsT[:], EPS), W=[epsT])
    c.op("pool", lambda e: e.memset(oneT[:], 1.0), W=[oneT])
    c.copy("dve", identb[:], ident[:], R=[ident], W=[identb])

    banks = [c.es.enter_context(nc.psum_tensor(f"ps{i}", [128, 512], F32)) for i in range(8)]
    bank_res = [Res(excl=True) for _ in range(8)]

    class PS:
        def __init__(self, b, c0, n):
            self.b, self.c0, self.n = b, c0, n
            self.rs = [bank_res[b]]

        def ap(self, p=128, n=None, off=0):
            n = self.n if n is None else n
            return banks[self.b][0:p, self.c0 + off:self.c0 + off + n]

    st = {"slot": 0}

    def pbank(b=None):
        if b is None:
            s = (st["slot"] + 3) // 4 * 4 % 32
            st["slot"] = (s + 4) % 32
            b = s // 4
        return PS(b, 0, 512)

    def palign():
        st["slot"] = (st["slot"] + 3) // 4 * 4 % 32

    def pslots4():
        palign()
        return [pslot() for _ in range(4)]

    def pslot():
        s = st["slot"]
        st["slot"] = (s + 1) % 32
        return PS(s // 4, (s % 4) * 128, 128)

    r_w = c.res("wcast")

    def wcast(dst, src, rows_per=512):
        n = src.shape[0]
        for r0 in range(0, n, rows_per):
            r1 = min(n, r0 + rows_per)
            c.dma("pool", "wc", dst[r0:r1], src[r0:r1], W=[r_w])

    for i in range(NE):
        wcast(wb_in_e[i], w_in_e[i]); wcast(wb_uq[i], w_uq_e[i]); wcast(wb_ukv[i], w_ukv_e[i]); wcast(wb_out_e[i], w_out_e[i])
    for i in range(NO):
        wcast(wb_in_o[i], w_in_o[i]); wcast(wb_out_o[i], w_out_o[i])
        for g in range(4):
            wcast(wb_pool[i, g], w_pool_o[i, g])

    r_modv = c.res("modv")
    with ExitStack() as ph:
        condT = c.sb("condT", [128, 16, 2], F32, ph)
        scT = c.sb("scT", [128, 16, 2], F32, ph)
        sgT = c.sb("sgT", [128, 16, 2], F32, ph)
        for ci_ in range(2):
            c.dma("sp", "ld0", condT[:, :, ci_], cond[ci_].rearrange("(k p) -> p k", p=128), W=[condT], allow_slow_non_contiguous=True)
        c.act(scT[:], condT[:], AF.Exp, R=[condT], W=[scT], scale=-1.0)
        c.ts("dve", scT[:], scT[:], 1.0, ALU.add, R=[scT], W=[scT])
        c.op("dve", lambda e: e.reciprocal(out=scT[:], in_=scT[:]), R=[scT], W=[scT])
        c.tt("dve", sgT[:], condT[:], scT[:], ALU.mult, R=[condT, scT], W=[sgT])
        wts = [c.sb(f"mw{i}", [128, 4, 512], F32, ph) for i in range(3)]
        mb = c.sb("mb", [2, 3 * D], F32, ph)
        mo = c.sb("mo", [2, 3 * D], F32, ph)
        nld = 0
        for layer in range(cfg.depth):
            i = layer // 2
            mw = (mod_w_e if layer % 2 == 0 else mod_w_o)[i]
            mbv = (mod_b_e if layer % 2 == 0 else mod_b_o)[i]
            c.dma("sp", "ld0", mb[:], mbv.partition_broadcast(2), W=[mb], R=[])
            for n in range(12):
                pb = pbank()
                for kg in range(4):
                    wt = wts[nld % 3]; nld += 1
                    c.dma("sp", "ldw", wt[:], mw[kg * 512:(kg + 1) * 512, n * 512:(n + 1) * 512].rearrange("(k p) e -> p k e", p=128), W=[wt])
                    for kk in range(4):
                        k = kg * 4 + kk
                        c.mm(pb.ap(2), sgT[:, k, :], wt[:, kk, :], k == 0, k == 15, R=[sgT, wt], W=pb.rs)
                c.tt("dve", mo[:, n * 512:(n + 1) * 512], pb.ap(2), mb[:, n * 512:(n + 1) * 512], ALU.add, R=pb.rs + [mb], W=[mo])
            c.dma("sp", "st0", modv[layer], mo[:], R=[mo], W=[r_modv])
    c.barrier()

    def load_cols(ph, name, vec, nchunk, q="sp"):
        t = c.sb(name, [128, nchunk], F32, ph)
        c.dma(q, "ld0", t[:], vec.rearrange("(k p) -> p k", p=128), W=[t], R=[r_modv], allow_slow_non_contiguous=True)
        return t

    def norm_transpose(ph_tiles, src_first, t0, n, A, B, hT):
        xt, junk, ss = ph_tiles
        for j in range(n // 128):
            xn = xt[j % 2]
            c.dma("sp", "ldx", xn[:], xrows(src_first, t0 + j * 128, 128), R=[c.res(("x", (t0 + j * 128) // 128))], W=[xn])
            c.act(junk[:].rearrange("p a b -> p (a b)")[:, 0:D], xn[:], AF.Square, R=[xn], W=[junk, ss], accum=ss[:, 0:1])
            c.act(ss[:, 1:2], ss[:, 0:1], AF.Ln, R=[ss, epsT], W=[ss], bias=epsT[:, 0:1], scale=1.0 / D)
            c.act(ss[:, 2:3], ss[:, 1:2], AF.Exp, R=[ss], W=[ss], scale=-0.5)
            c.ts("dve", xn[:], xn[:], ss[:, 2:3], ALU.mult, R=[xn, ss], W=[xn])
            for kg in range(4):
                pb = pbank()
                for kk in range(4):
                    k = kg * 4 + kk
                    c.op("pe", lambda e: e.transpose(out=pb.ap(128, 128, kk * 128), in_=xn[:, k * 128:(k + 1) * 128], identity=ident[:]), R=[xn, ident], W=pb.rs)
                for kk in range(4):
                    k = kg * 4 + kk
                    eng = "dve" if kk % 2 == 0 else "pool"
                    if eng == "pool":
                        c.act(hT[:, k, j * 128:(j + 1) * 128], pb.ap(128, 128, kk * 128), AF.Identity, R=pb.rs + [A, B], W=[hT],
                              bias=B[:, k:k + 1], scale=A[:, k:k + 1])
                    else:
                        c.ts("dve", hT[:, k, j * 128:(j + 1) * 128], pb.ap(128, 128, kk * 128), A[:, k:k + 1], ALU.mult, R=pb.rs + [A, B], W=[hT],
                             s2=B[:, k:k + 1], op1=ALU.add)

    def mod_cols(ph, layer, lnw_vec):
        lnw = load_cols(ph, f"lnw{layer}", lnw_vec, 16)
        outs = []
        for ci in range(2):
            sh = load_cols(ph, f"sh{layer}_{ci}", modv[layer, ci, 0:D], 16)
            sc = load_cols(ph, f"sc{layer}_{ci}", modv[layer, ci, D:2 * D], 16)
            A = c.sb(f"A{layer}_{ci}", [128, 16], F32, ph)
            c.stt("dve", A[:], sc[:], 1.0, lnw[:], ALU.add, ALU.mult, R=[sc, lnw], W=[A])
            outs.append((A, sh))
        return outs

    def rstd_from_ps(ps_sum, n, inv_n, out_t, tmp_t):
        c.act(tmp_t[:, 0:n], ps_sum.ap(128, n), AF.Ln, R=ps_sum.rs + [epsT], W=[tmp_t], bias=epsT[:, 0:1], scale=inv_n)
        c.act(out_t[:, 0:n], tmp_t[:, 0:n], AF.Exp, R=[tmp_t], W=[out_t], scale=-0.5)

    def even_E1(layer):
        i = layer // 2
        first = layer == 0
        with ExitStack() as ph:
            mods = mod_cols(ph, layer, ln_e[i])
            qg = load_cols(ph, "qg", q_norm_e[i], 4)
            kg_ = load_cols(ph, "kvg", kv_norm_e[i], 4)
            wuq = c.sb("wuq", [128, 4, 1536], BF16, ph)
            wuqp = c.sb("wuqp", [128, 4, 8, 64], BF16, ph)
            wukk = c.sb("wukk", [128, 4, 8, 128], BF16, ph)
            wukv = c.sb("wukv", [128, 4, 8, 128], BF16, ph)
            c.dma("sp", "ld0", wuq[:], wb_uq[i].rearrange("(k p) e -> p k e", p=128), R=[r_w], W=[wuq])
            uqv = wb_uq[i].rearrange("(k p) (h e) -> p k h e", p=128, e=192)
            for blk in range(2):
                for half in range(2):
                    src = uqv[:, :, :, 128 + blk * 32 + (1 - half) * 16:128 + blk * 32 + (1 - half) * 16 + 16]
                    for k in range(4):
                        c.dma("sp", "ld0", wuqp[:, k, :, blk * 32 + half * 16:blk * 32 + half * 16 + 16], src[:, k], R=[r_w], W=[wuqp],
                              allow_slow_non_contiguous=True)
            ukvv = wb_ukv[i].rearrange("(k p) (h t e) -> p k h t e", p=128, t=2, e=128)
            for k in range(4):
                c.dma("sp", "ld0", wukk[:, k], ukvv[:, k, :, 0, :], R=[r_w], W=[wukk])
                c.dma("sp", "ld0", wukv[:, k], ukvv[:, k, :, 1, :], R=[r_w], W=[wukv])
            TB = 512
            hTs = [c.sb(f"hT{b}", [128, 16, TB], BF16, ph) for b in range(2)]
            xt = [c.sb(f"xt{b}", [128, D], F32, ph) for b in range(2)]
            ss = c.sb("ss", [128, 4], F32, ph)
            wg = [c.sb(f"wg{b}", [128, 16, 512], BF16, ph) for b in range(2)]
            wsm = c.sb("wsm", [128, 16, 128 + 32], BF16, ph)
            raw0 = c.sb("raw0", [128, 4, TB], F32, ph)
            raw = [raw0, raw0]
            sq = c.sb("sq", [128, 4, TB], BF16, ph)
            rstd = c.sb("rstd", [128, TB], F32, ph)
            tmpn = c.sb("tmpn", [128, TB], F32, ph)
            cqn = c.sb("cqn", [128, 4, TB], BF16, ph)
            ckn = c.sb("ckn", [128, 4, TB], BF16, ph)
            ckn32 = raw0
            stq = c.sb("stq", [128, 2, TB], BF16, ph)
            stqr = c.sb("stqr", [64, 2, TB], BF16, ph)
            stk = c.sb("stk", [128, 2, TB], BF16, ph)
            stv = [c.sb(f"stv{b}", [128, 1024], BF16, ph) for b in range(2)]
            stkr = c.sb("stkr", [64, TB], BF16, ph)
            kpe32 = c.sb("kpe32", [64, TB], F32, ph)
            st32 = [c.sb(f"st32_{b}", [128, TB], F32, ph) for b in range(3)]
            stz = [c.sb(f"stz{b}", [128, TB], BF16, ph) for b in range(2)]
            stab = c.sb("stab", [128, 32], F32, ph)
            otok = [c.sb(f"otok{b}", [128, 576], F32, ph) for b in range(2)]
            ropec = c.sb("ropec", [64, TB], F32, ph)
            ropes = c.sb("ropes", [64, TB], F32, ph)
            rt1 = c.sb("rt1", [64, TB], F32, ph)
            rt2 = c.sb("rt2", [64, TB], F32, ph)
            wv = wb_in_e[i].rearrange("(k p) e -> p k e", p=128)
            for k4 in range(4):
                c.dma("sp", "ld0", wsm[:, k4 * 4:k4 * 4 + 4, 0:64], wv[:, k4 * 4:k4 * 4 + 4, 1024:1088], R=[r_w], W=[wsm])
            for blk in range(2):
                for half in range(2):
                    c0 = 1024 + blk * 32 + (1 - half) * 16
                    for k4 in range(4):
                        c.dma("sp", "ld0", wsm[:, k4 * 4:k4 * 4 + 4, 64 + blk * 32 + half * 16:64 + blk * 32 + half * 16 + 16],
                              wv[:, k4 * 4:k4 * 4 + 4, c0:c0 + 16], R=[r_w], W=[wsm], allow_slow_non_contiguous=True)
            for k4 in range(4):
                c.dma("sp", "ld0", wsm[:, k4 * 4:k4 * 4 + 4, 128:160], wv[:, k4 * 4:k4 * 4 + 4, 4160:4192], R=[r_w], W=[wsm])
            groups = [(0, 512, "cq"), (512, 512, "ckv")] + [(1088 + g * 512, 512, "qkv") for g in range(6)] + \
                     [(4192 + g * 512, 512, "z") for g in range(4)]
            nwg = [0]
            nblk = 0
            nst = [0, 0, 0]

            def sumsq_norm(rawt, n, gcol, outb, out32):
                pb = pbank()
                for k in range(4):
                    c.mm(pb.ap(128, n), onesb[:], sq[:, k, 0:n], k == 0, k == 3, R=[onesb, sq], W=pb.rs)
                rstd_from_ps(pb, n, 1.0 / 512, rstd, tmpn)
                for k in range(4):
                    c.stt("dve", outb[:, k, 0:n], rawt[:, k, 0:n], gcol[:, k:k + 1], rstd[:, 0:n], ALU.mult, ALU.mult, R=[rawt, gcol, rstd], W=[outb])
                    if out32 is not None:
                        c.stt("pool", out32[:, k, 0:n], rawt[:, k, 0:n], gcol[:, k:k + 1], rstd[:, 0:n], ALU.mult, ALU.mult, R=[rawt, gcol, rstd, outb], W=[out32])

            def kv_up(src_bf, n, kn_dst, v_dst_rows):
                for h in range(NH):
                    pb = pbank()
                    for k in range(4):
                        c.mm(pb.ap(128, n), wukk[:, k, h, :], src_bf[:, k, 0:n], k == 0, k == 3, R=[wukk, src_bf], W=pb.rs)
                    if h % 2 == 0:
                        c.copy("dve", stk[:, h % 2, 0:n], pb.ap(128, n), R=pb.rs, W=[stk])
                    else:
                        c.copy("act", stk[:, h % 2, 0:n], pb.ap(128, n), R=pb.rs, W=[stk])
                    c.dma("pool", "st1", kn_dst(h), stk[:, h % 2, 0:n], R=[stk], W=[c.res("kn")])
                for j in range(n // 128):
                    sv = stv[nst[0] % 2]; nst[0] += 1
                    for half in range(2):
                        pb = pbank()
                        for k in range(4):
                            c.mm(pb.ap(128, 512), src_bf[:, k, j * 128:(j + 1) * 128], wukv[:, k, half * 4:(half + 1) * 4, :], k == 0, k == 3,
                                 R=[src_bf, wukv], W=pb.rs)
                        if half == 0:
                            c.copy("dve", sv[:, 0:512], pb.ap(128, 512), R=pb.rs, W=[sv])
                        else:
                            c.copy("act", sv[:, 512:1024], pb.ap(128, 512), R=pb.rs, W=[sv])
                    c.dma("pool", "st1", v_dst_rows(j), sv[:], R=[sv], W=[c.res("v")])

            for sq_ in seqs:
                A, Bv = mods[sq_["ci"]]
                is_s = sq_["kind"] == "s"
                L = sq_["L"]
                for b0 in range(0, L, TB):
                    n = min(TB, L - b0)
                    t0 = sq_["t0"] + b0
                    hT = hTs[nblk % 2]; nblk += 1
                    cfg.chk(1 + (10 if is_s else 0))
                    norm_transpose((xt, sq, ss), first, t0, n, A, Bv, hT)
                    cfg.chk(2 + (10 if is_s else 0))
                    if is_s:
                        c.dma("sp", "ld0", ropec[:, 0:n], k_rope[0, :, b0:b0 + n], W=[ropec])
                        c.dma("sp", "ld0", ropes[:, 0:n], k_rope[1, :, b0:b0 + n], W=[ropes])
                    for (c0, ncol, kind) in groups:
                        wt = wg[nwg[0] % 2]; nwg[0] += 1
                        for k4 in range(4):
                            c.dma("sp", "ldw", wt[:, k4 * 4:k4 * 4 + 4, 0:ncol], wv[:, k4 * 4:k4 * 4 + 4, c0:c0 + ncol], R=[r_w], W=[wt])
                        for fc in range(ncol // 128):
                            pb = pbank()
                            for k in range(16):
                                c.mm(pb.ap(128, n), wt[:, k, fc * 128:(fc + 1) * 128], hT[:, k, 0:n], k == 0, k == 15, R=[wt, hT], W=pb.rs)
                            if kind in ("cq", "ckv"):
                                rw = raw[0 if kind == "cq" else 1]
                                c.copy("dve", rw[:, fc, 0:n], pb.ap(128, n), R=pb.rs, W=[rw])
                                c.act(sq[:, fc, 0:n], rw[:, fc, 0:n], AF.Square, R=[rw], W=[sq])
                            elif kind == "qkv":
                                s32 = st32[nst[1] % 3]; nst[1] += 1
                                if fc % 2 == 0:
                                    c.copy("dve", s32[:, 0:n], pb.ap(128, n), R=pb.rs, W=[s32])
                                else:
                                    c.copy("act", s32[:, 0:n], pb.ap(128, n), R=pb.rs, W=[s32])
                                r0 = c0 - 1088 + fc * 128
                                c.dma("pool", "st1", qkv_s[r0:r0 + 128, t0:t0 + n], s32[:, 0:n], R=[s32], W=[c.res("qkv")])
                            else:
                                sz = stz[nst[2] % 2]; nst[2] += 1
                                c.act(sz[:, 0:n], pb.ap(128, n), AF.Silu, R=pb.rs, W=[sz])
                                r0 = c0 - 4192 + fc * 128
                                c.dma("pool", "st1", sz_s[r0:r0 + 128, t0:t0 + n], sz[:, 0:n], R=[sz], W=[c.res("sz")])
                        cfg.chk(3 + (10 if is_s else 0))
                        if kind == "cq":
                            sumsq_norm(raw[0], n, qg, cqn, None)
                            for h in range(NH):
                                pb = pbank()
                                for k in range(4):
                                    c.mm(pb.ap(128, n), wuq[:, k, h * 192:h * 192 + 128], cqn[:, k, 0:n], k == 0, k == 3, R=[wuq, cqn], W=pb.rs)
                                c.copy("act", stq[:, h % 2, 0:n], pb.ap(128, n), R=pb.rs, W=[stq])
                                c.dma("pool", "st1", qn_s[h, :, t0:t0 + n], stq[:, h % 2, 0:n], R=[stq], W=[c.res("qn")])
                                pr = pbank()
                                for k in range(4):
                                    c.mm(pr.ap(64, n), wuq[:, k, h * 192 + 128:h * 192 + 192], cqn[:, k, 0:n], k == 0, k == 3, R=[wuq, cqn], W=pr.rs)
                                if is_s:
                                    pr2 = pbank()
                                    for k in range(4):
                                        c.mm(pr2.ap(64, n), wuqp[:, k, h, :], cqn[:, k, 0:n], k == 0, k == 3, R=[wuqp, cqn], W=pr2.rs)
                                    c.tt("dve", rt1[:, 0:n], pr.ap(64, n), ropec[:, 0:n], ALU.mult, R=pr.rs + [ropec], W=[rt1])
                                    c.tt("dve", rt2[:, 0:n], pr2.ap(64, n), ropes[:, 0:n], ALU.mult, R=pr2.rs + [ropes], W=[rt2])
                                    c.tt("pool", stqr[:, h % 2, 0:n], rt1[:, 0:n], rt2[:, 0:n], ALU.add, R=[rt1, rt2], W=[stqr])
                                else:
                                    c.copy("dve", stqr[:, h % 2, 0:n], pr.ap(64, n), R=pr.rs, W=[stqr])
                                c.dma("pool", "st1", qr_s[h, :, t0:t0 + n], stqr[:, h % 2, 0:n], R=[stqr], W=[c.res("qr")])
                        if kind == "ckv":
                            cfg.chk(4 + (10 if is_s else 0))
                            sumsq_norm(raw[1], n, kg_, ckn, None if is_s else ckn32)
                            if is_s:
                                kv_up(ckn, n, lambda h: kn_x[h, :, LC + b0:LC + b0 + n],
                                      lambda j: v_x[LC + b0 + j * 128:LC + b0 + (j + 1) * 128, :])
                            else:
                                kv_up(ckn, n, lambda h: kn_p[h, :, t0:t0 + n],
                                      lambda j: v_p[t0 + j * 128:t0 + (j + 1) * 128, :])
                    cfg.chk(5 + (10 if is_s else 0))
                    pk = pbank()
                    for k in range(16):
                        c.mm(pk.ap(64, n), wsm[:, k, 0:64], hT[:, k, 0:n], k == 0, k == 15, R=[wsm, hT], W=pk.rs)
                    if is_s:
                        pk2 = pbank()
                        for k in range(16):
                            c.mm(pk2.ap(64, n), wsm[:, k, 64:128], hT[:, k, 0:n], k == 0, k == 15, R=[wsm, hT], W=pk2.rs)
                        c.tt("dve", rt1[:, 0:n], pk.ap(64, n), ropec[:, 0:n], ALU.mult, R=pk.rs + [ropec], W=[rt1])
                        c.tt("dve", rt2[:, 0:n], pk2.ap(64, n), ropes[:, 0:n], ALU.mult, R=pk2.rs + [ropes], W=[rt2])
                        c.tt("pool", stkr[:, 0:n], rt1[:, 0:n], rt2[:, 0:n], ALU.add, R=[rt1, rt2], W=[stkr])
                        c.dma("pool", "st1", kr_x[:, LC + b0:LC + b0 + n], stkr[:, 0:n], R=[stkr], W=[c.res("kr")])
                    else:
                        c.copy("dve", kpe32[:, 0:n], pk.ap(64, n), R=pk.rs, W=[kpe32])
                        c.copy("act", stkr[:, 0:n], kpe32[:, 0:n], R=[kpe32], W=[stkr])
                        c.dma("pool", "st1", kr_p[:, t0:t0 + n], stkr[:, 0:n], R=[stkr], W=[c.res("kr")])
                        for j in range(n // 128):
                            ot = otok[j % 2]
                            pb = pbank()
                            for k in range(4):
                                c.op("pe", lambda e: e.transpose(out=pb.ap(128, 128, k * 128), in_=ckn32[:, k, j * 128:(j + 1) * 128], identity=ident[:]),
                                     R=[ckn32, ident], W=pb.rs)
                            c.copy("dve", ot[:, 0:512], pb.ap(128, 512), R=pb.rs, W=[ot])
                            pb2 = pslot()
                            c.op("pe", lambda e: e.transpose(out=pb2.ap(128, 64), in_=kpe32[:, j * 128:(j + 1) * 128], identity=ident[0:64, 0:64]),
                                 R=[kpe32, ident], W=pb2.rs)
                            c.copy("act", ot[:, 512:576], pb2.ap(128, 64), R=pb2.rs, W=[ot])
                            l0 = b0 + j * 128
                            c.dma("pool", "st2", nckv[sq_["idx"], i, l0:l0 + 128, :], ot[:, 0:512], R=[ot], W=[c.res("nckv")])
                            c.dma("pool", "st2", nkpe[sq_["idx"], i, l0:l0 + 128, :], ot[:, 512:576], R=[ot], W=[c.res("nkpe")])
                    cfg.chk(6 + (10 if is_s else 0))
                    pa = pbank()
                    for k in range(16):
                        c.mm(pa.ap(32, n), wsm[:, k, 128:160], hT[:, k, 0:n], k == 0, k == 15, R=[wsm, hT], W=pa.rs)
                    s32 = st32[nst[1] % 3]; nst[1] += 1
                    c.copy("dve", s32[0:32, 0:n], pa.ap(32, n), R=pa.rs, W=[s32])
                    c.dma("pool", "st1", abT_s[:, t0:t0 + n], s32[0:32, 0:n], R=[s32], W=[c.res("abT")])
                    cfg.chk(8 + (10 if is_s else 0))
            cfg.chk(7)
            ctx32 = c.sb("ctx32", [128, 576], F32, ph)
            for j in range(LC // 128):
                c.dma("sp", "ld0", ctx32[:, 0:512], cckv[i, j * 128:(j + 1) * 128, :], W=[ctx32])
                c.dma("sp", "ld0", ctx32[:, 512:576], ckpe[i, j * 128:(j + 1) * 128, :], W=[ctx32])
                pb = pbank()
                for k in range(4):
                    c.op("pe", lambda e: e.transpose(out=pb.ap(128, 128, k * 128), in_=ctx32[:, k * 128:(k + 1) * 128], identity=ident[:]),
                         R=[ctx32, ident], W=pb.rs)
                for k in range(4):
                    c.copy("dve", ckn[:, k, j * 128:(j + 1) * 128], pb.ap(128, 128, k * 128), R=pb.rs, W=[ckn])
                pb2 = pslot()
                c.op("pe", lambda e: e.transpose(out=pb2.ap(64, 128), in_=ctx32[:, 512:576], identity=ident[:]), R=[ctx32, ident], W=pb2.rs)
                c.copy("act", stkr[:, j * 128:(j + 1) * 128], pb2.ap(64, 128), R=pb2.rs, W=[stkr])
            c.dma("pool", "st1", kr_x[:, 0:LC], stkr[:, 0:LC], R=[stkr], W=[c.res("kr")])
            kv_up(ckn, LC, lambda h: kn_x[h, :, 0:LC], lambda j: v_x[j * 128:(j + 1) * 128, :])
        c.barrier()

    def attn(layer, ph):
        if True:
            LKMAX = LC + LS
            kn_t = [c.sb(f"kn{b}", [128, LKMAX], BF16, ph) for b in range(2)]
            v_t = [c.sb(f"vv{b}", [128, LKMAX // 128, 128], BF16, ph) for b in range(2)]
            kr_t = c.sb("krt", [64, LKMAX], BF16, ph)
            qn_t = [c.sb(f"qnt{b}", [128, 512], BF16, ph) for b in range(2)]
            qr_t = [c.sb(f"qrt{b}", [64, 512], BF16, ph) for b in range(2)]
            pT = [c.sb(f"pT{b}", [128, 512], BF16, ph) for b in range(4)]
            szt = [c.sb(f"szt{b}", [128, 512], BF16, ph) for b in range(2)]
            yst = [c.sb(f"yst{b}", [128, 512], BF16, ph) for b in range(2)]
            rden = c.sb("rden", [128, 512], F32, ph)
            tmpo = c.sb("tmpo", [128, 512], F32, ph)
            scale = 192.0 ** -0.5
            nh = 0; nq = 0; npt_box = [0]
            for sq_ in seqs:
                is_s = sq_["kind"] == "s"
                L = sq_["L"]; t0 = sq_["t0"]
                Lk = LC + L if is_s else L
                nkc = Lk // 128
                QB = min(512, L)
                c.dma("sp", "ld0", kr_t[:, 0:Lk], (kr_x[:, 0:Lk] if is_s else kr_p[:, t0:t0 + Lk]), R=[c.res("kr")], W=[kr_t])
                for h in range(NH):
                    knt = kn_t[nh % 2]; vt = v_t[nh % 2]; nh += 1
                    c.dma("sp", "ldk", knt[:, 0:Lk], (kn_x[h, :, 0:Lk] if is_s else kn_p[h, :, t0:t0 + Lk]), R=[c.res("kn")], W=[knt])
                    vsrc = (v_x[0:Lk, h * 128:(h + 1) * 128] if is_s else v_p[t0:t0 + Lk, h * 128:(h + 1) * 128])
                    c.dma("sp", "ldk", vt[:, 0:nkc, :], vsrc.rearrange("(n p) e -> p n e", p=128), R=[c.res("v")], W=[vt])
                    for q0 in range(0, L, QB):
                        qnt = qn_t[nq % 2]; qrt = qr_t[nq % 2]; szT = szt[nq % 2]; yT = yst[nq % 2]
                        a, b = t0 + q0, t0 + q0 + QB
                        c.dma("sp", "ldq", qnt[:, 0:QB], qn_s[h, :, a:b], R=[c.res("qn")], W=[qnt])
                        c.dma("sp", "ldq", qrt[:, 0:QB], qr_s[h, :, a:b], R=[c.res("qr")], W=[qrt])
                        c.dma("sp", "ldq", szT[:, 0:QB], sz_s[h * 128:(h + 1) * 128, a:b], R=[c.res("sz")], W=[szT])
                        po = pbank(4 + nq % 2); pd = pbank(6 + nq % 2); nq += 1
                        def scores(kc):
                            nonlocal_npt = npt_box[0]; npt_box[0] += 1
                            psb = pbank(nonlocal_npt % 4); p_t = pT[nonlocal_npt % 4]
                            c.mm(psb.ap(128, QB), knt[:, kc * 128:(kc + 1) * 128], qnt[:, 0:QB], True, False, R=[knt, qnt], W=psb.rs)
                            c.mm(psb.ap(128, QB), kr_t[:, kc * 128:(kc + 1) * 128], qrt[:, 0:QB], False, True, R=[kr_t, qrt], W=psb.rs)
                            c.act(p_t[:, 0:QB], psb.ap(128, QB), AF.Exp, R=psb.rs, W=[p_t], scale=scale)
                            return p_t
                        pend = [scores(0)]
                        if nkc > 1:
                            pend.append(scores(1))
                        for kc in range(nkc):
                            p_t = pend.pop(0)
                            if kc + 2 < nkc:
                                pend.append(scores(kc + 2))
                            c.mm(po.ap(128, QB), vt[:, kc, :], p_t[:, 0:QB], kc == 0, kc == nkc - 1, R=[vt, p_t], W=po.rs)
                            c.mm(pd.ap(128, QB), onesb[:], p_t[:, 0:QB], kc == 0, kc == nkc - 1, R=[onesb, p_t], W=pd.rs)
                        c.op("dve", lambda e: e.reciprocal(out=rden[:, 0:QB], in_=pd.ap(128, QB)), R=pd.rs, W=[rden])
                        c.tt("dve", tmpo[:, 0:QB], po.ap(128, QB), rden[:, 0:QB], ALU.mult, R=po.rs + [rden], W=[tmpo])
                        c.tt("pool", yT[:, 0:QB], tmpo[:, 0:QB], szT[:, 0:QB], ALU.mult, R=[tmpo, szT], W=[yT])
                        c.dma("pool", "st1", yT_s[h * 128:(h + 1) * 128, a:b], yT[:, 0:QB], R=[yT], W=[c.res("yT")])
                        yield

    def even_E3(layer, with_pre=False):
        with ExitStack() as ph:
            gens = [attn(layer, ph)] + ([gdn_pre(layer, ph)] if with_pre else [])
            while gens:
                for g_ in list(gens):
                    try:
                        next(g_)
                    except StopIteration:
                        gens.remove(g_)
        c.barrier()

    def out_proj(layer, wsrc):
        first = layer == 0
        last = layer == cfg.depth - 1
        with ExitStack() as ph:
            wo = c.sb("wo", [128, 16, D], BF16, ph)
            wov = wsrc.rearrange("(k p) e -> p k e", p=128)
            for k4 in range(8):
                c.dma("sp", "ld0", wo[:, k4 * 2:k4 * 2 + 2, :], wov[:, k4 * 2:k4 * 2 + 2, :], R=[r_w], W=[wo])
            gts = []
            for ci in range(2):
                g = c.sb(f"gt{ci}", [128, D], F32, ph)
                c.dma("sp", "ld0", g[:], modv[layer, ci, 2 * D:3 * D].partition_broadcast(128), R=[r_modv], W=[g])
                gts.append(g)
            fn = None
            if last:
                fn = c.sb("fnw", [128, D], F32, ph)
                c.dma("sp", "ld0", fn[:], final_norm.partition_broadcast(128), W=[fn])
            yts = [c.sb(f"yt{b}", [128, 16, 512], BF16, ph) for b in range(2)]
            xts = [c.sb(f"xo{b}", [128, D], F32, ph) for b in range(2)]
            tms = [c.sb(f"tm{b}", [128, 512], F32, ph) for b in range(2)]
            junk = c.sb("junkb", [128, D], BF16, ph)
            ss = c.sb("ss4", [128, 4], F32, ph)
            nb = 0; nt = 0; ntm = 0
            for sq_ in seqs:
                L = sq_["L"]; g = gts[sq_["ci"]]
                for b0 in range(0, L, 512):
                    n = min(512, L - b0)
                    t0 = sq_["t0"] + b0
                    yt = yts[nb % 2]; nb += 1
                    for k4 in range(4):
                        c.dma("sp", "ldk", yt[:, k4 * 4:k4 * 4 + 4, 0:n], yT_s.rearrange("(k p) l -> p k l", p=128)[:, k4 * 4:k4 * 4 + 4, t0:t0 + n],
                              R=[c.res("yT")], W=[yt])
                    for j in range(n // 128):
                        xt = xts[nt % 2]; nt += 1
                        tj = t0 + j * 128
                        xres = c.res(("x", tj // 128))
                        c.dma("sp", "ldx", xt[:], xrows(first, tj, 128), R=[xres], W=[xt])
                        for nn in range(4):
                            pb = pbank()
                            for k in range(16):
                                c.mm(pb.ap(128, 512), yt[:, k, j * 128:(j + 1) * 128], wo[:, k, nn * 512:(nn + 1) * 512], k == 0, k == 15, R=[yt, wo], W=pb.rs)
                            tm = tms[ntm % 2]; ntm += 1
                            c.tt("dve", tm[:], pb.ap(128, 512), g[:, nn * 512:(nn + 1) * 512], ALU.mult, R=pb.rs + [g], W=[tm])
                            c.tt("pool", xt[:, nn * 512:(nn + 1) * 512], xt[:, nn * 512:(nn + 1) * 512], tm[:], ALU.add, R=[tm, xt], W=[xt])
                        if last:
                            c.act(junk[:], xt[:], AF.Square, R=[xt], W=[junk, ss], accum=ss[:, 0:1])
                            c.act(ss[:, 1:2], ss[:, 0:1], AF.Ln, R=[ss, epsT], W=[ss], bias=epsT[:, 0:1], scale=1.0 / D)
                            c.act(ss[:, 2:3], ss[:, 1:2], AF.Exp, R=[ss], W=[ss], scale=-0.5)
                            c.stt("dve", xt[:], xt[:], ss[:, 2:3], fn[:], ALU.mult, ALU.mult, R=[xt, ss, fn], W=[xt])
                            dst = yp[tj:tj + 128, :] if tj < NPT else ys[tj - NPT:tj - NPT + 128, :]
                            c.dma("pool", "st2", dst, xt[:], R=[xt], W=[xres])
                        else:
                            dst = xw_p[tj:tj + 128, :] if tj < NPT else xw_s[tj - NPT:tj - NPT + 128, :]
                            c.dma("pool", "st2", dst, xt[:], R=[xt], W=[xres])
        c.barrier()

    def gdn_pre(layer, ph):
        i = layer // 2
        if True:
            cw = c.sb("cw", [128, 3, 24], F32, ph)
            for t in range(3):
                c.dma("sp", "ld0", cw[:, t, :], conv_e[i, t].rearrange("(k p) -> p k", p=128), W=[cw], allow_slow_non_contiguous=True)
            LM = max(s_["L"] for s_ in seqs)
            xr = [c.sb(f"xr{b}", [128, LM + 2], F32, ph) for b in range(2)]
            accs = [c.sb(f"acc{b}", [128, LM], F32, ph) for b in range(2)]
            ysls = [c.sb(f"ysl{b}", [128, LM], F32, ph) for b in range(2)]
            sqb = c.sb("sqb", [128, LM], BF16, ph)
            rinv = c.sb("rinv", [128, LM], F32, ph)
            tln = c.sb("tln", [128, 512], F32, ph)
            outb = [c.sb(f"outb{b}", [128, LM], BF16, ph) for b in range(2)]
            bbc = c.sb("bbc", [128, LM], F32, ph)
            nx = 0; no = 0
            for sq_ in seqs:
                L = sq_["L"]; t0 = sq_["t0"]
                for h in range(NH):
                    kn_keep = None
                    for which in (1, 0, 2):
                        fcx = which * 8 + h
                        x = xr[nx % 2]; acc = accs[nx % 2]; ysl = ysls[nx % 2]; nx += 1
                        c.op("pool", lambda e: e.memset(x[:, 0:1], 0.0), W=[x])
                        c.op("pool", lambda e: e.memset(x[:, L + 1:L + 2], 0.0), W=[x])
                        c.dma("sp", "ldk", x[:, 1:L + 1], qkv_s[fcx * 128:(fcx + 1) * 128, t0:t0 + L], R=[c.res("qkv")], W=[x])
                        c.act(acc[:, 0:L], x[:, 1:L + 1], AF.Identity, R=[x, cw], W=[acc], scale=cw[:, 1, fcx:fcx + 1])
                        c.stt("dve", acc[:, 0:L], x[:, 0:L], cw[:, 0, fcx:fcx + 1], acc[:, 0:L], ALU.mult, ALU.add, R=[x, cw, acc], W=[acc])
                        c.stt("dve", acc[:, 0:L], x[:, 2:L + 2], cw[:, 2, fcx:fcx + 1], acc[:, 0:L], ALU.mult, ALU.add, R=[x, cw, acc], W=[acc])
                        c.act(ysl[:, 0:L], acc[:, 0:L], AF.Silu, R=[acc], W=[ysl])
                        ob = outb[no % 2]; no += 1
                        if which == 2:
                            c.copy("act", ob[:, 0:L], ysl[:, 0:L], R=[ysl], W=[ob])
                            c.dma("pool", "st1", gv_s[h, :, t0:t0 + L], ob[:, 0:L], R=[ob], W=[c.res("gv")])
                            yield
                            continue
                        c.act(sqb[:, 0:L], ysl[:, 0:L], AF.Square, R=[ysl], W=[sqb])
                        for b0 in range(0, L, 512):
                            n = min(512, L - b0)
                            pb = pbank()
                            c.mm(pb.ap(128, n), onesb[:], sqb[:, b0:b0 + n], True, True, R=[onesb, sqb], W=pb.rs)
                            c.act(tln[:, 0:n], pb.ap(128, n), AF.Ln, R=pb.rs + [epsT], W=[tln], bias=epsT[:, 0:1], scale=1.0)
                            c.act(rinv[:, b0:b0 + n], tln[:, 0:n], AF.Exp, R=[tln], W=[rinv], scale=-0.5)
                        if which == 0:
                            c.stt("dve", ob[:, 0:L], ysl[:, 0:L], 128.0 ** -0.5, rinv[:, 0:L], ALU.mult, ALU.mult, R=[ysl, rinv], W=[ob])
                            c.dma("pool", "st1", gq_s[h, :, t0:t0 + L], ob[:, 0:L], R=[ob], W=[c.res("gq")])
                            yield
                        else:
                            c.tt("dve", ysl[:, 0:L], ysl[:, 0:L], rinv[:, 0:L], ALU.mult, R=[ysl, rinv], W=[ysl])
                            c.copy("act", ob[:, 0:L], ysl[:, 0:L], R=[ysl], W=[ob])
                            c.dma("pool", "st1", gk_s[h, :, t0:t0 + L], ob[:, 0:L], R=[ob], W=[c.res("gk")])
                            for d in range(2):
                                c.dma("sp", "ldk", bbc[:, 0:L], abT_s[16 + d * 8 + h, t0:t0 + L].partition_broadcast(128), R=[c.res("abT")], W=[bbc])
                                c.act(bbc[:, 0:L], bbc[:, 0:L], AF.Sigmoid, R=[bbc], W=[bbc])
                                ob2 = outb[no % 2]; no += 1
                                c.tt("dve", ob2[:, 0:L], ysl[:, 0:L], bbc[:, 0:L], ALU.mult, R=[ysl, bbc], W=[ob2])
                                c.dma("pool", "st1", gkb_s[d, h, :, t0:t0 + L], ob2[:, 0:L], R=[ob2], W=[c.res("gkb")])
                            yield

    def even_E2(layer):
        i = layer // 2
        with ExitStack() as ph0:
            for _ in gdn_pre(layer, ph0):
                pass
        c.barrier()
        with ExitStack() as ph:
            msk = c.sb("msk", [128, 9, 128], F32, ph)
            cum = c.sb("cum", [128, 2, 128], F32, ph)
            for m_ in range(9):
                c.dma("sp", "ld0", msk[:, m_, :], k_masks[m_], W=[msk])
            for m_ in range(2):
                c.dma("sp", "ld0", cum[:, m_, :], k_cum[m_], W=[cum])
            dtb = c.sb("dtb", [128, 16], F32, ph)
            nega = c.sb("nega", [128, 16], F32, ph)
            c.dma("sp", "ld0", dtb[:], dt_bias_e[i].partition_broadcast(128), W=[dtb])
            c.dma("sp", "ld0", nega[:], a_log_e[i].partition_broadcast(128), W=[nega])
            c.act(nega[:], nega[:], AF.Exp, R=[nega], W=[nega])
            c.ts("dve", nega[:], nega[:], -1.0, ALU.mult, R=[nega], W=[nega])
            onw = load_cols(ph, "onw", o_norm_e[i], 1)
            NCM = max(max(s_["L"] for s_ in seqs) // 128, 8)
            abT_sb = c.sb("abT_sb", [32, NCM * 128], F32, ph)
            ab_tm = c.sb("ab_tm", [128, NCM, 32], F32, ph)
            T = lambda nm: c.sb(nm, [128, NCM, 16], F32, ph)
            xg, ta, tb_, gg, beta, Gc, expG, kdsc, dec, bexpG = [T(n_) for n_ in ("xg", "ta", "tb_", "gg", "beta", "Gc", "expG", "kdsc", "dec", "bexpG")]
            fl = lambda t_: t_[:].rearrange("p a b -> p (a b)")
            G2 = lambda nm, dt: [c.sb(f"{nm}{g}", [128, 4, 128], dt, ph) for g in range(2)]
            FL = lambda t_: t_[:].rearrange("p a b -> p (a b)")
            ld = [[c.sb(f"ld{nm}{b}", [128, NH, 128], BF16, ph) for b in range(2)] for nm in ("k", "q", "v", "kb")]
            NDT = F32 if cfg.neu32 else BF16
            gam1, gamT, gamTs, egrow, tmp32 = [G2(n_, F32) for n_ in ("gam1", "gamT", "gamTs", "egrow", "tmp32")]
            diag = tmp32
            IT8, QD8 = [G2(n_, BF16) for n_ in ("IT8", "QD8")]
            A8, AT8 = [G2(n_, NDT) for n_ in ("A8", "AT8")]
            Ad8, ATd8, AL8, Td8, Z8 = [G2(n_, NDT) for n_ in ("Ad8", "ATd8", "AL8", "Td8", "Z8")]
            PTf = G2("PTf", BF16)
            PTfb = PTf
            identn = ident if cfg.neu32 else identb
            Xa, Xb, XTa, XTb, PTa, PTb = [G2(n_, NDT) for n_ in ("Xa", "Xb", "XTa", "XTb", "PTa", "PTb")]
            KBG, KD, VB, WT, VN = [G2(n_, BF16) for n_ in ("KBG", "KD", "VB", "WT", "VN")]
            U32, S32, O32 = [G2(n_, F32) for n_ in ("U32", "S32", "O32")]
            ON32 = U32
            Sbf = G2("Sbf", BF16)
            YG = G2("YG", BF16)
            rep = {}
            for nm_, src_ in (("m01f", k_masks[4]), ("m01b", k_masks[5]), ("bd", k_masks[6]), ("ll", k_masks[7]), ("ur", k_masks[8]), ("id", k_ident)):
                t_ = c.sb("rep_" + nm_, [128, 4, 128], F32, ph)
                for hh in range(4):
                    c.dma("sp", "ld0", t_[:, hh, :], src_, W=[t_])
                rep[nm_] = t_
            OG = c.sb("OG", [128, NH, 128], F32, ph)
            szc = c.sb("szc", [128, NH, 128], BF16, ph)
            ssq = c.sb("ssq", [128, 3, NH], F32, ph)
            junk = c.sb("junk2", [128, 128], BF16, ph)
            nld = 0
            for sq_ in seqs:
                L = sq_["L"]; t0 = sq_["t0"]; is_s = sq_["kind"] == "s"
                nch = L // 128
                NP_ = max(nch, 8)
                c.dma("sp", "ld0", abT_sb[:, 0:L], abT_s[:, t0:t0 + L], R=[c.res("abT")], W=[abT_sb])
                for c4 in range(0, nch, 4):
                    ps4 = pbank()
                    nn_ = min(4, nch - c4)
                    for q_ in range(nn_):
                        c.op("pe", lambda e: e.transpose(out=ps4.ap(128, 32, q_ * 32), in_=abT_sb[:, (c4 + q_) * 128:(c4 + q_ + 1) * 128], identity=ident[0:32, 0:32]),
                             R=[abT_sb, ident], W=ps4.rs)
                    c.copy("dve", ab_tm[:, c4:c4 + nn_, :].rearrange("p a b -> p (a b)"), ps4.ap(128, nn_ * 32), R=ps4.rs, W=[ab_tm])
                cfg.chk(22)
                if NP_ > nch:
                    c.op("pool", lambda e: e.memset(fl(gg), 0.0), W=[gg])
                for j_ in range(16):
                    c.ts("dve", xg[:, 0:nch, j_], ab_tm[:, 0:nch, j_], dtb[:, j_:j_ + 1], ALU.add, R=[ab_tm, dtb], W=[xg])
                xf, taf, tbf = xg[:, 0:nch, :], ta[:, 0:nch, :], tb_[:, 0:nch, :]
                c.ts("dve", taf, xf, -1.0, ALU.mult, R=[xg], W=[ta])
                c.tt("dve", taf, taf, xf, ALU.max, R=[ta, xg], W=[ta])
                c.act(taf, taf, AF.Exp, R=[ta], W=[ta], scale=-1.0)
                c.act(taf, taf, AF.Ln, R=[ta, oneT], W=[ta], bias=oneT[:, 0:1], scale=1.0)
                c.ts("dve", tbf, xf, 0.0, ALU.max, R=[xg], W=[tb_])
                c.tt("dve", taf, taf, tbf, ALU.add, R=[ta, tb_], W=[ta])
                for j_ in range(16):
                    c.ts("dve", gg[:, 0:nch, j_], ta[:, 0:nch, j_], nega[:, j_:j_ + 1], ALU.mult, R=[ta, nega], W=[gg])
                c.act(beta[:, 0:nch, :], ab_tm[:, 0:nch, 16:32], AF.Exp, R=[ab_tm], W=[beta], scale=-1.0)
                c.ts("dve", beta[:, 0:nch, :], beta[:, 0:nch, :], 1.0, ALU.add, R=[beta], W=[beta])
                c.op("dve", lambda e: e.reciprocal(out=beta[:, 0:nch, :], in_=beta[:, 0:nch, :]), R=[beta], W=[beta])
                pF = pbank(); pB = pbank(); pT_ = pbank()
                W_ = NP_ * 16
                c.mm(pF.ap(128, W_), cum[:, 0, :], fl(gg)[:, 0:W_], True, True, R=[cum, gg], W=pF.rs)
                c.mm(pB.ap(128, W_), cum[:, 1, :], fl(gg)[:, 0:W_], True, True, R=[cum, gg], W=pB.rs)
                c.mm(pT_.ap(128, W_), ones32[:], fl(gg)[:, 0:W_], True, True, R=[ones32, gg], W=pT_.rs)
                v3 = lambda ps_: ps_.ap(128, W_).rearrange("p (a b) -> p a b", b=16)
                c.copy("dve", Gc[:, 0:NP_, 0:8], v3(pF)[:, :, 0:8], R=pF.rs, W=[Gc])
                c.copy("dve", Gc[:, 0:NP_, 8:16], v3(pB)[:, :, 8:16], R=pB.rs, W=[Gc])
                c.act(fl(expG)[:, 0:W_], fl(Gc)[:, 0:W_], AF.Exp, R=[Gc], W=[expG])
                c.tt("dve", fl(kdsc)[:, 0:W_], pT_.ap(128, W_), fl(Gc)[:, 0:W_], ALU.subtract, R=pT_.rs + [Gc], W=[kdsc])
                c.act(fl(kdsc)[:, 0:W_], fl(kdsc)[:, 0:W_], AF.Exp, R=[kdsc], W=[kdsc])
                c.act(fl(dec)[:, 0:W_], pT_.ap(128, W_), AF.Exp, R=pT_.rs, W=[dec])
                c.tt("dve", fl(bexpG)[:, 0:W_], fl(beta)[:, 0:W_], fl(expG)[:, 0:W_], ALU.mult, R=[beta, expG], W=[bexpG])
                if sq_ is seqs[0]:
                    for nm_, t_ in (("Gc", Gc), ("expG", expG), ("kdsc", kdsc), ("dec", dec), ("bexpG", bexpG), ("beta", beta), ("gg", gg)):
                        dbg(nm_, t_[:], [128, NCM, 16], R=[t_])
                    dbg("ab_tm", ab_tm[:], [128, NCM, 32], R=[ab_tm])
                cfg.chk(23)
                for d in range(2):
                    mA, mB, m01 = (0, 1, 4) if d == 0 else (2, 3, 5)
                    for gi in range(2):
                        for hh in range(4):
                            h = gi * 4 + hh
                            if is_s:
                                c.dma("sp", "ld0", S32[gi][:, hh, :], (sfw if d == 0 else sbw)[i, h], W=[S32[gi]])
                        if not is_s:
                            c.op("pool", lambda e: e.memset(FL(S32[gi]), 0.0), W=[S32[gi]])
                        c.copy("act", FL(Sbf[gi]), FL(S32[gi]), R=[S32[gi]], W=[Sbf[gi]])
                    order = range(nch) if d == 0 else range(nch - 1, -1, -1)
                    for ci in order:
                        a_, b_ = t0 + ci * 128, t0 + (ci + 1) * 128
                        lk, lq, lv, lkb = [ld[x_][nld % 2] for x_ in range(4)]; nld += 1
                        c.dma("sp", "ldq", lk[:], gk_s.rearrange("h p l -> p h l")[:, :, a_:b_], R=[c.res("gk")], W=[lk])
                        c.dma("sp", "ldq", lq[:], gq_s.rearrange("h p l -> p h l")[:, :, a_:b_], R=[c.res("gq")], W=[lq])
                        c.dma("sp", "ldq", lv[:], gv_s.rearrange("h p l -> p h l")[:, :, a_:b_], R=[c.res("gv")], W=[lv])
                        c.dma("sp", "ldq", lkb[:], gkb_s[d].rearrange("h p l -> p h l")[:, :, a_:b_], R=[c.res("gkb")], W=[lkb])
                        if d == 1:
                            c.dma("sp", "ldq", OG[:].rearrange("p h e -> p (h e)"), og_s[a_:b_, :], R=[c.res(("og", a_))], W=[OG])
                            c.dma("sp", "ldq", szc[:], sz_s[1024:2048, a_:b_].rearrange("(h p) l -> p h l", p=128), R=[c.res("sz")], W=[szc])
                        col = lambda t_, h: t_[:, ci, d * 8 + h:d * 8 + h + 1]
                        m01r = rep["m01f"] if d == 0 else rep["m01b"]
                        offr = rep["ll"] if d == 0 else rep["ur"]
                        GI = (0, 1)
                        hsl = lambda gi: slice(gi * 4, gi * 4 + 4)
                        def mm4(pb, lhs, rhs, R):
                            for hh in range(4):
                                c.mm(pb.ap(128, 128, hh * 128), lhs(hh), rhs(hh), True, True, R=R, W=pb.rs)
                        pg = {}
                        for gi in GI:
                            for hh in range(4):
                                c.act(diag[gi][:, hh, :], ident[:], AF.Identity, R=[ident, Gc], W=[diag[gi]], scale=col(Gc, gi * 4 + hh))
                            pg[gi] = pbank()
                            mm4(pg[gi], lambda hh: ones32[:], lambda hh: diag[gi][:, hh, :], [ones32, diag[gi]])
                        for gi in GI:
                            for hh in range(4):
                                h = gi * 4 + hh
                                c.stt("dve", gam1[gi][:, hh, :], pg[gi].ap(128, 128, hh * 128), col(Gc, h), msk[:, mA, :], ALU.subtract, ALU.subtract,
                                      R=pg[gi].rs + [Gc, msk], W=[gam1[gi]])
                                c.stt("dve", gamT[gi][:, hh, :], pg[gi].ap(128, 128, hh * 128), col(Gc, h), msk[:, mB, :], ALU.subtract, ALU.add,
                                      R=pg[gi].rs + [Gc, msk], W=[gamT[gi]])
                            c.act(FL(egrow[gi]), pg[gi].ap(128, 512), AF.Exp, R=pg[gi].rs, W=[egrow[gi]])
                            c.act(FL(gam1[gi]), FL(gam1[gi]), AF.Exp, R=[gam1[gi]], W=[gam1[gi]], scale=-1.0)
                            c.act(FL(gamT[gi]), FL(gamT[gi]), AF.Exp, R=[gamT[gi]], W=[gamT[gi]])
                            c.tt("pool", FL(gamTs[gi]), FL(gamT[gi]), FL(m01r), ALU.mult, R=[gamT[gi], m01r], W=[gamTs[gi]])
                        cfg.chk(24)
                        for gi in GI:
                            o4 = gi * 4
                            p1 = pbank(); mm4(p1, lambda hh: lkb[:, o4 + hh, :], lambda hh: lk[:, o4 + hh, :], [lkb, lk])
                            p2 = pbank(); mm4(p2, lambda hh: lk[:, o4 + hh, :], lambda hh: lkb[:, o4 + hh, :], [lkb, lk])
                            p3 = pbank(); mm4(p3, lambda hh: lk[:, o4 + hh, :], lambda hh: lq[:, o4 + hh, :], [lk, lq])
                            c.tt("dve", FL(A8[gi]), p1.ap(128, 512), FL(gam1[gi]), ALU.mult, R=p1.rs + [gam1[gi]], W=[A8[gi]])
                            c.tt("dve", FL(AT8[gi]), p2.ap(128, 512), FL(gamTs[gi]), ALU.mult, R=p2.rs + [gamTs[gi]], W=[AT8[gi]])
                            c.tt("dve", FL(IT8[gi]), p3.ap(128, 512), FL(gamT[gi]), ALU.mult, R=p3.rs + [gamT[gi]], W=[IT8[gi]])
                            c.tt("pool", FL(QD8[gi]), lq[:, hsl(gi), :].rearrange("p a b -> p (a b)"), FL(egrow[gi]), ALU.mult, R=[lq, egrow[gi]], W=[QD8[gi]])
                            c.tt("pool", FL(Ad8[gi]), FL(A8[gi]), FL(rep["bd"]), ALU.mult, R=[A8[gi], rep["bd"]], W=[Ad8[gi]])
                            c.tt("pool", FL(ATd8[gi]), FL(AT8[gi]), FL(rep["bd"]), ALU.mult, R=[AT8[gi], rep["bd"]], W=[ATd8[gi]])
                            c.tt("pool", FL(AL8[gi]), FL(A8[gi]), FL(offr), ALU.mult, R=[A8[gi], offr], W=[AL8[gi]])
                            c.tt("pool", FL(PTa[gi]), FL(rep["id"]), FL(ATd8[gi]), ALU.subtract, R=[rep["id"], ATd8[gi]], W=[PTa[gi]])
                        cfg.chk(25)
                        NL = 5
                        X, XT, PT = Ad8, ATd8, PTa
                        Xn, XTn, PTn = Xa, XTa, PTb
                        for lvl in range(1, NL + 1):
                            sA, sB, sC = {}, {}, {}
                            for gi in GI:
                                sA[gi] = pbank(); mm4(sA[gi], lambda hh: XT[gi][:, hh, :], lambda hh: X[gi][:, hh, :], [XT[gi], X[gi]])
                                if lvl < NL:
                                    sB[gi] = pbank(); mm4(sB[gi], lambda hh: X[gi][:, hh, :], lambda hh: XT[gi][:, hh, :], [XT[gi], X[gi]])
                            for gi in GI:
                                c.copy("act", FL(Xn[gi]), sA[gi].ap(128, 512), R=sA[gi].rs, W=[Xn[gi]])
                                if lvl < NL:
                                    c.copy("act" if lvl % 2 == 0 else "dve", FL(XTn[gi]), sB[gi].ap(128, 512), R=sB[gi].rs, W=[XTn[gi]])
                            for gi in GI:
                                sC[gi] = pbank(); mm4(sC[gi], lambda hh: Xn[gi][:, hh, :], lambda hh: PT[gi][:, hh, :], [Xn[gi], PT[gi]])
                            for gi in GI:
                                c.tt("dve", FL(PTn[gi]), sC[gi].ap(128, 512), FL(PT[gi]), ALU.add, R=sC[gi].rs + [PT[gi]], W=[PTn[gi]])
                            X, XT, PT = Xn, XTn, PTn
                            Xn = Xb if X is Xa else Xa
                            XTn = XTb if XT is XTa else XTa
                            PTn = PTa if PT is PTb else PTb
                        TdT = PT
                        sT, sZ, sR = {}, {}, {}
                        for gi in GI:
                            sT[gi] = pbank(); mm4(sT[gi], lambda hh: TdT[gi][:, hh, :], lambda hh: identn[:], [TdT[gi], identn])
                            sZ[gi] = pbank(); mm4(sZ[gi], lambda hh: AL8[gi][:, hh, :], lambda hh: TdT[gi][:, hh, :], [TdT[gi], AL8[gi]])
                        for gi in GI:
                            c.copy("act", FL(Td8[gi]), sT[gi].ap(128, 512), R=sT[gi].rs, W=[Td8[gi]])
                            c.copy("dve", FL(Z8[gi]), sZ[gi].ap(128, 512), R=sZ[gi].rs, W=[Z8[gi]])
                        for gi in GI:
                            sR[gi] = pbank(); mm4(sR[gi], lambda hh: Td8[gi][:, hh, :], lambda hh: Z8[gi][:, hh, :], [Td8[gi], Z8[gi]])
                        for gi in GI:
                            c.tt("dve", FL(PTf[gi]), FL(TdT[gi]), sR[gi].ap(128, 512), ALU.subtract, R=sR[gi].rs + [TdT[gi]], W=[PTf[gi]])
                        PT = PTfb
                        cfg.chk(26)
                        for gi in GI:
                            o4 = gi * 4
                            p1 = pbank(); mm4(p1, lambda hh: lk[:, o4 + hh, :], lambda hh: identb[:], [lk, identb])
                            p2 = pbank(); mm4(p2, lambda hh: lv[:, o4 + hh, :], lambda hh: identb[:], [lv, identb])
                            for hh in range(4):
                                c.act(KBG[gi][:, hh, :], p1.ap(128, 128, hh * 128), AF.Identity, R=p1.rs + [bexpG], W=[KBG[gi]], scale=col(bexpG, o4 + hh))
                            for hh in range(4):
                                c.ts("dve", VB[gi][:, hh, :], p2.ap(128, 128, hh * 128), col(beta, o4 + hh), ALU.mult, R=p2.rs + [beta], W=[VB[gi]])
                            for hh in range(4):
                                c.ts("dve", KD[gi][:, hh, :], p1.ap(128, 128, hh * 128), col(kdsc, o4 + hh), ALU.mult, R=p1.rs + [kdsc], W=[KD[gi]])
                        for gi in GI:
                            p1 = pbank(); mm4(p1, lambda hh: PT[gi][:, hh, :], lambda hh: VB[gi][:, hh, :], [PT[gi], VB[gi]])
                            p2 = pbank(); mm4(p2, lambda hh: KBG[gi][:, hh, :], lambda hh: PT[gi][:, hh, :], [PT[gi], KBG[gi]])
                            c.copy("act", FL(U32[gi]), p1.ap(128, 512), R=p1.rs, W=[U32[gi]])
                            c.copy("dve", FL(WT[gi]), p2.ap(128, 512), R=p2.rs, W=[WT[gi]])
                        cfg.chk(27)
                        for gi in GI:
                            p1 = pbank(); mm4(p1, lambda hh: WT[gi][:, hh, :], lambda hh: Sbf[gi][:, hh, :], [WT[gi], Sbf[gi]])
                            c.tt("dve", FL(VN[gi]), FL(U32[gi]), p1.ap(128, 512), ALU.subtract, R=p1.rs + [U32[gi]], W=[VN[gi]])
                        for gi in GI:
                            o4 = gi * 4
                            p2 = pbank()
                            for hh in range(4):
                                c.mm(p2.ap(128, 128, hh * 128), QD8[gi][:, hh, :], Sbf[gi][:, hh, :], True, False, R=[QD8[gi], Sbf[gi]], W=p2.rs)
                                c.mm(p2.ap(128, 128, hh * 128), IT8[gi][:, hh, :], VN[gi][:, hh, :], False, True, R=[IT8[gi], VN[gi]], W=p2.rs)
                            p3 = pbank(); mm4(p3, lambda hh: KD[gi][:, hh, :], lambda hh: VN[gi][:, hh, :], [KD[gi], VN[gi]])
                            for hh in range(4):
                                c.stt("dve", S32[gi][:, hh, :], S32[gi][:, hh, :], col(dec, o4 + hh), p3.ap(128, 128, hh * 128), ALU.mult, ALU.add,
                                      R=p3.rs + [S32[gi], dec], W=[S32[gi]])
                            c.copy("act", FL(Sbf[gi]), FL(S32[gi]), R=[S32[gi]], W=[Sbf[gi]])
                            if d == 0:
                                c.copy("act", FL(O32[gi]), p2.ap(128, 512), R=p2.rs, W=[O32[gi]])
                                c.dma("pool", "st1", og_s[a_:b_, o4 * 128:(o4 + 4) * 128], FL(O32[gi]), R=[O32[gi]], W=[c.res(("og", a_))])
                            else:
                                c.tt("dve", FL(O32[gi]), p2.ap(128, 512), OG[:, hsl(gi), :].rearrange("p a b -> p (a b)"), ALU.add, R=p2.rs + [OG], W=[O32[gi]])
                                for hh in range(4):
                                    c.act(junk[:], O32[gi][:, hh, :], AF.Square, R=[O32[gi]], W=[junk, ssq], accum=ssq[:, 0, o4 + hh:o4 + hh + 1])
                        if d == 1:
                            c.act(ssq[:, 1, :], ssq[:, 0, :], AF.Ln, R=[ssq, epsT], W=[ssq], bias=epsT[:, 0:1], scale=1.0 / 128)
                            c.act(ssq[:, 2, :], ssq[:, 1, :], AF.Exp, R=[ssq], W=[ssq], scale=-0.5)
                            for gi in GI:
                                o4 = gi * 4
                                for hh in range(4):
                                    c.act(ON32[gi][:, hh, :], O32[gi][:, hh, :], AF.Identity, R=[O32[gi], ssq], W=[ON32[gi]], scale=ssq[:, 2, o4 + hh:o4 + hh + 1])
                                p1 = pbank()
                                for hh in range(4):
                                    c.op("pe", lambda e: e.transpose(out=p1.ap(128, 128, hh * 128), in_=ON32[gi][:, hh, :], identity=ident[:]), R=[ON32[gi], ident], W=p1.rs)
                                c.stt("dve", FL(YG[gi]), p1.ap(128, 512), onw[:, 0:1], szc[:, hsl(gi), :].rearrange("p a b -> p (a b)"), ALU.mult, ALU.mult,
                                      R=p1.rs + [onw, szc], W=[YG[gi]])
                                c.dma("pool", "st1", yT_s[(8 + o4) * 128:(12 + o4) * 128, a_:b_].rearrange("(h p) l -> p h l", p=128), YG[gi][:], R=[YG[gi]], W=[c.res("yT")])
                    if not is_s:
                        for h in range(NH):
                            dst = (nsf if d == 0 else nsb)[sq_["idx"], i, h]
                            c.dma("pool", "st2", dst, S32[h // 4][:, h % 4, :], R=[S32[h // 4]], W=[c.res("ns")])
        c.barrier()

    def odd_O1(layer):
        i = layer // 2
        first = False
        with ExitStack() as ph:
            mods = mod_cols(ph, layer, ln_o[i])
            TB = 512
            hTs = [c.sb(f"hT{b}", [128, 16, TB], BF16, ph) for b in range(2)]
            xt = [c.sb(f"xt{b}", [128, D], F32, ph) for b in range(2)]
            ss = c.sb("ss", [128, 4], F32, ph)
            junk = c.sb("junk3", [128, 4, TB], BF16, ph)
            wg = [c.sb(f"wg{b}", [128, 16, 512], BF16, ph) for b in range(2)]
            st32 = [c.sb(f"st32_{b}", [128, TB], F32, ph) for b in range(3)]
            stz = [c.sb(f"stz{b}", [128, TB], BF16, ph) for b in range(2)]
            wv = wb_in_o[i].rearrange("(k p) e -> p k e", p=128)
            nwg = 0; nblk = 0; n32 = 0; nz = 0
            for sq_ in seqs:
                A, Bv = mods[sq_["ci"]]
                L = sq_["L"]
                for b0 in range(0, L, TB):
                    n = min(TB, L - b0)
                    t0 = sq_["t0"] + b0
                    hT = hTs[nblk % 2]; nblk += 1
                    norm_transpose((xt, junk, ss), first, t0, n, A, Bv, hT)
                    for gidx in range(8):
                        c0 = gidx * 512
                        wt = wg[nwg % 2]; nwg += 1
                        for k4 in range(4):
                            c.dma("sp", "ldw", wt[:, k4 * 4:k4 * 4 + 4, :], wv[:, k4 * 4:k4 * 4 + 4, c0:c0 + 512], R=[r_w], W=[wt])
                        for fc in range(4):
                            pb = pbank()
                            for k in range(16):
                                c.mm(pb.ap(128, n), wt[:, k, fc * 128:(fc + 1) * 128], hT[:, k, 0:n], k == 0, k == 15, R=[wt, hT], W=pb.rs)
                            s32 = st32[n32 % 3]; n32 += 1
                            if gidx < 4:
                                if fc % 2 == 0:
                                    c.copy("dve", s32[:, 0:n], pb.ap(128, n), R=pb.rs, W=[s32])
                                else:
                                    c.copy("act", s32[:, 0:n], pb.ap(128, n), R=pb.rs, W=[s32])
                                r0 = c0 + fc * 128
                                c.dma("pool", "st1", pin_s[r0:r0 + 128, t0:t0 + n], s32[:, 0:n], R=[s32], W=[c.res("pin")])
                            else:
                                sz = stz[nz % 2]; nz += 1
                                c.act(sz[:, 0:n], pb.ap(128, n), AF.Silu, R=pb.rs, W=[sz])
                                r0 = c0 - 2048 + fc * 128
                                c.dma("pool", "st1", sz_s[r0:r0 + 128, t0:t0 + n], sz[:, 0:n], R=[sz], W=[c.res("sz")])
        c.barrier()

    def odd_O2(layer):
        i = layer // 2
        with ExitStack() as ph:
            wp = c.sb("wp", [128, 4, 4, 512], BF16, ph)
            for g in range(4):
                c.dma("sp", "ld0", wp[:, g], wb_pool[i, g].rearrange("(k p) e -> p k e", p=128), R=[r_w], W=[wp])
            psc = load_cols(ph, "psc", pool_scale_o[i], 16)
            TB = 512
            HW = 8
            xr = [c.sb(f"pxr{b}", [128, TB + 2 * HW], F32, ph) for b in range(3)]
            sa = [c.sb(f"psa{b}", [128, TB + 2 * HW], F32, ph) for b in range(2)]
            pooled = [[c.sb(f"ppl{b}_{k}", [128, TB], BF16, ph) for k in range(4)] for b in range(2)]
            inv = [c.sb(f"pinv{b}", [128, TB], F32, ph) for b in range(2)]
            szt = [c.sb(f"pszt{b}", [128, TB], BF16, ph) for b in range(2)]
            yst = [c.sb(f"pyst{b}", [128, TB], BF16, ph) for b in range(2)]
            nx = 0; npl = 0; ni = 0; nz = 0
            for sq_ in seqs:
                L = sq_["L"]; ts0 = sq_["t0"]
                pinv_src = k_pinv_s if sq_["kind"] == "s" else k_pinv_p
                for b0 in range(0, L, TB):
                    n = min(TB, L - b0)
                    for g, w in enumerate((2, 4, 8, 16)):
                        iv = inv[ni % 2]; ni += 1
                        c.dma("sp", "ld0", iv[:, 0:n], pinv_src[g, b0:b0 + n].partition_broadcast(128), W=[iv])
                        pl = pooled[npl % 2]; npl += 1
                        for c4 in range(4):
                            x = xr[nx % 3]; nx += 1
                            lo = max(b0 - HW, 0); hi = min(b0 + n + HW, L)
                            if lo > b0 - HW:
                                c.op("pool", lambda e: e.memset(x[:, 0:HW], 0.0), W=[x])
                            if hi < b0 + n + HW:
                                c.op("pool", lambda e: e.memset(x[:, HW + n:HW + n + HW], 0.0), W=[x])
                            r0 = (g * 4 + c4) * 128
                            c.dma("sp", "ldk", x[:, lo - (b0 - HW):hi - (b0 - HW)], pin_s[r0:r0 + 128, ts0 + lo:ts0 + hi], R=[c.res("pin")], W=[x])
                            W_ = n + 2 * HW
                            cur = x; step = 1; length = W_
                            k_ = 0
                            while step < w:
                                dst = sa[k_ % 2]; k_ += 1
                                length -= step
                                c.tt("pool", dst[:, 0:length], cur[:, 0:length], cur[:, step:step + length], ALU.add, R=[cur], W=[dst])
                                cur = dst; step *= 2
                            o0 = HW - w // 2
                            tmpd = sa[k_ % 2]
                            c.tt("dve", tmpd[:, 0:n], cur[:, o0:o0 + n], iv[:, 0:n], ALU.mult, R=[cur, iv], W=[tmpd])
                            c.tt("dve", pl[c4][:, 0:n], tmpd[:, 0:n], x[:, HW:HW + n], ALU.subtract, R=[tmpd, x], W=[pl[c4]])
                        for e_ in range(4):
                            pb = pbank()
                            for c4 in range(4):
                                c.mm(pb.ap(128, n), wp[:, g, c4, e_ * 128:(e_ + 1) * 128], pl[c4][:, 0:n], c4 == 0, c4 == 3, R=[wp, pl[c4]], W=pb.rs)
                            fcx = g * 4 + e_
                            sz = szt[nz % 2]; ys_ = yst[nz % 2]; nz += 1
                            c.dma("sp", "ldq", sz[:, 0:n], sz_s[fcx * 128:(fcx + 1) * 128, ts0 + b0:ts0 + b0 + n], R=[c.res("sz")], W=[sz])
                            c.stt("dve", ys_[:, 0:n], pb.ap(128, n), psc[:, fcx:fcx + 1], sz[:, 0:n], ALU.mult, ALU.mult, R=pb.rs + [psc, sz], W=[ys_])
                            c.dma("pool", "st1", yT_s[fcx * 128:(fcx + 1) * 128, ts0 + b0:ts0 + b0 + n], ys_[:, 0:n], R=[ys_], W=[c.res("yT")])
        c.barrier()

    PH = {}
    PH["even_E1"] = even_E1
    PH["odd_O1"] = odd_O1
    PH["odd_O2"] = odd_O2
    PH["even_E2"] = even_E2
    PH["out_proj"] = out_proj
    PH["even_E3"] = even_E3
    def run_all():
        for layer in range(cfg.depth):
            i = layer // 2
            layer_now[0] = layer
            if layer % 2 == 0:
                even_E1(layer)
                even_E3(layer)
                even_E2(layer)
                out_proj(layer, wb_out_e[i])
            else:
                odd_O1(layer)
                odd_O2(layer)
                out_proj(layer, wb_out_o[i])
        c.barrier()
        nc.all_engine_barrier()
        c.es.close()

    PH["run_all"] = run_all
    return nc, c, PH, locals()


def make_consts(cfg):
    p = np.arange(128)[:, None]
    f = np.arange(128)[None, :]
    neg = lambda m: np.where(m, 0.0, NEG).astype(np.float32)
    bd = ((p // 64) == (f // 64)).astype(np.float32)
    ll = ((p >= 64) & (f < 64)).astype(np.float32)
    ur = ((p < 64) & (f >= 64)).astype(np.float32)
    masks = np.stack([neg(p > f), neg(f >= p), neg(f > p), neg(p >= f), (f > p).astype(np.float32), (p > f).astype(np.float32), bd, ll, ur])
    cum = np.stack([(p <= f).astype(np.float32), (p >= f).astype(np.float32)])
    LS = cfg.ls
    t = np.arange(LS)
    row = (t // 64).astype(np.float32)
    col = (t % 64).astype(np.float32)
    inv_freq = (10000.0 ** (-np.arange(0, 32, 2, dtype=np.float32) / 32)).astype(np.float32)
    cosT = np.zeros((64, LS), np.float32)
    sinS = np.zeros((64, LS), np.float32)
    for q in range(64):
        blk, half, j = q // 32, (q % 32) // 16, q % 16
        ang = (row if blk == 0 else col) * inv_freq[j]
        cosT[q] = np.cos(ang)
        sinS[q] = np.sin(ang) * (-1.0 if half == 0 else 1.0)

    def pinv(L):
        out = np.zeros((4, L), np.float32)
        for gi, w in enumerate((2, 4, 8, 16)):
            lo = np.clip(t[:L] - w // 2, 0, L) if L <= LS else None
            tt_ = np.arange(L)
            lo = np.clip(tt_ - w // 2, 0, L)
            hi = np.clip(tt_ + (w - w // 2), 0, L)
            out[gi] = 1.0 / (hi - lo).astype(np.float32)
        return out

    return dict(k_ident=np.eye(128, dtype=np.float32), k_masks=masks, k_cum=cum, k_rope=np.stack([cosT, sinS]),
                k_pinv_p=pinv(cfg.lp), k_pinv_s=pinv(cfg.ls))


WNAMES = ["ln_e", "mod_w_e", "mod_b_e", "w_in_e", "q_norm_e", "kv_norm_e", "w_uq_e", "w_ukv_e", "conv_e", "o_norm_e", "w_out_e",
          "ln_o", "mod_w_o", "mod_b_o", "w_in_o", "w_pool_o", "pool_scale_o", "w_out_o", "final_norm"]


def core_inputs(cfg, inp, j, consts):
    f = lambda a: np.ascontiguousarray(np.asarray(a, dtype=np.float32))
    m = dict(consts)
    m["xp"] = f(inp["x_prompt"][j * cfg.nps:(j + 1) * cfg.nps]).reshape(cfg.nps * cfg.lp, D)
    m["xs"] = f(inp["x_sample"][j])
    m["cckv"] = f(inp["cache_ckv"][j]); m["ckpe"] = f(inp["cache_kpe"][j])
    m["sfw"] = f(inp["state_fwd"][j]); m["sbw"] = f(inp["state_bwd"][j])
    m["cond"] = f(np.stack([np.asarray(inp["c_ctx"]), np.asarray(inp["c"][j])]))
    for k in WNAMES:
        m[k] = f(inp[k])
    m["a_log_e"] = f(inp["a_log_e"]).reshape(2, 16)
    m["dt_bias_e"] = f(inp["dt_bias_e"]).reshape(2, 16)
    return m


N_CORES = 8


def kernel(**inputs):
    cfg = Cfg(nps=4, lp=256, ls=4096, lc=256, depth=4)
    nc, c, PH, _ = build(cfg)
    PH["run_all"]()
    consts = make_consts(cfg)
    in_maps = [core_inputs(cfg, inputs, j, consts) for j in range(N_CORES)]
    res = run_bass_kernel_spmd(nc, in_maps, core_ids=list(range(N_CORES)))
    rs = res.results
    f = lambda k, shp: np.concatenate([np.asarray(r[k], dtype=np.float32).reshape(shp) for r in rs], axis=0)
    y_prompt = f("yp", (4, 256, D))
    y_sample = f("ys", (1, 4096, D))
    nckv = f("nckv", (4, 2, 256, 512))
    nkpe = f("nkpe", (4, 2, 256, 64))
    nsf = f("nsf", (4, 2, NH, 128, 128))
    nsb = f("nsb", (4, 2, NH, 128, 128))
    return (y_prompt, y_sample, nckv, nkpe, nsf, nsb)
```

```python
import numpy as np
import ml_dtypes
from contextlib import ExitStack
import concourse.bass as bass
import concourse.mybir as mybir
from concourse.bass_utils import run_bass_kernel_spmd

F32 = mybir.dt.float32
BF16 = mybir.dt.bfloat16
AF = mybir.ActivationFunctionType
ALU = mybir.AluOpType
AX = mybir.AxisListType

D = 2048
NH = 8
IN_EVEN = 6240
EPS = 1e-6
NEG = -1.0e9


class Res:
    __slots__ = ("w", "r", "excl")

    def __init__(self, excl=False):
        self.w = None
        self.r = {}
        self.excl = excl


class Tile:
    def __init__(self, t, res=None):
        self.t = t
        self.res = res if res is not None else Res()

    def __getitem__(self, k):
        return self.t[k]


class Ctx:
    def __init__(self, nc):
        self.nc = nc
        self.es = ExitStack()
        self.eng = {"pe": nc.tensor, "act": nc.scalar, "dve": nc.vector, "pool": nc.gpsimd, "sp": nc.sync}
        self.sems = {}
        self.count = {}
        self.seen = {e: {} for e in self.eng}
        for e in ("pe", "act", "dve", "pool"):
            self._sem(e)
        self.resd = {}
        self.n_ins = 0
        self.dead = False
        self.ring_n = {}

    def _sem(self, name):
        if name not in self.sems:
            self.sems[name] = self.es.enter_context(self.nc.semaphore("s_" + name))
            self.count[name] = 0
        return self.sems[name]

    def res(self, key):
        r = self.resd.get(key)
        if r is None:
            r = self.resd[key] = Res()
        return r

    def sb(self, name, shape, dt, stack=None):
        self.n_sb = getattr(self, "n_sb", 0) + 1
        name = f"{name}_u{self.n_sb}"
        t = (stack or self.es).enter_context(self.nc.sbuf_tensor(name, list(shape), dt))
        return Tile(t)

    def _rs(self, x):
        return x.res if isinstance(x, Tile) else x

    def _waits(self, e, R, W):
        need = {}
        for r in R:
            r = self._rs(r)
            if r.w is not None:
                s, v = r.w
                if need.get(s, 0) < v:
                    need[s] = v
            if r.excl:
                for s, v in r.r.items():
                    if s != e and need.get(s, 0) < v:
                        need[s] = v
        for w in W:
            w = self._rs(w)
            if w.w is not None:
                s, v = w.w
                if need.get(s, 0) < v:
                    need[s] = v
            for s, v in w.r.items():
                if need.get(s, 0) < v:
                    need[s] = v
        seen = self.seen[e]
        for s, v in need.items():
            if e == "pe" and s == "pe":
                continue
            if seen.get(s, 0) < v:
                self.eng[e].wait_ge(self.sems[s], v)
                seen[s] = v

    def _mark(self, ticket, R, W):
        s, v = ticket
        for r in R:
            r = self._rs(r)
            if r.r.get(s, 0) < v:
                r.r[s] = v
        for w in W:
            w = self._rs(w)
            w.w = ticket
            w.r = {}

    def op(self, e, emit, R=(), W=()):
        if self.dead:
            return None
        self._waits(e, R, W)
        ins = emit(self.eng[e])
        self.count[e] += 1
        ins.then_inc(self.sems[e], 1)
        self._mark((e, self.count[e]), R, W)
        self.n_ins += 1
        return ins

    RING = 8

    def dma(self, q, stream, out, in_, R=(), W=(), **kw):
        if self.dead:
            return None
        n = self.ring_n.get(stream, 0)
        self.ring_n[stream] = n + 1
        sname = f"{stream}_{n % self.RING}"
        self._sem(sname)
        prev = self.count[sname]
        if prev > 0 and self.seen[q].get(sname, 0) < prev:
            self.eng[q].wait_ge(self.sems[sname], prev)
            self.seen[q][sname] = prev
        self._waits(q, R, W)
        ins = self.eng[q].dma_start(out=out, in_=in_, **kw)
        self.count[sname] += 16
        ins.then_inc(self.sems[sname], 16)
        self._mark((sname, self.count[sname]), R, W)
        self.n_ins += 1

    def barrier(self):
        if self.dead:
            self.dead = False
        for e in self.eng:
            seen = self.seen[e]
            for s, v in self.count.items():
                if e == "pe" and s == "pe":
                    continue
                if v > 0 and seen.get(s, 0) < v:
                    self.eng[e].wait_ge(self.sems[s], v)
                    seen[s] = v

    def mm(self, out, lhsT, rhs, start, stop, R, W):
        return self.op("pe", lambda e: e.matmul(out, lhsT=lhsT, rhs=rhs, start=start, stop=stop), R, W)

    def act(self, out, in_, func, R, W, bias=None, scale=None, accum=None, eng="act"):
        kw = {}
        if bias is not None:
            kw["bias"] = bias
        if scale is not None:
            kw["scale"] = scale
        if accum is not None:
            kw["accum_out"] = accum
        return self.op(eng, lambda e: e.activation(out=out, in_=in_, func=func, **kw), R, W)

    def copy(self, eng, out, in_, R, W):
        if eng == "pool":
            eng = "dve"
        if eng == "act":
            return self.op("act", lambda e: e.copy(out=out, in_=in_), R, W)
        return self.op(eng, lambda e: e.tensor_copy(out=out, in_=in_), R, W)

    def tt(self, eng, out, in0, in1, op, R, W):
        if eng == "pool":
            eng = "dve"
        return self.op(eng, lambda e: e.tensor_tensor(out=out, in0=in0, in1=in1, op=op), R, W)

    def ts(self, eng, out, in0, s1, op0, R, W, s2=None, op1=None):
        if eng == "pool":
            eng = "dve"
        if op1 is None:
            return self.op(eng, lambda e: e.tensor_scalar(out=out, in0=in0, scalar1=s1, scalar2=None, op0=op0), R, W)
        return self.op(eng, lambda e: e.tensor_scalar(out=out, in0=in0, scalar1=s1, scalar2=s2, op0=op0, op1=op1), R, W)

    def stt(self, eng, out, in0, scalar, in1, op0, op1, R, W):
        eng = "dve"
        return self.op(eng, lambda e: e.scalar_tensor_tensor(out=out, in0=in0, scalar=scalar, in1=in1, op0=op0, op1=op1), R, W)


class StopBuild(Exception):
    pass


class Cfg:
    def __init__(self, nps=4, lp=256, ls=4096, lc=256, depth=4, debug=False):
        self.nps, self.lp, self.ls, self.lc, self.depth, self.debug = nps, lp, ls, lc, depth, debug
        self.n_even = (depth + 1) // 2
        self.n_odd = depth // 2
        self.stop = None
        self.neu32 = True
        self.gs = 2

    def chk(self, k):
        if self.stop == k:
            self.ctx.dead = True


def build(cfg):
    nc = bass.Bass("TRN2", target_bir_lowering=False)
    c = Ctx(nc)
    cfg.ctx = c
    NPS, LP, LS, LC = cfg.nps, cfg.lp, cfg.ls, cfg.lc
    NE, NO = cfg.n_even, cfg.n_odd
    NPT = NPS * LP

    def din(name, shape, dt=F32):
        return nc.dram_tensor(name, list(shape), dt, kind="ExternalInput").ap()

    def dout(name, shape, dt=F32):
        return nc.dram_tensor(name, list(shape), dt, kind="ExternalOutput").ap()

    def dscr(name, shape, dt=F32):
        kind = "ExternalOutput" if cfg.debug else "Internal"
        return nc.dram_tensor(name, list(shape), dt, kind=kind).ap()

    xp = din("xp", [NPT, D])
    xs = din("xs", [LS, D])
    cckv = din("cckv", [2, LC, 512])
    ckpe = din("ckpe", [2, LC, 64])
    sfw = din("sfw", [2, NH, 128, 128])
    sbw = din("sbw", [2, NH, 128, 128])
    cond = din("cond", [2, D])
    ln_e = din("ln_e", [2, D]); mod_w_e = din("mod_w_e", [2, D, 3 * D]); mod_b_e = din("mod_b_e", [2, 3 * D])
    w_in_e = din("w_in_e", [2, D, IN_EVEN]); q_norm_e = din("q_norm_e", [2, 512]); kv_norm_e = din("kv_norm_e", [2, 512])
    w_uq_e = din("w_uq_e", [2, 512, 1536]); w_ukv_e = din("w_ukv_e", [2, 512, 2048]); conv_e = din("conv_e", [2, 3, 3072])
    a_log_e = din("a_log_e", [2, 16]); dt_bias_e = din("dt_bias_e", [2, 16]); o_norm_e = din("o_norm_e", [2, 128])
    w_out_e = din("w_out_e", [2, D, D])
    ln_o = din("ln_o", [2, D]); mod_w_o = din("mod_w_o", [2, D, 3 * D]); mod_b_o = din("mod_b_o", [2, 3 * D])
    w_in_o = din("w_in_o", [2, D, 2 * D]); w_pool_o = din("w_pool_o", [2, 4, 512, 512]); pool_scale_o = din("pool_scale_o", [2, D])
    w_out_o = din("w_out_o", [2, D, D]); final_norm = din("final_norm", [D])
    k_ident = din("k_ident", [128, 128])
    k_masks = din("k_masks", [9, 128, 128])
    k_cum = din("k_cum", [2, 128, 128])
    k_rope = din("k_rope", [2, 64, LS])
    k_pinv_p = din("k_pinv_p", [4, LP]); k_pinv_s = din("k_pinv_s", [4, LS])

    yp = dout("yp", [NPT, D]); ys = dout("ys", [LS, D])
    nckv = dout("nckv", [NPS, 2, LP, 512]); nkpe = dout("nkpe", [NPS, 2, LP, 64])
    nsf = dout("nsf", [NPS, 2, NH, 128, 128]); nsb = dout("nsb", [NPS, 2, NH, 128, 128])

    xw_p = dscr("xw_p", [NPT, D]); xw_s = dscr("xw_s", [LS, D])
    modv = dscr("modv", [4, 2, 3 * D])
    wb_in_e = dscr("wb_in_e", [NE, D, IN_EVEN], BF16); wb_uq = dscr("wb_uq", [NE, 512, 1536], BF16)
    wb_ukv = dscr("wb_ukv", [NE, 512, 2048], BF16); wb_out_e = dscr("wb_out_e", [NE, D, D], BF16)
    wb_in_o = dscr("wb_in_o", [max(NO, 1), D, 2 * D], BF16); wb_pool = dscr("wb_pool", [max(NO, 1), 4, 512, 512], BF16)
    wb_out_o = dscr("wb_out_o", [max(NO, 1), D, D], BF16)
    LT = NPT + LS
    LKS = LC + LS
    qn_s = dscr("qn_s", [NH, 128, LT], BF16); qr_s = dscr("qr_s", [NH, 64, LT], BF16)
    kn_p = dscr("kn_p", [NH, 128, NPT], BF16); kr_p = dscr("kr_p", [64, NPT], BF16); v_p = dscr("v_p", [NPT, 1024], BF16)
    kn_x = dscr("kn_x", [NH, 128, LKS], BF16); kr_x = dscr("kr_x", [64, LKS], BF16); v_x = dscr("v_x", [LKS, 1024], BF16)
    qkv_s = dscr("qkv_s", [3072, LT]); ab_s = dscr("ab_s", [LT, 32]); abT_s = dscr("abT_s", [32, LT])
    sz_s = dscr("sz_s", [D, LT], BF16); yT_s = dscr("yT_s", [D, LT], BF16)
    gq_s = dscr("gq_s", [NH, 128, LT], BF16); gk_s = dscr("gk_s", [NH, 128, LT], BF16); gv_s = dscr("gv_s", [NH, 128, LT], BF16)
    gkb_s = dscr("gkb_s", [2, NH, 128, LT], BF16)
    og_s = dscr("og_s", [LT, 1024])
    pin_s = dscr("pin_s", [D, LT])

    seqs = [dict(t0=i * LP, L=LP, ci=0, kind="p", idx=i) for i in range(NPS)] + [dict(t0=NPT, L=LS, ci=1, kind="s", idx=0)]

    def xrows(src_first, t0, n):
        if t0 < NPT:
            return (xp if src_first else xw_p)[t0:t0 + n, :]
        return (xs if src_first else xw_s)[t0 - NPT:t0 - NPT + n, :]

    dbg_n = [0]
    layer_now = [0]

    def dbg(name, ap, shape, dt=F32, R=()):
        if not cfg.debug or layer_now[0] != 0:
            return
        t = nc.dram_tensor("dbg_" + name, list(shape), dt, kind="ExternalOutput").ap()
        c.dma("pool", "st2", t, ap, R=list(R))

    ident = c.sb("ident", [128, 128], F32)
    identb = c.sb("identb", [128, 128], BF16)
    ones32 = c.sb("ones32", [128, 128], F32)
    onesb = c.sb("onesb", [128, 128], BF16)
    epsT = c.sb("epsT", [128, 1], F32)
    oneT = c.sb("oneT", [128, 1], F32)
    c.dma("sp", "ld0", ident[:], k_ident, W=[ident])
    c.op("pool", lambda e: e.memset(ones32[:], 1.0), W=[ones32])
    c.op("pool", lambda e: e.memset(onesb[:], 1.0), W=[onesb])
    c.op("pool", lambda e: e.memset(epsT[:], EPS), W=[epsT])
    c.op("pool", lambda e: e.memset(oneT[:], 1.0), W=[oneT])
    c.copy("dve", identb[:], ident[:], R=[ident], W=[identb])

    banks = [c.es.enter_context(nc.psum_tensor(f"ps{i}", [128, 512], F32)) for i in range(8)]
    bank_res = [Res(excl=True) for _ in range(8)]

    class PS:
        def __init__(self, b, c0, n):
            self.b, self.c0, self.n = b, c0, n
            self.rs = [bank_res[b]]

        def ap(self, p=128, n=None, off=0):
            n = self.n if n is None else n
            return banks[self.b][0:p, self.c0 + off:self.c0 + off + n]

    st = {"slot": 0}

    def pbank(b=None):
        if b is None:
            s = (st["slot"] + 3) // 4 * 4 % 32
            st["slot"] = (s + 4) % 32
            b = s // 4
        return PS(b, 0, 512)

    def palign():
        st["slot"] = (st["slot"] + 3) // 4 * 4 % 32

    def pslots4():
        palign()
        return [pslot() for _ in range(4)]

    def pslot():
        s = st["slot"]
        st["slot"] = (s + 1) % 32
        return PS(s // 4, (s % 4) * 128, 128)

    r_w = c.res("wcast")

    def wcast(dst, src, rows_per=512):
        n = src.shape[0]
        for r0 in range(0, n, rows_per):
            r1 = min(n, r0 + rows_per)
            c.dma("pool", "wc", dst[r0:r1], src[r0:r1], W=[r_w])

    for i in range(NE):
        wcast(wb_in_e[i], w_in_e[i]); wcast(wb_uq[i], w_uq_e[i]); wcast(wb_ukv[i], w_ukv_e[i]); wcast(wb_out_e[i], w_out_e[i])
    for i in range(NO):
        wcast(wb_in_o[i], w_in_o[i]); wcast(wb_out_o[i], w_out_o[i])
        for g in range(4):
            wcast(wb_pool[i, g], w_pool_o[i, g])

    r_modv = c.res("modv")
    with ExitStack() as ph:
        condT = c.sb("condT", [128, 16, 2], F32, ph)
        scT = c.sb("scT", [128, 16, 2], F32, ph)
        sgT = c.sb("sgT", [128, 16, 2], F32, ph)
        for ci_ in range(2):
            c.dma("sp", "ld0", condT[:, :, ci_], cond[ci_].rearrange("(k p) -> p k", p=128), W=[condT], allow_slow_non_contiguous=True)
        c.act(scT[:], condT[:], AF.Exp, R=[condT], W=[scT], scale=-1.0)
        c.ts("dve", scT[:], scT[:], 1.0, ALU.add, R=[scT], W=[scT])
        c.op("dve", lambda e: e.reciprocal(out=scT[:], in_=scT[:]), R=[scT], W=[scT])
        c.tt("dve", sgT[:], condT[:], scT[:], ALU.mult, R=[condT, scT], W=[sgT])
        wts = [c.sb(f"mw{i}", [128, 4, 512], F32, ph) for i in range(3)]
        mb = c.sb("mb", [2, 3 * D], F32, ph)
        mo = c.sb("mo", [2, 3 * D], F32, ph)
        nld = 0
        for layer in range(cfg.depth):
            i = layer // 2
            mw = (mod_w_e if layer % 2 == 0 else mod_w_o)[i]
            mbv = (mod_b_e if layer % 2 == 0 else mod_b_o)[i]
            c.dma("sp", "ld0", mb[:], mbv.partition_broadcast(2), W=[mb], R=[])
            for n in range(12):
                pb = pbank()
                for kg in range(4):
                    wt = wts[nld % 3]; nld += 1
                    c.dma("sp", "ldw", wt[:], mw[kg * 512:(kg + 1) * 512, n * 512:(n + 1) * 512].rearrange("(k p) e -> p k e", p=128), W=[wt])
                    for kk in range(4):
                        k = kg * 4 + kk
                        c.mm(pb.ap(2), sgT[:, k, :], wt[:, kk, :], k == 0, k == 15, R=[sgT, wt], W=pb.rs)
                c.tt("dve", mo[:, n * 512:(n + 1) * 512], pb.ap(2), mb[:, n * 512:(n + 1) * 512], ALU.add, R=pb.rs + [mb], W=[mo])
            c.dma("sp", "st0", modv[layer], mo[:], R=[mo], W=[r_modv])
    c.barrier()

    def load_cols(ph, name, vec, nchunk, q="sp"):
        t = c.sb(name, [128, nchunk], F32, ph)
        c.dma(q, "ld0", t[:], vec.rearrange("(k p) -> p k", p=128), W=[t], R=[r_modv], allow_slow_non_contiguous=True)
        return t

    def norm_transpose(ph_tiles, src_first, t0, n, A, B, hT):
        xt, junk, ss = ph_tiles
        for j in range(n // 128):
            xn = xt[j % 2]
            c.dma("sp", "ldx", xn[:], xrows(src_first, t0 + j * 128, 128), R=[c.res(("x", (t0 + j * 128) // 128))], W=[xn])
            c.act(junk[:].rearrange("p a b -> p (a b)")[:, 0:D], xn[:], AF.Square, R=[xn], W=[junk, ss], accum=ss[:, 0:1])
            c.act(ss[:, 1:2], ss[:, 0:1], AF.Ln, R=[ss, epsT], W=[ss], bias=epsT[:, 0:1], scale=1.0 / D)
            c.act(ss[:, 2:3], ss[:, 1:2], AF.Exp, R=[ss], W=[ss], scale=-0.5)
            c.ts("dve", xn[:], xn[:], ss[:, 2:3], ALU.mult, R=[xn, ss], W=[xn])
            for kg in range(4):
                pb = pbank()
                for kk in range(4):
                    k = kg * 4 + kk
                    c.op("pe", lambda e: e.transpose(out=pb.ap(128, 128, kk * 128), in_=xn[:, k * 128:(k + 1) * 128], identity=ident[:]), R=[xn, ident], W=pb.rs)
                for kk in range(4):
                    k = kg * 4 + kk
                    eng = "dve" if kk % 2 == 0 else "pool"
                    if eng == "pool":
                        c.act(hT[:, k, j * 128:(j + 1) * 128], pb.ap(128, 128, kk * 128), AF.Identity, R=pb.rs + [A, B], W=[hT],
                              bias=B[:, k:k + 1], scale=A[:, k:k + 1])
                    else:
                        c.ts("dve", hT[:, k, j * 128:(j + 1) * 128], pb.ap(128, 128, kk * 128), A[:, k:k + 1], ALU.mult, R=pb.rs + [A, B], W=[hT],
                             s2=B[:, k:k + 1], op1=ALU.add)

    def mod_cols(ph, layer, lnw_vec):
        lnw = load_cols(ph, f"lnw{layer}", lnw_vec, 16)
        outs = []
        for ci in range(2):
            sh = load_cols(ph, f"sh{layer}_{ci}", modv[layer, ci, 0:D], 16)
            sc = load_cols(ph, f"sc{layer}_{ci}", modv[layer, ci, D:2 * D], 16)
            A = c.sb(f"A{layer}_{ci}", [128, 16], F32, ph)
            c.stt("dve", A[:], sc[:], 1.0, lnw[:], ALU.add, ALU.mult, R=[sc, lnw], W=[A])
            outs.append((A, sh))
        return outs

    def rstd_from_ps(ps_sum, n, inv_n, out_t, tmp_t):
        c.act(tmp_t[:, 0:n], ps_sum.ap(128, n), AF.Ln, R=ps_sum.rs + [epsT], W=[tmp_t], bias=epsT[:, 0:1], scale=inv_n)
        c.act(out_t[:, 0:n], tmp_t[:, 0:n], AF.Exp, R=[tmp_t], W=[out_t], scale=-0.5)

    def even_E1(layer):
        i = layer // 2
        first = layer == 0
        with ExitStack() as ph:
            mods = mod_cols(ph, layer, ln_e[i])
            qg = load_cols(ph, "qg", q_norm_e[i], 4)
            kg_ = load_cols(ph, "kvg", kv_norm_e[i], 4)
            wuq = c.sb("wuq", [128, 4, 1536], BF16, ph)
            wuqp = c.sb("wuqp", [128, 4, 8, 64], BF16, ph)
            wukk = c.sb("wukk", [128, 4, 8, 128], BF16, ph)
            wukv = c.sb("wukv", [128, 4, 8, 128], BF16, ph)
            c.dma("sp", "ld0", wuq[:], wb_uq[i].rearrange("(k p) e -> p k e", p=128), R=[r_w], W=[wuq])
            uqv = wb_uq[i].rearrange("(k p) (h e) -> p k h e", p=128, e=192)
            for blk in range(2):
                for half in range(2):
                    src = uqv[:, :, :, 128 + blk * 32 + (1 - half) * 16:128 + blk * 32 + (1 - half) * 16 + 16]
                    for k in range(4):
                        c.dma("sp", "ld0", wuqp[:, k, :, blk * 32 + half * 16:blk * 32 + half * 16 + 16], src[:, k], R=[r_w], W=[wuqp],
                              allow_slow_non_contiguous=True)
            ukvv = wb_ukv[i].rearrange("(k p) (h t e) -> p k h t e", p=128, t=2, e=128)
            for k in range(4):
                c.dma("sp", "ld0", wukk[:, k], ukvv[:, k, :, 0, :], R=[r_w], W=[wukk])
                c.dma("sp", "ld0", wukv[:, k], ukvv[:, k, :, 1, :], R=[r_w], W=[wukv])
            TB = 512
            hTs = [c.sb(f"hT{b}", [128, 16, TB], BF16, ph) for b in range(2)]
            xt = [c.sb(f"xt{b}", [128, D], F32, ph) for b in range(2)]
            ss = c.sb("ss", [128, 4], F32, ph)
            wg = [c.sb(f"wg{b}", [128, 16, 512], BF16, ph) for b in range(2)]
            wsm = c.sb("wsm", [128, 16, 128 + 32], BF16, ph)
            raw0 = c.sb("raw0", [128, 4, TB], F32, ph)
            raw = [raw0, raw0]
            sq = c.sb("sq", [128, 4, TB], BF16, ph)
            rstd = c.sb("rstd", [128, TB], F32, ph)
            tmpn = c.sb("tmpn", [128, TB], F32, ph)
            cqn = c.sb("cqn", [128, 4, TB], BF16, ph)
            ckn = c.sb("ckn", [128, 4, TB], BF16, ph)
            ckn32 = raw0
            stq = c.sb("stq", [128, 2, TB], BF16, ph)
            stqr = c.sb("stqr", [64, 2, TB], BF16, ph)
            stk = c.sb("stk", [128, 2, TB], BF16, ph)
            stv = [c.sb(f"stv{b}", [128, 1024], BF16, ph) for b in range(2)]
            stkr = c.sb("stkr", [64, TB], BF16, ph)
            kpe32 = c.sb("kpe32", [64, TB], F32, ph)
            st32 = [c.sb(f"st32_{b}", [128, TB], F32, ph) for b in range(3)]
            stz = [c.sb(f"stz{b}", [128, TB], BF16, ph) for b in range(2)]
            stab = c.sb("stab", [128, 32], F32, ph)
            otok = [c.sb(f"otok{b}", [128, 576], F32, ph) for b in range(2)]
            ropec = c.sb("ropec", [64, TB], F32, ph)
            ropes = c.sb("ropes", [64, TB], F32, ph)
            rt1 = c.sb("rt1", [64, TB], F32, ph)
            rt2 = c.sb("rt2", [64, TB], F32, ph)
            wv = wb_in_e[i].rearrange("(k p) e -> p k e", p=128)
            for k4 in range(4):
                c.dma("sp", "ld0", wsm[:, k4 * 4:k4 * 4 + 4, 0:64], wv[:, k4 * 4:k4 * 4 + 4, 1024:1088], R=[r_w], W=[wsm])
            for blk in range(2):
                for half in range(2):
                    c0 = 1024 + blk * 32 + (1 - half) * 16
                    for k4 in range(4):
                        c.dma("sp", "ld0", wsm[:, k4 * 4:k4 * 4 + 4, 64 + blk * 32 + half * 16:64 + blk * 32 + half * 16 + 16],
                              wv[:, k4 * 4:k4 * 4 + 4, c0:c0 + 16], R=[r_w], W=[wsm], allow_slow_non_contiguous=True)
            for k4 in range(4):
                c.dma("sp", "ld0", wsm[:, k4 * 4:k4 * 4 + 4, 128:160], wv[:, k4 * 4:k4 * 4 + 4, 4160:4192], R=[r_w], W=[wsm])
            groups = [(0, 512, "cq"), (512, 512, "ckv")] + [(1088 + g * 512, 512, "qkv") for g in range(6)] + \
                     [(4192 + g * 512, 512, "z") for g in range(4)]
            nwg = [0]
            nblk = 0
            nst = [0, 0, 0]

            def sumsq_norm(rawt, n, gcol, outb, out32):
                pb = pbank()
                for k in range(4):
                    c.mm(pb.ap(128, n), onesb[:], sq[:, k, 0:n], k == 0, k == 3, R=[onesb, sq], W=pb.rs)
                rstd_from_ps(pb, n, 1.0 / 512, rstd, tmpn)
                for k in range(4):
                    c.stt("dve", outb[:, k, 0:n], rawt[:, k, 0:n], gcol[:, k:k + 1], rstd[:, 0:n], ALU.mult, ALU.mult, R=[rawt, gcol, rstd], W=[outb])
                    if out32 is not None:
                        c.stt("pool", out32[:, k, 0:n], rawt[:, k, 0:n], gcol[:, k:k + 1], rstd[:, 0:n], ALU.mult, ALU.mult, R=[rawt, gcol, rstd, outb], W=[out32])

            def kv_up(src_bf, n, kn_dst, v_dst_rows):
                for h in range(NH):
                    pb = pbank()
                    for k in range(4):
                        c.mm(pb.ap(128, n), wukk[:, k, h, :], src_bf[:, k, 0:n], k == 0, k == 3, R=[wukk, src_bf], W=pb.rs)
                    if h % 2 == 0:
                        c.copy("dve", stk[:, h % 2, 0:n], pb.ap(128, n), R=pb.rs, W=[stk])
                    else:
                        c.copy("act", stk[:, h % 2, 0:n], pb.ap(128, n), R=pb.rs, W=[stk])
                    c.dma("pool", "st1", kn_dst(h), stk[:, h % 2, 0:n], R=[stk], W=[c.res("kn")])
                for j in range(n // 128):
                    sv = stv[nst[0] % 2]; nst[0] += 1
                    for half in range(2):
                        pb = pbank()
                        for k in range(4):
                            c.mm(pb.ap(128, 512), src_bf[:, k, j * 128:(j + 1) * 128], wukv[:, k, half * 4:(half + 1) * 4, :], k == 0, k == 3,
                                 R=[src_bf, wukv], W=pb.rs)
                        if half == 0:
                            c.copy("dve", sv[:, 0:512], pb.ap(128, 512), R=pb.rs, W=[sv])
                        else:
                            c.copy("act", sv[:, 512:1024], pb.ap(128, 512), R=pb.rs, W=[sv])
                    c.dma("pool", "st1", v_dst_rows(j), sv[:], R=[sv], W=[c.res("v")])

            for sq_ in seqs:
                A, Bv = mods[sq_["ci"]]
                is_s = sq_["kind"] == "s"
                L = sq_["L"]
                for b0 in range(0, L, TB):
                    n = min(TB, L - b0)
                    t0 = sq_["t0"] + b0
                    hT = hTs[nblk % 2]; nblk += 1
                    cfg.chk(1 + (10 if is_s else 0))
                    norm_transpose((xt, sq, ss), first, t0, n, A, Bv, hT)
                    cfg.chk(2 + (10 if is_s else 0))
                    if is_s:
                        c.dma("sp", "ld0", ropec[:, 0:n], k_rope[0, :, b0:b0 + n], W=[ropec])
                        c.dma("sp", "ld0", ropes[:, 0:n], k_rope[1, :, b0:b0 + n], W=[ropes])
                    for (c0, ncol, kind) in groups:
                        wt = wg[nwg[0] % 2]; nwg[0] += 1
                        for k4 in range(4):
                            c.dma("sp", "ldw", wt[:, k4 * 4:k4 * 4 + 4, 0:ncol], wv[:, k4 * 4:k4 * 4 + 4, c0:c0 + ncol], R=[r_w], W=[wt])
                        for fc in range(ncol // 128):
                            pb = pbank()
                            for k in range(16):
                                c.mm(pb.ap(128, n), wt[:, k, fc * 128:(fc + 1) * 128], hT[:, k, 0:n], k == 0, k == 15, R=[wt, hT], W=pb.rs)
                            if kind in ("cq", "ckv"):
                                rw = raw[0 if kind == "cq" else 1]
                                c.copy("dve", rw[:, fc, 0:n], pb.ap(128, n), R=pb.rs, W=[rw])
                                c.act(sq[:, fc, 0:n], rw[:, fc, 0:n], AF.Square, R=[rw], W=[sq])
                            elif kind == "qkv":
                                s32 = st32[nst[1] % 3]; nst[1] += 1
                                if fc % 2 == 0:
                                    c.copy("dve", s32[:, 0:n], pb.ap(128, n), R=pb.rs, W=[s32])
                                else:
                                    c.copy("act", s32[:, 0:n], pb.ap(128, n), R=pb.rs, W=[s32])
                                r0 = c0 - 1088 + fc * 128
                                c.dma("pool", "st1", qkv_s[r0:r0 + 128, t0:t0 + n], s32[:, 0:n], R=[s32], W=[c.res("qkv")])
                            else:
                                sz = stz[nst[2] % 2]; nst[2] += 1
                                c.act(sz[:, 0:n], pb.ap(128, n), AF.Silu, R=pb.rs, W=[sz])
                                r0 = c0 - 4192 + fc * 128
                                c.dma("pool", "st1", sz_s[r0:r0 + 128, t0:t0 + n], sz[:, 0:n], R=[sz], W=[c.res("sz")])
                        cfg.chk(3 + (10 if is_s else 0))
                        if kind == "cq":
                            sumsq_norm(raw[0], n, qg, cqn, None)
                            for h in range(NH):
                                pb = pbank()
                                for k in range(4):
                                    c.mm(pb.ap(128, n), wuq[:, k, h * 192:h * 192 + 128], cqn[:, k, 0:n], k == 0, k == 3, R=[wuq, cqn], W=pb.rs)
                                c.copy("act", stq[:, h % 2, 0:n], pb.ap(128, n), R=pb.rs, W=[stq])
                                c.dma("pool", "st1", qn_s[h, :, t0:t0 + n], stq[:, h % 2, 0:n], R=[stq], W=[c.res("qn")])
                                pr = pbank()
                                for k in range(4):
                                    c.mm(pr.ap(64, n), wuq[:, k, h * 192 + 128:h * 192 + 192], cqn[:, k, 0:n], k == 0, k == 3, R=[wuq, cqn], W=pr.rs)
                                if is_s:
                                    pr2 = pbank()
                                    for k in range(4):
                                        c.mm(pr2.ap(64, n), wuqp[:, k, h, :], cqn[:, k, 0:n], k == 0, k == 3, R=[wuqp, cqn], W=pr2.rs)
                                    c.tt("dve", rt1[:, 0:n], pr.ap(64, n), ropec[:, 0:n], ALU.mult, R=pr.rs + [ropec], W=[rt1])
                                    c.tt("dve", rt2[:, 0:n], pr2.ap(64, n), ropes[:, 0:n], ALU.mult, R=pr2.rs + [ropes], W=[rt2])
                                    c.tt("pool", stqr[:, h % 2, 0:n], rt1[:, 0:n], rt2[:, 0:n], ALU.add, R=[rt1, rt2], W=[stqr])
                                else:
                                    c.copy("dve", stqr[:, h % 2, 0:n], pr.ap(64, n), R=pr.rs, W=[stqr])
                                c.dma("pool", "st1", qr_s[h, :, t0:t0 + n], stqr[:, h % 2, 0:n], R=[stqr], W=[c.res("qr")])
                        if kind == "ckv":
                            cfg.chk(4 + (10 if is_s else 0))
                            sumsq_norm(raw[1], n, kg_, ckn, None if is_s else ckn32)
                            if is_s:
                                kv_up(ckn, n, lambda h: kn_x[h, :, LC + b0:LC + b0 + n],
                                      lambda j: v_x[LC + b0 + j * 128:LC + b0 + (j + 1) * 128, :])
                            else:
                                kv_up(ckn, n, lambda h: kn_p[h, :, t0:t0 + n],
                                      lambda j: v_p[t0 + j * 128:t0 + (j + 1) * 128, :])
                    cfg.chk(5 + (10 if is_s else 0))
                    pk = pbank()
                    for k in range(16):
                        c.mm(pk.ap(64, n), wsm[:, k, 0:64], hT[:, k, 0:n], k == 0, k == 15, R=[wsm, hT], W=pk.rs)
                    if is_s:
                        pk2 = pbank()
                        for k in range(16):
                            c.mm(pk2.ap(64, n), wsm[:, k, 64:128], hT[:, k, 0:n], k == 0, k == 15, R=[wsm, hT], W=pk2.rs)
                        c.tt("dve", rt1[:, 0:n], pk.ap(64, n), ropec[:, 0:n], ALU.mult, R=pk.rs + [ropec], W=[rt1])
                        c.tt("dve", rt2[:, 0:n], pk2.ap(64, n), ropes[:, 0:n], ALU.mult, R=pk2.rs + [ropes], W=[rt2])
                        c.tt("pool", stkr[:, 0:n], rt1[:, 0:n], rt2[:, 0:n], ALU.add, R=[rt1, rt2], W=[stkr])
                        c.dma("pool", "st1", kr_x[:, LC + b0:LC + b0 + n], stkr[:, 0:n], R=[stkr], W=[c.res("kr")])
                    else:
                        c.copy("dve", kpe32[:, 0:n], pk.ap(64, n), R=pk.rs, W=[kpe32])
                        c.copy("act", stkr[:, 0:n], kpe32[:, 0:n], R=[kpe32], W=[stkr])
                        c.dma("pool", "st1", kr_p[:, t0:t0 + n], stkr[:, 0:n], R=[stkr], W=[c.res("kr")])
                        for j in range(n // 128):
                            ot = otok[j % 2]
                            pb = pbank()
                            for k in range(4):
                                c.op("pe", lambda e: e.transpose(out=pb.ap(128, 128, k * 128), in_=ckn32[:, k, j * 128:(j + 1) * 128], identity=ident[:]),
                                     R=[ckn32, ident], W=pb.rs)
                            c.copy("dve", ot[:, 0:512], pb.ap(128, 512), R=pb.rs, W=[ot])
                            pb2 = pslot()
                            c.op("pe", lambda e: e.transpose(out=pb2.ap(128, 64), in_=kpe32[:, j * 128:(j + 1) * 128], identity=ident[0:64, 0:64]),
                                 R=[kpe32, ident], W=pb2.rs)
                            c.copy("act", ot[:, 512:576], pb2.ap(128, 64), R=pb2.rs, W=[ot])
                            l0 = b0 + j * 128
                            c.dma("pool", "st2", nckv[sq_["idx"], i, l0:l0 + 128, :], ot[:, 0:512], R=[ot], W=[c.res("nckv")])
                            c.dma("pool", "st2", nkpe[sq_["idx"], i, l0:l0 + 128, :], ot[:, 512:576], R=[ot], W=[c.res("nkpe")])
                    cfg.chk(6 + (10 if is_s else 0))
                    pa = pbank()
                    for k in range(16):
                        c.mm(pa.ap(32, n), wsm[:, k, 128:160], hT[:, k, 0:n], k == 0, k == 15, R=[wsm, hT], W=pa.rs)
                    s32 = st32[nst[1] % 3]; nst[1] += 1
                    c.copy("dve", s32[0:32, 0:n], pa.ap(32, n), R=pa.rs, W=[s32])
                    c.dma("pool", "st1", abT_s[:, t0:t0 + n], s32[0:32, 0:n], R=[s32], W=[c.res("abT")])
                    cfg.chk(8 + (10 if is_s else 0))
            cfg.chk(7)
            ctx32 = c.sb("ctx32", [128, 576], F32, ph)
            for j in range(LC // 128):
                c.dma("sp", "ld0", ctx32[:, 0:512], cckv[i, j * 128:(j + 1) * 128, :], W=[ctx32])
                c.dma("sp", "ld0", ctx32[:, 512:576], ckpe[i, j * 128:(j + 1) * 128, :], W=[ctx32])
                pb = pbank()
                for k in range(4):
                    c.op("pe", lambda e: e.transpose(out=pb.ap(128, 128, k * 128), in_=ctx32[:, k * 128:(k + 1) * 128], identity=ident[:]),
                         R=[ctx32, ident], W=pb.rs)
                for k in range(4):
                    c.copy("dve", ckn[:, k, j * 128:(j + 1) * 128], pb.ap(128, 128, k * 128), R=pb.rs, W=[ckn])
                pb2 = pslot()
                c.op("pe", lambda e: e.transpose(out=pb2.ap(64, 128), in_=ctx32[:, 512:576], identity=ident[:]), R=[ctx32, ident], W=pb2.rs)
                c.copy("act", stkr[:, j * 128:(j + 1) * 128], pb2.ap(64, 128), R=pb2.rs, W=[stkr])
            c.dma("pool", "st1", kr_x[:, 0:LC], stkr[:, 0:LC], R=[stkr], W=[c.res("kr")])
            kv_up(ckn, LC, lambda h: kn_x[h, :, 0:LC], lambda j: v_x[j * 128:(j + 1) * 128, :])
        c.barrier()

    def attn(layer, ph):
        if True:
            LKMAX = LC + LS
            kn_t = [c.sb(f"kn{b}", [128, LKMAX], BF16, ph) for b in range(2)]
            v_t = [c.sb(f"vv{b}", [128, LKMAX // 128, 128], BF16, ph) for b in range(2)]
            kr_t = c.sb("krt", [64, LKMAX], BF16, ph)
            qn_t = [c.sb(f"qnt{b}", [128, 512], BF16, ph) for b in range(2)]
            qr_t = [c.sb(f"qrt{b}", [64, 512], BF16, ph) for b in range(2)]
            pT = [c.sb(f"pT{b}", [128, 512], BF16, ph) for b in range(4)]
            szt = [c.sb(f"szt{b}", [128, 512], BF16, ph) for b in range(2)]
            yst = [c.sb(f"yst{b}", [128, 512], BF16, ph) for b in range(2)]
            rden = c.sb("rden", [128, 512], F32, ph)
            tmpo = c.sb("tmpo", [128, 512], F32, ph)
            scale = 192.0 ** -0.5
            nh = 0; nq = 0; npt_box = [0]
            for sq_ in seqs:
                is_s = sq_["kind"] == "s"
                L = sq_["L"]; t0 = sq_["t0"]
                Lk = LC + L if is_s else L
                nkc = Lk // 128
                QB = min(512, L)
                c.dma("sp", "ld0", kr_t[:, 0:Lk], (kr_x[:, 0:Lk] if is_s else kr_p[:, t0:t0 + Lk]), R=[c.res("kr")], W=[kr_t])
                for h in range(NH):
                    knt = kn_t[nh % 2]; vt = v_t[nh % 2]; nh += 1
                    c.dma("sp", "ldk", knt[:, 0:Lk], (kn_x[h, :, 0:Lk] if is_s else kn_p[h, :, t0:t0 + Lk]), R=[c.res("kn")], W=[knt])
                    vsrc = (v_x[0:Lk, h * 128:(h + 1) * 128] if is_s else v_p[t0:t0 + Lk, h * 128:(h + 1) * 128])
                    c.dma("sp", "ldk", vt[:, 0:nkc, :], vsrc.rearrange("(n p) e -> p n e", p=128), R=[c.res("v")], W=[vt])
                    for q0 in range(0, L, QB):
                        qnt = qn_t[nq % 2]; qrt = qr_t[nq % 2]; szT = szt[nq % 2]; yT = yst[nq % 2]
                        a, b = t0 + q0, t0 + q0 + QB
                        c.dma("sp", "ldq", qnt[:, 0:QB], qn_s[h, :, a:b], R=[c.res("qn")], W=[qnt])
                        c.dma("sp", "ldq", qrt[:, 0:QB], qr_s[h, :, a:b], R=[c.res("qr")], W=[qrt])
                        c.dma("sp", "ldq", szT[:, 0:QB], sz_s[h * 128:(h + 1) * 128, a:b], R=[c.res("sz")], W=[szT])
                        po = pbank(4 + nq % 2); pd = pbank(6 + nq % 2); nq += 1
                        def scores(kc):
                            nonlocal_npt = npt_box[0]; npt_box[0] += 1
                            psb = pbank(nonlocal_npt % 4); p_t = pT[nonlocal_npt % 4]
                            c.mm(psb.ap(128, QB), knt[:, kc * 128:(kc + 1) * 128], qnt[:, 0:QB], True, False, R=[knt, qnt], W=psb.rs)
                            c.mm(psb.ap(128, QB), kr_t[:, kc * 128:(kc + 1) * 128], qrt[:, 0:QB], False, True, R=[kr_t, qrt], W=psb.rs)
                            c.act(p_t[:, 0:QB], psb.ap(128, QB), AF.Exp, R=psb.rs, W=[p_t], scale=scale)
                            return p_t
                        pend = [scores(0)]
                        if nkc > 1:
                            pend.append(scores(1))
                        for kc in range(nkc):
                            p_t = pend.pop(0)
                            if kc + 2 < nkc:
                                pend.append(scores(kc + 2))
                            c.mm(po.ap(128, QB), vt[:, kc, :], p_t[:, 0:QB], kc == 0, kc == nkc - 1, R=[vt, p_t], W=po.rs)
                            c.mm(pd.ap(128, QB), onesb[:], p_t[:, 0:QB], kc == 0, kc == nkc - 1, R=[onesb, p_t], W=pd.rs)
                        c.op("dve", lambda e: e.reciprocal(out=rden[:, 0:QB], in_=pd.ap(128, QB)), R=pd.rs, W=[rden])
                        c.tt("dve", tmpo[:, 0:QB], po.ap(128, QB), rden[:, 0:QB], ALU.mult, R=po.rs + [rden], W=[tmpo])
                        c.tt("pool", yT[:, 0:QB], tmpo[:, 0:QB], szT[:, 0:QB], ALU.mult, R=[tmpo, szT], W=[yT])
                        c.dma("pool", "st1", yT_s[h * 128:(h + 1) * 128, a:b], yT[:, 0:QB], R=[yT], W=[c.res("yT")])
                        yield

    def even_E3(layer, with_pre=False):
        with ExitStack() as ph:
            gens = [attn(layer, ph)] + ([gdn_pre(layer, ph)] if with_pre else [])
            while gens:
                for g_ in list(gens):
                    try:
                        next(g_)
                    except StopIteration:
                        gens.remove(g_)
        c.barrier()

    def out_proj(layer, wsrc):
        first = layer == 0
        last = layer == cfg.depth - 1
        with ExitStack() as ph:
            wo = c.sb("wo", [128, 16, D], BF16, ph)
            wov = wsrc.rearrange("(k p) e -> p k e", p=128)
            for k4 in range(8):
                c.dma("sp", "ld0", wo[:, k4 * 2:k4 * 2 + 2, :], wov[:, k4 * 2:k4 * 2 + 2, :], R=[r_w], W=[wo])
            gts = []
            for ci in range(2):
                g = c.sb(f"gt{ci}", [128, D], F32, ph)
                c.dma("sp", "ld0", g[:], modv[layer, ci, 2 * D:3 * D].partition_broadcast(128), R=[r_modv], W=[g])
                gts.append(g)
            fn = None
            if last:
                fn = c.sb("fnw", [128, D], F32, ph)
                c.dma("sp", "ld0", fn[:], final_norm.partition_broadcast(128), W=[fn])
            yts = [c.sb(f"yt{b}", [128, 16, 512], BF16, ph) for b in range(2)]
            xts = [c.sb(f"xo{b}", [128, D], F32, ph) for b in range(2)]
            tms = [c.sb(f"tm{b}", [128, 512], F32, ph) for b in range(2)]
            junk = c.sb("junkb", [128, D], BF16, ph)
            ss = c.sb("ss4", [128, 4], F32, ph)
            nb = 0; nt = 0; ntm = 0
            for sq_ in seqs:
                L = sq_["L"]; g = gts[sq_["ci"]]
                for b0 in range(0, L, 512):
                    n = min(512, L - b0)
                    t0 = sq_["t0"] + b0
                    yt = yts[nb % 2]; nb += 1
                    for k4 in range(4):
                        c.dma("sp", "ldk", yt[:, k4 * 4:k4 * 4 + 4, 0:n], yT_s.rearrange("(k p) l -> p k l", p=128)[:, k4 * 4:k4 * 4 + 4, t0:t0 + n],
                              R=[c.res("yT")], W=[yt])
                    for j in range(n // 128):
                        xt = xts[nt % 2]; nt += 1
                        tj = t0 + j * 128
                        xres = c.res(("x", tj // 128))
                        c.dma("sp", "ldx", xt[:], xrows(first, tj, 128), R=[xres], W=[xt])
                        for nn in range(4):
                            pb = pbank()
                            for k in range(16):
                                c.mm(pb.ap(128, 512), yt[:, k, j * 128:(j + 1) * 128], wo[:, k, nn * 512:(nn + 1) * 512], k == 0, k == 15, R=[yt, wo], W=pb.rs)
                            tm = tms[ntm % 2]; ntm += 1
                            c.tt("dve", tm[:], pb.ap(128, 512), g[:, nn * 512:(nn + 1) * 512], ALU.mult, R=pb.rs + [g], W=[tm])
                            c.tt("pool", xt[:, nn * 512:(nn + 1) * 512], xt[:, nn * 512:(nn + 1) * 512], tm[:], ALU.add, R=[tm, xt], W=[xt])
                        if last:
                            c.act(junk[:], xt[:], AF.Square, R=[xt], W=[junk, ss], accum=ss[:, 0:1])
                            c.act(ss[:, 1:2], ss[:, 0:1], AF.Ln, R=[ss, epsT], W=[ss], bias=epsT[:, 0:1], scale=1.0 / D)
                            c.act(ss[:, 2:3], ss[:, 1:2], AF.Exp, R=[ss], W=[ss], scale=-0.5)
                            c.stt("dve", xt[:], xt[:], ss[:, 2:3], fn[:], ALU.mult, ALU.mult, R=[xt, ss, fn], W=[xt])
                            dst = yp[tj:tj + 128, :] if tj < NPT else ys[tj - NPT:tj - NPT + 128, :]
                            c.dma("pool", "st2", dst, xt[:], R=[xt], W=[xres])
                        else:
                            dst = xw_p[tj:tj + 128, :] if tj < NPT else xw_s[tj - NPT:tj - NPT + 128, :]
                            c.dma("pool", "st2", dst, xt[:], R=[xt], W=[xres])
        c.barrier()

    def gdn_pre(layer, ph):
        i = layer // 2
        if True:
            cw = c.sb("cw", [128, 3, 24], F32, ph)
            for t in range(3):
                c.dma("sp", "ld0", cw[:, t, :], conv_e[i, t].rearrange("(k p) -> p k", p=128), W=[cw], allow_slow_non_contiguous=True)
            LM = max(s_["L"] for s_ in seqs)
            xr = [c.sb(f"xr{b}", [128, LM + 2], F32, ph) for b in range(2)]
            accs = [c.sb(f"acc{b}", [128, LM], F32, ph) for b in range(2)]
            ysls = [c.sb(f"ysl{b}", [128, LM], F32, ph) for b in range(2)]
            sqb = c.sb("sqb", [128, LM], BF16, ph)
            rinv = c.sb("rinv", [128, LM], F32, ph)
            tln = c.sb("tln", [128, 512], F32, ph)
            outb = [c.sb(f"outb{b}", [128, LM], BF16, ph) for b in range(2)]
            bbc = c.sb("bbc", [128, LM], F32, ph)
            nx = 0; no = 0
            for sq_ in seqs:
                L = sq_["L"]; t0 = sq_["t0"]
                for h in range(NH):
                    kn_keep = None
                    for which in (1, 0, 2):
                        fcx = which * 8 + h
                        x = xr[nx % 2]; acc = accs[nx % 2]; ysl = ysls[nx % 2]; nx += 1
                        c.op("pool", lambda e: e.memset(x[:, 0:1], 0.0), W=[x])
                        c.op("pool", lambda e: e.memset(x[:, L + 1:L + 2], 0.0), W=[x])
                        c.dma("sp", "ldk", x[:, 1:L + 1], qkv_s[fcx * 128:(fcx + 1) * 128, t0:t0 + L], R=[c.res("qkv")], W=[x])
                        c.act(acc[:, 0:L], x[:, 1:L + 1], AF.Identity, R=[x, cw], W=[acc], scale=cw[:, 1, fcx:fcx + 1])
                        c.stt("dve", acc[:, 0:L], x[:, 0:L], cw[:, 0, fcx:fcx + 1], acc[:, 0:L], ALU.mult, ALU.add, R=[x, cw, acc], W=[acc])
                        c.stt("dve", acc[:, 0:L], x[:, 2:L + 2], cw[:, 2, fcx:fcx + 1], acc[:, 0:L], ALU.mult, ALU.add, R=[x, cw, acc], W=[acc])
                        c.act(ysl[:, 0:L], acc[:, 0:L], AF.Silu, R=[acc], W=[ysl])
                        ob = outb[no % 2]; no += 1
                        if which == 2:
                            c.copy("act", ob[:, 0:L], ysl[:, 0:L], R=[ysl], W=[ob])
                            c.dma("pool", "st1", gv_s[h, :, t0:t0 + L], ob[:, 0:L], R=[ob], W=[c.res("gv")])
                            yield
                            continue
                        c.act(sqb[:, 0:L], ysl[:, 0:L], AF.Square, R=[ysl], W=[sqb])
                        for b0 in range(0, L, 512):
                            n = min(512, L - b0)
                            pb = pbank()
                            c.mm(pb.ap(128, n), onesb[:], sqb[:, b0:b0 + n], True, True, R=[onesb, sqb], W=pb.rs)
                            c.act(tln[:, 0:n], pb.ap(128, n), AF.Ln, R=pb.rs + [epsT], W=[tln], bias=epsT[:, 0:1], scale=1.0)
                            c.act(rinv[:, b0:b0 + n], tln[:, 0:n], AF.Exp, R=[tln], W=[rinv], scale=-0.5)
                        if which == 0:
                            c.stt("dve", ob[:, 0:L], ysl[:, 0:L], 128.0 ** -0.5, rinv[:, 0:L], ALU.mult, ALU.mult, R=[ysl, rinv], W=[ob])
                            c.dma("pool", "st1", gq_s[h, :, t0:t0 + L], ob[:, 0:L], R=[ob], W=[c.res("gq")])
                            yield
                        else:
                            c.tt("dve", ysl[:, 0:L], ysl[:, 0:L], rinv[:, 0:L], ALU.mult, R=[ysl, rinv], W=[ysl])
                            c.copy("act", ob[:, 0:L], ysl[:, 0:L], R=[ysl], W=[ob])
                            c.dma("pool", "st1", gk_s[h, :, t0:t0 + L], ob[:, 0:L], R=[ob], W=[c.res("gk")])
                            for d in range(2):
                                c.dma("sp", "ldk", bbc[:, 0:L], abT_s[16 + d * 8 + h, t0:t0 + L].partition_broadcast(128), R=[c.res("abT")], W=[bbc])
                                c.act(bbc[:, 0:L], bbc[:, 0:L], AF.Sigmoid, R=[bbc], W=[bbc])
                                ob2 = outb[no % 2]; no += 1
                                c.tt("dve", ob2[:, 0:L], ysl[:, 0:L], bbc[:, 0:L], ALU.mult, R=[ysl, bbc], W=[ob2])
                                c.dma("pool", "st1", gkb_s[d, h, :, t0:t0 + L], ob2[:, 0:L], R=[ob2], W=[c.res("gkb")])
                            yield

    def even_E2(layer):
        i = layer // 2
        with ExitStack() as ph0:
            for _ in gdn_pre(layer, ph0):
                pass
        c.barrier()
        with ExitStack() as ph:
            msk = c.sb("msk", [128, 9, 128], F32, ph)
            cum = c.sb("cum", [128, 2, 128], F32, ph)
            for m_ in range(9):
                c.dma("sp", "ld0", msk[:, m_, :], k_masks[m_], W=[msk])
            for m_ in range(2):
                c.dma("sp", "ld0", cum[:, m_, :], k_cum[m_], W=[cum])
            dtb = c.sb("dtb", [128, 16], F32, ph)
            nega = c.sb("nega", [128, 16], F32, ph)
            c.dma("sp", "ld0", dtb[:], dt_bias_e[i].partition_broadcast(128), W=[dtb])
            c.dma("sp", "ld0", nega[:], a_log_e[i].partition_broadcast(128), W=[nega])
            c.act(nega[:], nega[:], AF.Exp, R=[nega], W=[nega])
            c.ts("dve", nega[:], nega[:], -1.0, ALU.mult, R=[nega], W=[nega])
            onw = load_cols(ph, "onw", o_norm_e[i], 1)
            NCM = max(max(s_["L"] for s_ in seqs) // 128, 8)
            abT_sb = c.sb("abT_sb", [32, NCM * 128], F32, ph)
            ab_tm = c.sb("ab_tm", [128, NCM, 32], F32, ph)
            T = lambda nm: c.sb(nm, [128, NCM, 16], F32, ph)
            xg, ta, tb_, gg, beta, Gc, expG, kdsc, dec, bexpG = [T(n_) for n_ in ("xg", "ta", "tb_", "gg", "beta", "Gc", "expG", "kdsc", "dec", "bexpG")]
            fl = lambda t_: t_[:].rearrange("p a b -> p (a b)")
            GS = cfg.gs; NG = NH // GS
            G2 = lambda nm, dt: [c.sb(f"{nm}{g}", [128, GS, 128], dt, ph) for g in range(NG)]
            FL = lambda t_: t_[:].rearrange("p a b -> p (a b)")
            ld = [[c.sb(f"ld{nm}{b}", [128, NH, 128], BF16, ph) for b in range(2)] for nm in ("k", "q", "v", "kb")]
            NDT = F32 if cfg.neu32 else BF16
            gam1, gamT, gamTs, egrow, tmp32 = [G2(n_, F32) for n_ in ("gam1", "gamT", "gamTs", "egrow", "tmp32")]
            diag = tmp32
            IT8, QD8 = [G2(n_, BF16) for n_ in ("IT8", "QD8")]
            A8, AT8 = [G2(n_, NDT) for n_ in ("A8", "AT8")]
            Ad8, ATd8, AL8, Td8, Z8 = [G2(n_, NDT) for n_ in ("Ad8", "ATd8", "AL8", "Td8", "Z8")]
            PTf = G2("PTf", BF16)
            PTfb = PTf
            identn = ident if cfg.neu32 else identb
            Xa, Xb, XTa, XTb, PTa, PTb = [G2(n_, NDT) for n_ in ("Xa", "Xb", "XTa", "XTb", "PTa", "PTb")]
            KBG, KD, VB, WT, VN = [G2(n_, BF16) for n_ in ("KBG", "KD", "VB", "WT", "VN")]
            U32, S32, O32 = [G2(n_, F32) for n_ in ("U32", "S32", "O32")]
            ON32 = U32
            Sbf = G2("Sbf", BF16)
            YG = G2("YG", BF16)
            rep = {}
            for nm_, src_ in (("m01f", k_masks[4]), ("m01b", k_masks[5]), ("bd", k_masks[6]), ("ll", k_masks[7]), ("ur", k_masks[8]), ("id", k_ident)):
                t_ = c.sb("rep_" + nm_, [128, GS, 128], F32, ph)
                for hh in range(GS):
                    c.dma("sp", "ld0", t_[:, hh, :], src_, W=[t_])
                rep[nm_] = t_
            OG = c.sb("OG", [128, NH, 128], F32, ph)
            szc = c.sb("szc", [128, NH, 128], BF16, ph)
            ssq = c.sb("ssq", [128, 3, NH], F32, ph)
            junk = c.sb("junk2", [128, 128], BF16, ph)
            nld = 0
            for sq_ in seqs:
                L = sq_["L"]; t0 = sq_["t0"]; is_s = sq_["kind"] == "s"
                nch = L // 128
                NP_ = max(nch, 8)
                c.dma("sp", "ld0", abT_sb[:, 0:L], abT_s[:, t0:t0 + L], R=[c.res("abT")], W=[abT_sb])
                for c4 in range(0, nch, 4):
                    ps4 = pbank()
                    nn_ = min(4, nch - c4)
                    for q_ in range(nn_):
                        c.op("pe", lambda e: e.transpose(out=ps4.ap(128, 32, q_ * 32), in_=abT_sb[:, (c4 + q_) * 128:(c4 + q_ + 1) * 128], identity=ident[0:32, 0:32]),
                             R=[abT_sb, ident], W=ps4.rs)
                    c.copy("dve", ab_tm[:, c4:c4 + nn_, :].rearrange("p a b -> p (a b)"), ps4.ap(128, nn_ * 32), R=ps4.rs, W=[ab_tm])
                cfg.chk(22)
                if NP_ > nch:
                    c.op("pool", lambda e: e.memset(fl(gg), 0.0), W=[gg])
                for j_ in range(16):
                    c.ts("dve", xg[:, 0:nch, j_], ab_tm[:, 0:nch, j_], dtb[:, j_:j_ + 1], ALU.add, R=[ab_tm, dtb], W=[xg])
                xf, taf, tbf = xg[:, 0:nch, :], ta[:, 0:nch, :], tb_[:, 0:nch, :]
                c.ts("dve", taf, xf, -1.0, ALU.mult, R=[xg], W=[ta])
                c.tt("dve", taf, taf, xf, ALU.max, R=[ta, xg], W=[ta])
                c.act(taf, taf, AF.Exp, R=[ta], W=[ta], scale=-1.0)
                c.act(taf, taf, AF.Ln, R=[ta, oneT], W=[ta], bias=oneT[:, 0:1], scale=1.0)
                c.ts("dve", tbf, xf, 0.0, ALU.max, R=[xg], W=[tb_])
                c.tt("dve", taf, taf, tbf, ALU.add, R=[ta, tb_], W=[ta])
                for j_ in range(16):
                    c.ts("dve", gg[:, 0:nch, j_], ta[:, 0:nch, j_], nega[:, j_:j_ + 1], ALU.mult, R=[ta, nega], W=[gg])
                c.act(beta[:, 0:nch, :], ab_tm[:, 0:nch, 16:32], AF.Exp, R=[ab_tm], W=[beta], scale=-1.0)
                c.ts("dve", beta[:, 0:nch, :], beta[:, 0:nch, :], 1.0, ALU.add, R=[beta], W=[beta])
                c.op("dve", lambda e: e.reciprocal(out=beta[:, 0:nch, :], in_=beta[:, 0:nch, :]), R=[beta], W=[beta])
                pF = pbank(); pB = pbank(); pT_ = pbank()
                W_ = NP_ * 16
                c.mm(pF.ap(128, W_), cum[:, 0, :], fl(gg)[:, 0:W_], True, True, R=[cum, gg], W=pF.rs)
                c.mm(pB.ap(128, W_), cum[:, 1, :], fl(gg)[:, 0:W_], True, True, R=[cum, gg], W=pB.rs)
                c.mm(pT_.ap(128, W_), ones32[:], fl(gg)[:, 0:W_], True, True, R=[ones32, gg], W=pT_.rs)
                v3 = lambda ps_: ps_.ap(128, W_).rearrange("p (a b) -> p a b", b=16)
                c.copy("dve", Gc[:, 0:NP_, 0:8], v3(pF)[:, :, 0:8], R=pF.rs, W=[Gc])
                c.copy("dve", Gc[:, 0:NP_, 8:16], v3(pB)[:, :, 8:16], R=pB.rs, W=[Gc])
                c.act(fl(expG)[:, 0:W_], fl(Gc)[:, 0:W_], AF.Exp, R=[Gc], W=[expG])
                c.tt("dve", fl(kdsc)[:, 0:W_], pT_.ap(128, W_), fl(Gc)[:, 0:W_], ALU.subtract, R=pT_.rs + [Gc], W=[kdsc])
                c.act(fl(kdsc)[:, 0:W_], fl(kdsc)[:, 0:W_], AF.Exp, R=[kdsc], W=[kdsc])
                c.act(fl(dec)[:, 0:W_], pT_.ap(128, W_), AF.Exp, R=pT_.rs, W=[dec])
                c.tt("dve", fl(bexpG)[:, 0:W_], fl(beta)[:, 0:W_], fl(expG)[:, 0:W_], ALU.mult, R=[beta, expG], W=[bexpG])
                if sq_ is seqs[0]:
                    for nm_, t_ in (("Gc", Gc), ("expG", expG), ("kdsc", kdsc), ("dec", dec), ("bexpG", bexpG), ("beta", beta), ("gg", gg)):
                        dbg(nm_, t_[:], [128, NCM, 16], R=[t_])
                    dbg("ab_tm", ab_tm[:], [128, NCM, 32], R=[ab_tm])
                cfg.chk(23)
                for d in range(2):
                    mA, mB, m01 = (0, 1, 4) if d == 0 else (2, 3, 5)
                    for gi in range(NG):
                        for hh in range(GS):
                            h = gi * GS + hh
                            if is_s:
                                c.dma("sp", "ld0", S32[gi][:, hh, :], (sfw if d == 0 else sbw)[i, h], W=[S32[gi]])
                        if not is_s:
                            c.op("pool", lambda e: e.memset(FL(S32[gi]), 0.0), W=[S32[gi]])
                        c.copy("act", FL(Sbf[gi]), FL(S32[gi]), R=[S32[gi]], W=[Sbf[gi]])
                    order = range(nch) if d == 0 else range(nch - 1, -1, -1)
                    for ci in order:
                        a_, b_ = t0 + ci * 128, t0 + (ci + 1) * 128
                        lk, lq, lv, lkb = [ld[x_][nld % 2] for x_ in range(4)]; nld += 1
                        c.dma("sp", "ldq", lk[:], gk_s.rearrange("h p l -> p h l")[:, :, a_:b_], R=[c.res("gk")], W=[lk])
                        c.dma("sp", "ldq", lq[:], gq_s.rearrange("h p l -> p h l")[:, :, a_:b_], R=[c.res("gq")], W=[lq])
                        c.dma("sp", "ldq", lv[:], gv_s.rearrange("h p l -> p h l")[:, :, a_:b_], R=[c.res("gv")], W=[lv])
                        c.dma("sp", "ldq", lkb[:], gkb_s[d].rearrange("h p l -> p h l")[:, :, a_:b_], R=[c.res("gkb")], W=[lkb])
                        if d == 1:
                            c.dma("sp", "ldq", OG[:].rearrange("p h e -> p (h e)"), og_s[a_:b_, :], R=[c.res(("og", a_))], W=[OG])
                            c.dma("sp", "ldq", szc[:], sz_s[1024:2048, a_:b_].rearrange("(h p) l -> p h l", p=128), R=[c.res("sz")], W=[szc])
                        col = lambda t_, h: t_[:, ci, d * 8 + h:d * 8 + h + 1]
                        m01r = rep["m01f"] if d == 0 else rep["m01b"]
                        offr = rep["ll"] if d == 0 else rep["ur"]
                        GI = tuple(range(NG))
                        hsl = lambda gi: slice(gi * GS, gi * GS + GS)
                        def mm4(pb, lhs, rhs, R):
                            for hh in range(GS):
                                c.mm(pb.ap(128, 128, hh * 128), lhs(hh), rhs(hh), True, True, R=R, W=pb.rs)
                        pg = {}
                        for gi in GI:
                            for hh in range(GS):
                                c.act(diag[gi][:, hh, :], ident[:], AF.Identity, R=[ident, Gc], W=[diag[gi]], scale=col(Gc, gi * GS + hh))
                            pg[gi] = pbank()
                            mm4(pg[gi], lambda hh: ones32[:], lambda hh: diag[gi][:, hh, :], [ones32, diag[gi]])
                        for gi in GI:
                            for hh in range(GS):
                                h = gi * GS + hh
                                c.stt("dve", gam1[gi][:, hh, :], pg[gi].ap(128, 128, hh * 128), col(Gc, h), msk[:, mA, :], ALU.subtract, ALU.subtract,
                                      R=pg[gi].rs + [Gc, msk], W=[gam1[gi]])
                                c.stt("dve", gamT[gi][:, hh, :], pg[gi].ap(128, 128, hh * 128), col(Gc, h), msk[:, mB, :], ALU.subtract, ALU.add,
                                      R=pg[gi].rs + [Gc, msk], W=[gamT[gi]])
                            c.act(FL(egrow[gi]), pg[gi].ap(128, GS * 128), AF.Exp, R=pg[gi].rs, W=[egrow[gi]])
                            c.act(FL(gam1[gi]), FL(gam1[gi]), AF.Exp, R=[gam1[gi]], W=[gam1[gi]], scale=-1.0)
                            c.act(FL(gamT[gi]), FL(gamT[gi]), AF.Exp, R=[gamT[gi]], W=[gamT[gi]])
                            c.tt("pool", FL(gamTs[gi]), FL(gamT[gi]), FL(m01r), ALU.mult, R=[gamT[gi], m01r], W=[gamTs[gi]])
                        cfg.chk(24)
                        for gi in GI:
                            o4 = gi * GS
                            p1 = pbank(); mm4(p1, lambda hh: lkb[:, o4 + hh, :], lambda hh: lk[:, o4 + hh, :], [lkb, lk])
                            p2 = pbank(); mm4(p2, lambda hh: lk[:, o4 + hh, :], lambda hh: lkb[:, o4 + hh, :], [lkb, lk])
                            p3 = pbank(); mm4(p3, lambda hh: lk[:, o4 + hh, :], lambda hh: lq[:, o4 + hh, :], [lk, lq])
                            c.tt("dve", FL(A8[gi]), p1.ap(128, GS * 128), FL(gam1[gi]), ALU.mult, R=p1.rs + [gam1[gi]], W=[A8[gi]])
                            c.tt("dve", FL(AT8[gi]), p2.ap(128, GS * 128), FL(gamTs[gi]), ALU.mult, R=p2.rs + [gamTs[gi]], W=[AT8[gi]])
                            c.tt("dve", FL(IT8[gi]), p3.ap(128, GS * 128), FL(gamT[gi]), ALU.mult, R=p3.rs + [gamT[gi]], W=[IT8[gi]])
                            c.tt("pool", FL(QD8[gi]), lq[:, hsl(gi), :].rearrange("p a b -> p (a b)"), FL(egrow[gi]), ALU.mult, R=[lq, egrow[gi]], W=[QD8[gi]])
                            c.tt("pool", FL(Ad8[gi]), FL(A8[gi]), FL(rep["bd"]), ALU.mult, R=[A8[gi], rep["bd"]], W=[Ad8[gi]])
                            c.tt("pool", FL(ATd8[gi]), FL(AT8[gi]), FL(rep["bd"]), ALU.mult, R=[AT8[gi], rep["bd"]], W=[ATd8[gi]])
                            c.tt("pool", FL(AL8[gi]), FL(A8[gi]), FL(offr), ALU.mult, R=[A8[gi], offr], W=[AL8[gi]])
                            c.tt("pool", FL(PTa[gi]), FL(rep["id"]), FL(ATd8[gi]), ALU.subtract, R=[rep["id"], ATd8[gi]], W=[PTa[gi]])
                        cfg.chk(25)
                        NL = 5
                        X, XT, PT = Ad8, ATd8, PTa
                        Xn, XTn, PTn = Xa, XTa, PTb
                        for lvl in range(1, NL + 1):
                            sA, sB, sC = {}, {}, {}
                            for gi in GI:
                                sA[gi] = pbank(); mm4(sA[gi], lambda hh: XT[gi][:, hh, :], lambda hh: X[gi][:, hh, :], [XT[gi], X[gi]])
                                if lvl < NL:
                                    sB[gi] = pbank(); mm4(sB[gi], lambda hh: X[gi][:, hh, :], lambda hh: XT[gi][:, hh, :], [XT[gi], X[gi]])
                            for gi in GI:
                                c.copy("act", FL(Xn[gi]), sA[gi].ap(128, GS * 128), R=sA[gi].rs, W=[Xn[gi]])
                                if lvl < NL:
                                    c.copy("act" if lvl % 2 == 0 else "dve", FL(XTn[gi]), sB[gi].ap(128, GS * 128), R=sB[gi].rs, W=[XTn[gi]])
                            for gi in GI:
                                sC[gi] = pbank(); mm4(sC[gi], lambda hh: Xn[gi][:, hh, :], lambda hh: PT[gi][:, hh, :], [Xn[gi], PT[gi]])
                            for gi in GI:
                                c.tt("dve", FL(PTn[gi]), sC[gi].ap(128, GS * 128), FL(PT[gi]), ALU.add, R=sC[gi].rs + [PT[gi]], W=[PTn[gi]])
                            X, XT, PT = Xn, XTn, PTn
                            Xn = Xb if X is Xa else Xa
                            XTn = XTb if XT is XTa else XTa
                            PTn = PTa if PT is PTb else PTb
                        TdT = PT
                        sT, sZ, sR = {}, {}, {}
                        for gi in GI:
                            sT[gi] = pbank(); mm4(sT[gi], lambda hh: TdT[gi][:, hh, :], lambda hh: identn[:], [TdT[gi], identn])
                            sZ[gi] = pbank(); mm4(sZ[gi], lambda hh: AL8[gi][:, hh, :], lambda hh: TdT[gi][:, hh, :], [TdT[gi], AL8[gi]])
                        for gi in GI:
                            c.copy("act", FL(Td8[gi]), sT[gi].ap(128, GS * 128), R=sT[gi].rs, W=[Td8[gi]])
                            c.copy("dve", FL(Z8[gi]), sZ[gi].ap(128, GS * 128), R=sZ[gi].rs, W=[Z8[gi]])
                        for gi in GI:
                            sR[gi] = pbank(); mm4(sR[gi], lambda hh: Td8[gi][:, hh, :], lambda hh: Z8[gi][:, hh, :], [Td8[gi], Z8[gi]])
                        for gi in GI:
                            c.tt("dve", FL(PTf[gi]), FL(TdT[gi]), sR[gi].ap(128, GS * 128), ALU.subtract, R=sR[gi].rs + [TdT[gi]], W=[PTf[gi]])
                        PT = PTfb
                        cfg.chk(26)
                        for gi in GI:
                            o4 = gi * GS
                            p1 = pbank(); mm4(p1, lambda hh: lk[:, o4 + hh, :], lambda hh: identb[:], [lk, identb])
                            p2 = pbank(); mm4(p2, lambda hh: lv[:, o4 + hh, :], lambda hh: identb[:], [lv, identb])
                            for hh in range(GS):
                                c.act(KBG[gi][:, hh, :], p1.ap(128, 128, hh * 128), AF.Identity, R=p1.rs + [bexpG], W=[KBG[gi]], scale=col(bexpG, o4 + hh))
                            for hh in range(GS):
                                c.ts("dve", VB[gi][:, hh, :], p2.ap(128, 128, hh * 128), col(beta, o4 + hh), ALU.mult, R=p2.rs + [beta], W=[VB[gi]])
                            for hh in range(GS):
                                c.ts("dve", KD[gi][:, hh, :], p1.ap(128, 128, hh * 128), col(kdsc, o4 + hh), ALU.mult, R=p1.rs + [kdsc], W=[KD[gi]])
                        for gi in GI:
                            p1 = pbank(); mm4(p1, lambda hh: PT[gi][:, hh, :], lambda hh: VB[gi][:, hh, :], [PT[gi], VB[gi]])
                            p2 = pbank(); mm4(p2, lambda hh: KBG[gi][:, hh, :], lambda hh: PT[gi][:, hh, :], [PT[gi], KBG[gi]])
                            c.copy("act", FL(U32[gi]), p1.ap(128, GS * 128), R=p1.rs, W=[U32[gi]])
                            c.copy("dve", FL(WT[gi]), p2.ap(128, GS * 128), R=p2.rs, W=[WT[gi]])
                        cfg.chk(27)
                        for gi in GI:
                            p1 = pbank(); mm4(p1, lambda hh: WT[gi][:, hh, :], lambda hh: Sbf[gi][:, hh, :], [WT[gi], Sbf[gi]])
                            c.tt("dve", FL(VN[gi]), FL(U32[gi]), p1.ap(128, GS * 128), ALU.subtract, R=p1.rs + [U32[gi]], W=[VN[gi]])
                        for gi in GI:
                            o4 = gi * GS
                            p2 = pbank()
                            for hh in range(GS):
                                c.mm(p2.ap(128, 128, hh * 128), QD8[gi][:, hh, :], Sbf[gi][:, hh, :], True, False, R=[QD8[gi], Sbf[gi]], W=p2.rs)
                                c.mm(p2.ap(128, 128, hh * 128), IT8[gi][:, hh, :], VN[gi][:, hh, :], False, True, R=[IT8[gi], VN[gi]], W=p2.rs)
                            p3 = pbank(); mm4(p3, lambda hh: KD[gi][:, hh, :], lambda hh: VN[gi][:, hh, :], [KD[gi], VN[gi]])
                            for hh in range(GS):
                                c.stt("dve", S32[gi][:, hh, :], S32[gi][:, hh, :], col(dec, o4 + hh), p3.ap(128, 128, hh * 128), ALU.mult, ALU.add,
                                      R=p3.rs + [S32[gi], dec], W=[S32[gi]])
                            c.copy("act", FL(Sbf[gi]), FL(S32[gi]), R=[S32[gi]], W=[Sbf[gi]])
                            if d == 0:
                                c.copy("act", FL(O32[gi]), p2.ap(128, GS * 128), R=p2.rs, W=[O32[gi]])
                                c.dma("pool", "st1", og_s[a_:b_, o4 * 128:(o4 + GS) * 128], FL(O32[gi]), R=[O32[gi]], W=[c.res(("og", a_))])
                            else:
                                c.tt("dve", FL(O32[gi]), p2.ap(128, GS * 128), OG[:, hsl(gi), :].rearrange("p a b -> p (a b)"), ALU.add, R=p2.rs + [OG], W=[O32[gi]])
                                for hh in range(GS):
                                    c.act(junk[:], O32[gi][:, hh, :], AF.Square, R=[O32[gi]], W=[junk, ssq], accum=ssq[:, 0, o4 + hh:o4 + hh + 1])
                        if d == 1:
                            c.act(ssq[:, 1, :], ssq[:, 0, :], AF.Ln, R=[ssq, epsT], W=[ssq], bias=epsT[:, 0:1], scale=1.0 / 128)
                            c.act(ssq[:, 2, :], ssq[:, 1, :], AF.Exp, R=[ssq], W=[ssq], scale=-0.5)
                            for gi in GI:
                                o4 = gi * GS
                                for hh in range(GS):
                                    c.act(ON32[gi][:, hh, :], O32[gi][:, hh, :], AF.Identity, R=[O32[gi], ssq], W=[ON32[gi]], scale=ssq[:, 2, o4 + hh:o4 + hh + 1])
                                p1 = pbank()
                                for hh in range(GS):
                                    c.op("pe", lambda e: e.transpose(out=p1.ap(128, 128, hh * 128), in_=ON32[gi][:, hh, :], identity=ident[:]), R=[ON32[gi], ident], W=p1.rs)
                                c.stt("dve", FL(YG[gi]), p1.ap(128, GS * 128), onw[:, 0:1], szc[:, hsl(gi), :].rearrange("p a b -> p (a b)"), ALU.mult, ALU.mult,
                                      R=p1.rs + [onw, szc], W=[YG[gi]])
                                c.dma("pool", "st1", yT_s[(8 + o4) * 128:(8 + GS + o4) * 128, a_:b_].rearrange("(h p) l -> p h l", p=128), YG[gi][:], R=[YG[gi]], W=[c.res("yT")])
                    if not is_s:
                        for h in range(NH):
                            dst = (nsf if d == 0 else nsb)[sq_["idx"], i, h]
                            c.dma("pool", "st2", dst, S32[h // GS][:, h % GS, :], R=[S32[h // GS]], W=[c.res("ns")])
        c.barrier()

    def odd_O1(layer):
        i = layer // 2
        first = False
        with ExitStack() as ph:
            mods = mod_cols(ph, layer, ln_o[i])
            TB = 512
            hTs = [c.sb(f"hT{b}", [128, 16, TB], BF16, ph) for b in range(2)]
            xt = [c.sb(f"xt{b}", [128, D], F32, ph) for b in range(2)]
            ss = c.sb("ss", [128, 4], F32, ph)
            junk = c.sb("junk3", [128, 4, TB], BF16, ph)
            wg = [c.sb(f"wg{b}", [128, 16, 512], BF16, ph) for b in range(2)]
            st32 = [c.sb(f"st32_{b}", [128, TB], F32, ph) for b in range(3)]
            stz = [c.sb(f"stz{b}", [128, TB], BF16, ph) for b in range(2)]
            wv = wb_in_o[i].rearrange("(k p) e -> p k e", p=128)
            nwg = 0; nblk = 0; n32 = 0; nz = 0
            for sq_ in seqs:
                A, Bv = mods[sq_["ci"]]
                L = sq_["L"]
                for b0 in range(0, L, TB):
                    n = min(TB, L - b0)
                    t0 = sq_["t0"] + b0
                    hT = hTs[nblk % 2]; nblk += 1
                    norm_transpose((xt, junk, ss), first, t0, n, A, Bv, hT)
                    for gidx in range(8):
                        c0 = gidx * 512
                        wt = wg[nwg % 2]; nwg += 1
                        for k4 in range(4):
                            c.dma("sp", "ldw", wt[:, k4 * 4:k4 * 4 + 4, :], wv[:, k4 * 4:k4 * 4 + 4, c0:c0 + 512], R=[r_w], W=[wt])
                        for fc in range(4):
                            pb = pbank()
                            for k in range(16):
                                c.mm(pb.ap(128, n), wt[:, k, fc * 128:(fc + 1) * 128], hT[:, k, 0:n], k == 0, k == 15, R=[wt, hT], W=pb.rs)
                            s32 = st32[n32 % 3]; n32 += 1
                            if gidx < 4:
                                if fc % 2 == 0:
                                    c.copy("dve", s32[:, 0:n], pb.ap(128, n), R=pb.rs, W=[s32])
                                else:
                                    c.copy("act", s32[:, 0:n], pb.ap(128, n), R=pb.rs, W=[s32])
                                r0 = c0 + fc * 128
                                c.dma("pool", "st1", pin_s[r0:r0 + 128, t0:t0 + n], s32[:, 0:n], R=[s32], W=[c.res("pin")])
                            else:
                                sz = stz[nz % 2]; nz += 1
                                c.act(sz[:, 0:n], pb.ap(128, n), AF.Silu, R=pb.rs, W=[sz])
                                r0 = c0 - 2048 + fc * 128
                                c.dma("pool", "st1", sz_s[r0:r0 + 128, t0:t0 + n], sz[:, 0:n], R=[sz], W=[c.res("sz")])
        c.barrier()

    def odd_O2(layer):
        i = layer // 2
        with ExitStack() as ph:
            wp = c.sb("wp", [128, 4, 4, 512], BF16, ph)
            for g in range(4):
                c.dma("sp", "ld0", wp[:, g], wb_pool[i, g].rearrange("(k p) e -> p k e", p=128), R=[r_w], W=[wp])
            psc = load_cols(ph, "psc", pool_scale_o[i], 16)
            TB = 512
            HW = 8
            xr = [c.sb(f"pxr{b}", [128, TB + 2 * HW], F32, ph) for b in range(3)]
            sa = [c.sb(f"psa{b}", [128, TB + 2 * HW], F32, ph) for b in range(2)]
            pooled = [[c.sb(f"ppl{b}_{k}", [128, TB], BF16, ph) for k in range(4)] for b in range(2)]
            inv = [c.sb(f"pinv{b}", [128, TB], F32, ph) for b in range(2)]
            szt = [c.sb(f"pszt{b}", [128, TB], BF16, ph) for b in range(2)]
            yst = [c.sb(f"pyst{b}", [128, TB], BF16, ph) for b in range(2)]
            nx = 0; npl = 0; ni = 0; nz = 0
            for sq_ in seqs:
                L = sq_["L"]; ts0 = sq_["t0"]
                pinv_src = k_pinv_s if sq_["kind"] == "s" else k_pinv_p
                for b0 in range(0, L, TB):
                    n = min(TB, L - b0)
                    for g, w in enumerate((2, 4, 8, 16)):
                        iv = inv[ni % 2]; ni += 1
                        c.dma("sp", "ld0", iv[:, 0:n], pinv_src[g, b0:b0 + n].partition_broadcast(128), W=[iv])
                        pl = pooled[npl % 2]; npl += 1
                        for c4 in range(4):
                            x = xr[nx % 3]; nx += 1
                            lo = max(b0 - HW, 0); hi = min(b0 + n + HW, L)
                            if lo > b0 - HW:
                                c.op("pool", lambda e: e.memset(x[:, 0:HW], 0.0), W=[x])
                            if hi < b0 + n + HW:
                                c.op("pool", lambda e: e.memset(x[:, HW + n:HW + n + HW], 0.0), W=[x])
                            r0 = (g * 4 + c4) * 128
                            c.dma("sp", "ldk", x[:, lo - (b0 - HW):hi - (b0 - HW)], pin_s[r0:r0 + 128, ts0 + lo:ts0 + hi], R=[c.res("pin")], W=[x])
                            W_ = n + 2 * HW
                            cur = x; step = 1; length = W_
                            k_ = 0
                            while step < w:
                                dst = sa[k_ % 2]; k_ += 1
                                length -= step
                                c.tt("pool", dst[:, 0:length], cur[:, 0:length], cur[:, step:step + length], ALU.add, R=[cur], W=[dst])
                                cur = dst; step *= 2
                            o0 = HW - w // 2
                            tmpd = sa[k_ % 2]
                            c.tt("dve", tmpd[:, 0:n], cur[:, o0:o0 + n], iv[:, 0:n], ALU.mult, R=[cur, iv], W=[tmpd])
                            c.tt("dve", pl[c4][:, 0:n], tmpd[:, 0:n], x[:, HW:HW + n], ALU.subtract, R=[tmpd, x], W=[pl[c4]])
                        for e_ in range(4):
                            pb = pbank()
                            for c4 in range(4):
                                c.mm(pb.ap(128, n), wp[:, g, c4, e_ * 128:(e_ + 1) * 128], pl[c4][:, 0:n], c4 == 0, c4 == 3, R=[wp, pl[c4]], W=pb.rs)
                            fcx = g * 4 + e_
                            sz = szt[nz % 2]; ys_ = yst[nz % 2]; nz += 1
                            c.dma("sp", "ldq", sz[:, 0:n], sz_s[fcx * 128:(fcx + 1) * 128, ts0 + b0:ts0 + b0 + n], R=[c.res("sz")], W=[sz])
                            c.stt("dve", ys_[:, 0:n], pb.ap(128, n), psc[:, fcx:fcx + 1], sz[:, 0:n], ALU.mult, ALU.mult, R=pb.rs + [psc, sz], W=[ys_])
                            c.dma("pool", "st1", yT_s[fcx * 128:(fcx + 1) * 128, ts0 + b0:ts0 + b0 + n], ys_[:, 0:n], R=[ys_], W=[c.res("yT")])
        c.barrier()

    PH = {}
    PH["even_E1"] = even_E1
    PH["odd_O1"] = odd_O1
    PH["odd_O2"] = odd_O2
    PH["even_E2"] = even_E2
    PH["out_proj"] = out_proj
    PH["even_E3"] = even_E3
    def run_all():
        for layer in range(cfg.depth):
            i = layer // 2
            layer_now[0] = layer
            if layer % 2 == 0:
                even_E1(layer)
                even_E3(layer)
                even_E2(layer)
                out_proj(layer, wb_out_e[i])
            else:
                odd_O1(layer)
                odd_O2(layer)
                out_proj(layer, wb_out_o[i])
        c.barrier()
        nc.all_engine_barrier()
        c.es.close()

    PH["run_all"] = run_all
    return nc, c, PH, locals()


def make_consts(cfg):
    p = np.arange(128)[:, None]
    f = np.arange(128)[None, :]
    neg = lambda m: np.where(m, 0.0, NEG).astype(np.float32)
    bd = ((p // 64) == (f // 64)).astype(np.float32)
    ll = ((p >= 64) & (f < 64)).astype(np.float32)
    ur = ((p < 64) & (f >= 64)).astype(np.float32)
    masks = np.stack([neg(p > f), neg(f >= p), neg(f > p), neg(p >= f), (f > p).astype(np.float32), (p > f).astype(np.float32), bd, ll, ur])
    cum = np.stack([(p <= f).astype(np.float32), (p >= f).astype(np.float32)])
    LS = cfg.ls
    t = np.arange(LS)
    row = (t // 64).astype(np.float32)
    col = (t % 64).astype(np.float32)
    inv_freq = (10000.0 ** (-np.arange(0, 32, 2, dtype=np.float32) / 32)).astype(np.float32)
    cosT = np.zeros((64, LS), np.float32)
    sinS = np.zeros((64, LS), np.float32)
    for q in range(64):
        blk, half, j = q // 32, (q % 32) // 16, q % 16
        ang = (row if blk == 0 else col) * inv_freq[j]
        cosT[q] = np.cos(ang)
        sinS[q] = np.sin(ang) * (-1.0 if half == 0 else 1.0)

    def pinv(L):
        out = np.zeros((4, L), np.float32)
        for gi, w in enumerate((2, 4, 8, 16)):
            lo = np.clip(t[:L] - w // 2, 0, L) if L <= LS else None
            tt_ = np.arange(L)
            lo = np.clip(tt_ - w // 2, 0, L)
            hi = np.clip(tt_ + (w - w // 2), 0, L)
            out[gi] = 1.0 / (hi - lo).astype(np.float32)
        return out

    return dict(k_ident=np.eye(128, dtype=np.float32), k_masks=masks, k_cum=cum, k_rope=np.stack([cosT, sinS]),
                k_pinv_p=pinv(cfg.lp), k_pinv_s=pinv(cfg.ls))


WNAMES = ["ln_e", "mod_w_e", "mod_b_e", "w_in_e", "q_norm_e", "kv_norm_e", "w_uq_e", "w_ukv_e", "conv_e", "o_norm_e", "w_out_e",
          "ln_o", "mod_w_o", "mod_b_o", "w_in_o", "w_pool_o", "pool_scale_o", "w_out_o", "final_norm"]


def core_inputs(cfg, inp, j, consts):
    f = lambda a: np.ascontiguousarray(np.asarray(a, dtype=np.float32))
    m = dict(consts)
    m["xp"] = f(inp["x_prompt"][j * cfg.nps:(j + 1) * cfg.nps]).reshape(cfg.nps * cfg.lp, D)
    m["xs"] = f(inp["x_sample"][j])
    m["cckv"] = f(inp["cache_ckv"][j]); m["ckpe"] = f(inp["cache_kpe"][j])
    m["sfw"] = f(inp["state_fwd"][j]); m["sbw"] = f(inp["state_bwd"][j])
    m["cond"] = f(np.stack([np.asarray(inp["c_ctx"]), np.asarray(inp["c"][j])]))
    for k in WNAMES:
        m[k] = f(inp[k])
    m["a_log_e"] = f(inp["a_log_e"]).reshape(2, 16)
    m["dt_bias_e"] = f(inp["dt_bias_e"]).reshape(2, 16)
    return m


N_CORES = 8


def kernel(**inputs):
    cfg = Cfg(nps=4, lp=256, ls=4096, lc=256, depth=4)
    nc, c, PH, _ = build(cfg)
    PH["run_all"]()
    consts = make_consts(cfg)
    in_maps = [core_inputs(cfg, inputs, j, consts) for j in range(N_CORES)]
    res = run_bass_kernel_spmd(nc, in_maps, core_ids=list(range(N_CORES)))
    rs = res.results
    f = lambda k, shp: np.concatenate([np.asarray(r[k], dtype=np.float32).reshape(shp) for r in rs], axis=0)
    y_prompt = f("yp", (4, 256, D))
    y_sample = f("ys", (1, 4096, D))
    nckv = f("nckv", (4, 2, 256, 512))
    nkpe = f("nkpe", (4, 2, 256, 64))
    nsf = f("nsf", (4, 2, NH, 128, 128))
    nsb = f("nsb", (4, 2, NH, 128, 128))
    return (y_prompt, y_sample, nckv, nkpe, nsf, nsb)
```

```python
import numpy as np
import ml_dtypes
from contextlib import ExitStack
import concourse.bass as bass
import concourse.mybir as mybir
from concourse.bass_utils import run_bass_kernel_spmd

F32 = mybir.dt.float32
BF16 = mybir.dt.bfloat16
AF = mybir.ActivationFunctionType
ALU = mybir.AluOpType
AX = mybir.AxisListType

D = 2048
NH = 8
IN_EVEN = 6240
EPS = 1e-6
NEG = -1.0e9


class Res:
    __slots__ = ("w", "r", "excl")

    def __init__(self, excl=False):
        self.w = None
        self.r = {}
        self.excl = excl


class Tile:
    def __init__(self, t, res=None):
        self.t = t
        self.res = res if res is not None else Res()

    def __getitem__(self, k):
        return self.t[k]


class Ctx:
    def __init__(self, nc):
        self.nc = nc
        self.es = ExitStack()
        self.eng = {"pe": nc.tensor, "act": nc.scalar, "dve": nc.vector, "pool": nc.gpsimd, "sp": nc.sync}
        self.sems = {}
        self.count = {}
        self.seen = {e: {} for e in self.eng}
        for e in ("pe", "act", "dve", "pool"):
            self._sem(e)
        self.resd = {}
        self.n_ins = 0
        self.dead = False
        self.ring_n = {}

    def _sem(self, name):
        if name not in self.sems:
            self.sems[name] = self.es.enter_context(self.nc.semaphore("s_" + name))
            self.count[name] = 0
        return self.sems[name]

    def res(self, key):
        r = self.resd.get(key)
        if r is None:
            r = self.resd[key] = Res()
        return r

    def sb(self, name, shape, dt, stack=None):
        self.n_sb = getattr(self, "n_sb", 0) + 1
        name = f"{name}_u{self.n_sb}"
        t = (stack or self.es).enter_context(self.nc.sbuf_tensor(name, list(shape), dt))
        return Tile(t)

    def _rs(self, x):
        return x.res if isinstance(x, Tile) else x

    def _waits(self, e, R, W):
        need = {}
        for r in R:
            r = self._rs(r)
            if r.w is not None:
                s, v = r.w
                if need.get(s, 0) < v:
                    need[s] = v
            if r.excl:
                for s, v in r.r.items():
                    if s != e and need.get(s, 0) < v:
                        need[s] = v
        for w in W:
            w = self._rs(w)
            if w.w is not None:
                s, v = w.w
                if need.get(s, 0) < v:
                    need[s] = v
            for s, v in w.r.items():
                if need.get(s, 0) < v:
                    need[s] = v
        seen = self.seen[e]
        for s, v in need.items():
            if e == "pe" and s == "pe":
                continue
            if seen.get(s, 0) < v:
                self.eng[e].wait_ge(self.sems[s], v)
                seen[s] = v

    def _mark(self, ticket, R, W):
        s, v = ticket
        for r in R:
            r = self._rs(r)
            if r.r.get(s, 0) < v:
                r.r[s] = v
        for w in W:
            w = self._rs(w)
            w.w = ticket
            w.r = {}

    def op(self, e, emit, R=(), W=()):
        if self.dead:
            return None
        self._waits(e, R, W)
        ins = emit(self.eng[e])
        self.count[e] += 1
        ins.then_inc(self.sems[e], 1)
        self._mark((e, self.count[e]), R, W)
        self.n_ins += 1
        return ins

    RING = 8

    def dma(self, q, stream, out, in_, R=(), W=(), **kw):
        if self.dead:
            return None
        n = self.ring_n.get(stream, 0)
        self.ring_n[stream] = n + 1
        sname = f"{stream}_{n % self.RING}"
        self._sem(sname)
        prev = self.count[sname]
        if prev > 0 and self.seen[q].get(sname, 0) < prev:
            self.eng[q].wait_ge(self.sems[sname], prev)
            self.seen[q][sname] = prev
        self._waits(q, R, W)
        ins = self.eng[q].dma_start(out=out, in_=in_, **kw)
        self.count[sname] += 16
        ins.then_inc(self.sems[sname], 16)
        self._mark((sname, self.count[sname]), R, W)
        self.n_ins += 1

    def barrier(self):
        if self.dead:
            self.dead = False
        for e in self.eng:
            seen = self.seen[e]
            for s, v in self.count.items():
                if e == "pe" and s == "pe":
                    continue
                if v > 0 and seen.get(s, 0) < v:
                    self.eng[e].wait_ge(self.sems[s], v)
                    seen[s] = v

    def mm(self, out, lhsT, rhs, start, stop, R, W):
        return self.op("pe", lambda e: e.matmul(out, lhsT=lhsT, rhs=rhs, start=start, stop=stop), R, W)

    def act(self, out, in_, func, R, W, bias=None, scale=None, accum=None, eng="act"):
        kw = {}
        if bias is not None:
            kw["bias"] = bias
        if scale is not None:
            kw["scale"] = scale
        if accum is not None:
            kw["accum_out"] = accum
        return self.op(eng, lambda e: e.activation(out=out, in_=in_, func=func, **kw), R, W)

    def copy(self, eng, out, in_, R, W):
        if eng == "pool":
            eng = "dve"
        if eng == "act":
            return self.op("act", lambda e: e.copy(out=out, in_=in_), R, W)
        return self.op(eng, lambda e: e.tensor_copy(out=out, in_=in_), R, W)

    def tt(self, eng, out, in0, in1, op, R, W):
        if eng == "pool":
            eng = "dve"
        return self.op(eng, lambda e: e.tensor_tensor(out=out, in0=in0, in1=in1, op=op), R, W)

    def ts(self, eng, out, in0, s1, op0, R, W, s2=None, op1=None):
        if eng == "pool":
            eng = "dve"
        if op1 is None:
            return self.op(eng, lambda e: e.tensor_scalar(out=out, in0=in0, scalar1=s1, scalar2=None, op0=op0), R, W)
        return self.op(eng, lambda e: e.tensor_scalar(out=out, in0=in0, scalar1=s1, scalar2=s2, op0=op0, op1=op1), R, W)

    def stt(self, eng, out, in0, scalar, in1, op0, op1, R, W):
        eng = "dve"
        return self.op(eng, lambda e: e.scalar_tensor_tensor(out=out, in0=in0, scalar=scalar, in1=in1, op0=op0, op1=op1), R, W)


class StopBuild(Exception):
    pass


class Cfg:
    def __init__(self, nps=4, lp=256, ls=4096, lc=256, depth=4, debug=False):
        self.nps, self.lp, self.ls, self.lc, self.depth, self.debug = nps, lp, ls, lc, depth, debug
        self.n_even = (depth + 1) // 2
        self.n_odd = depth // 2
        self.stop = None
        self.neu32 = True
        self.gs = 2

    def chk(self, k):
        if self.stop == k:
            self.ctx.dead = True


def build(cfg):
    nc = bass.Bass("TRN2", target_bir_lowering=False)
    c = Ctx(nc)
    cfg.ctx = c
    NPS, LP, LS, LC = cfg.nps, cfg.lp, cfg.ls, cfg.lc
    NE, NO = cfg.n_even, cfg.n_odd
    NPT = NPS * LP

    def din(name, shape, dt=F32):
        return nc.dram_tensor(name, list(shape), dt, kind="ExternalInput").ap()

    def dout(name, shape, dt=F32):
        return nc.dram_tensor(name, list(shape), dt, kind="ExternalOutput").ap()

    def dscr(name, shape, dt=F32):
        kind = "ExternalOutput" if cfg.debug else "Internal"
        return nc.dram_tensor(name, list(shape), dt, kind=kind).ap()

    xp = din("xp", [NPT, D])
    xs = din("xs", [LS, D])
    cckv = din("cckv", [2, LC, 512])
    ckpe = din("ckpe", [2, LC, 64])
    sfw = din("sfw", [2, NH, 128, 128])
    sbw = din("sbw", [2, NH, 128, 128])
    cond = din("cond", [2, D])
    ln_e = din("ln_e", [2, D]); mod_w_e = din("mod_w_e", [2, D, 3 * D]); mod_b_e = din("mod_b_e", [2, 3 * D])
    w_in_e = din("w_in_e", [2, D, IN_EVEN]); q_norm_e = din("q_norm_e", [2, 512]); kv_norm_e = din("kv_norm_e", [2, 512])
    w_uq_e = din("w_uq_e", [2, 512, 1536]); w_ukv_e = din("w_ukv_e", [2, 512, 2048]); conv_e = din("conv_e", [2, 3, 3072])
    a_log_e = din("a_log_e", [2, 16]); dt_bias_e = din("dt_bias_e", [2, 16]); o_norm_e = din("o_norm_e", [2, 128])
    w_out_e = din("w_out_e", [2, D, D])
    ln_o = din("ln_o", [2, D]); mod_w_o = din("mod_w_o", [2, D, 3 * D]); mod_b_o = din("mod_b_o", [2, 3 * D])
    w_in_o = din("w_in_o", [2, D, 2 * D]); w_pool_o = din("w_pool_o", [2, 4, 512, 512]); pool_scale_o = din("pool_scale_o", [2, D])
    w_out_o = din("w_out_o", [2, D, D]); final_norm = din("final_norm", [D])
    k_ident = din("k_ident", [128, 128])
    k_masks = din("k_masks", [9, 128, 128])
    k_cum = din("k_cum", [2, 128, 128])
    k_rope = din("k_rope", [2, 64, LS])
    k_pinv_p = din("k_pinv_p", [4, LP]); k_pinv_s = din("k_pinv_s", [4, LS])

    yp = dout("yp", [NPT, D]); ys = dout("ys", [LS, D])
    nckv = dout("nckv", [NPS, 2, LP, 512]); nkpe = dout("nkpe", [NPS, 2, LP, 64])
    nsf = dout("nsf", [NPS, 2, NH, 128, 128]); nsb = dout("nsb", [NPS, 2, NH, 128, 128])

    xw_p = dscr("xw_p", [NPT, D]); xw_s = dscr("xw_s", [LS, D])
    modv = dscr("modv", [4, 2, 3 * D])
    wb_in_e = dscr("wb_in_e", [NE, D, IN_EVEN], BF16); wb_uq = dscr("wb_uq", [NE, 512, 1536], BF16)
    wb_ukv = dscr("wb_ukv", [NE, 512, 2048], BF16); wb_out_e = dscr("wb_out_e", [NE, D, D], BF16)
    wb_in_o = dscr("wb_in_o", [max(NO, 1), D, 2 * D], BF16); wb_pool = dscr("wb_pool", [max(NO, 1), 4, 512, 512], BF16)
    wb_out_o = dscr("wb_out_o", [max(NO, 1), D, D], BF16)
    LT = NPT + LS
    LKS = LC + LS
    qn_s = dscr("qn_s", [NH, 128, LT], BF16); qr_s = dscr("qr_s", [NH, 64, LT], BF16)
    kn_p = dscr("kn_p", [NH, 128, NPT], BF16); kr_p = dscr("kr_p", [64, NPT], BF16); v_p = dscr("v_p", [NPT, 1024], BF16)
    kn_x = dscr("kn_x", [NH, 128, LKS], BF16); kr_x = dscr("kr_x", [64, LKS], BF16); v_x = dscr("v_x", [LKS, 1024], BF16)
    qkv_s = dscr("qkv_s", [3072, LT]); ab_s = dscr("ab_s", [LT, 32]); abT_s = dscr("abT_s", [32, LT])
    sz_s = dscr("sz_s", [D, LT], BF16); yT_s = dscr("yT_s", [D, LT], BF16)
    gq_s = dscr("gq_s", [NH, 128, LT], BF16); gk_s = dscr("gk_s", [NH, 128, LT], BF16); gv_s = dscr("gv_s", [NH, 128, LT], BF16)
    gkb_s = dscr("gkb_s", [2, NH, 128, LT], BF16)
    og_s = dscr("og_s", [LT, 1024])
    pin_s = dscr("pin_s", [D, LT])

    seqs = [dict(t0=i * LP, L=LP, ci=0, kind="p", idx=i) for i in range(NPS)] + [dict(t0=NPT, L=LS, ci=1, kind="s", idx=0)]

    def xrows(src_first, t0, n):
        if t0 < NPT:
            return (xp if src_first else xw_p)[t0:t0 + n, :]
        return (xs if src_first else xw_s)[t0 - NPT:t0 - NPT + n, :]

    dbg_n = [0]
    layer_now = [0]

    def dbg(name, ap, shape, dt=F32, R=()):
        if not cfg.debug or layer_now[0] != 0:
            return
        t = nc.dram_tensor("dbg_" + name, list(shape), dt, kind="ExternalOutput").ap()
        c.dma("pool", "st2", t, ap, R=list(R))

    ident = c.sb("ident", [128, 128], F32)
    identb = c.sb("identb", [128, 128], BF16)
    ones32 = c.sb("ones32", [128, 128], F32)
    onesb = c.sb("onesb", [128, 128], BF16)
    epsT = c.sb("epsT", [128, 1], F32)
    oneT = c.sb("oneT", [128, 1], F32)
    c.dma("sp", "ld0", ident[:], k_ident, W=[ident])
    c.op("pool", lambda e: e.memset(ones32[:], 1.0), W=[ones32])
    c.op("pool", lambda e: e.memset(onesb[:], 1.0), W=[onesb])
    c.op("pool", lambda e: e.memset(epsT[:], EPS), W=[epsT])
    c.op("pool", lambda e: e.memset(oneT[:], 1.0), W=[oneT])
    c.copy("dve", identb[:], ident[:], R=[ident], W=[identb])

    banks = [c.es.enter_context(nc.psum_tensor(f"ps{i}", [128, 512], F32)) for i in range(8)]
    bank_res = [Res(excl=True) for _ in range(8)]

    class PS:
        def __init__(self, b, c0, n):
            self.b, self.c0, self.n = b, c0, n
            self.rs = [bank_res[b]]

        def ap(self, p=128, n=None, off=0):
            n = self.n if n is None else n
            return banks[self.b][0:p, self.c0 + off:self.c0 + off + n]

    st = {"slot": 0}

    def pbank(b=None):
        if b is None:
            s = (st["slot"] + 3) // 4 * 4 % 32
            st["slot"] = (s + 4) % 32
            b = s // 4
        return PS(b, 0, 512)

    def palign():
        st["slot"] = (st["slot"] + 3) // 4 * 4 % 32

    def pslots4():
        palign()
        return [pslot() for _ in range(4)]

    def pslot():
        s = st["slot"]
        st["slot"] = (s + 1) % 32
        return PS(s // 4, (s % 4) * 128, 128)

    r_w = c.res("wcast")

    def wcast(dst, src, rows_per=512):
        n = src.shape[0]
        for r0 in range(0, n, rows_per):
            r1 = min(n, r0 + rows_per)
            c.dma("pool", "wc", dst[r0:r1], src[r0:r1], W=[r_w])

    for i in range(NE):
        wcast(wb_in_e[i], w_in_e[i]); wcast(wb_uq[i], w_uq_e[i]); wcast(wb_ukv[i], w_ukv_e[i]); wcast(wb_out_e[i], w_out_e[i])
    for i in range(NO):
        wcast(wb_in_o[i], w_in_o[i]); wcast(wb_out_o[i], w_out_o[i])
        for g in range(4):
            wcast(wb_pool[i, g], w_pool_o[i, g])

    r_modv = c.res("modv")
    with ExitStack() as ph:
        condT = c.sb("condT", [128, 16, 2], F32, ph)
        scT = c.sb("scT", [128, 16, 2], F32, ph)
        sgT = c.sb("sgT", [128, 16, 2], F32, ph)
        for ci_ in range(2):
            c.dma("sp", "ld0", condT[:, :, ci_], cond[ci_].rearrange("(k p) -> p k", p=128), W=[condT], allow_slow_non_contiguous=True)
        c.act(scT[:], condT[:], AF.Exp, R=[condT], W=[scT], scale=-1.0)
        c.ts("dve", scT[:], scT[:], 1.0, ALU.add, R=[scT], W=[scT])
        c.op("dve", lambda e: e.reciprocal(out=scT[:], in_=scT[:]), R=[scT], W=[scT])
        c.tt("dve", sgT[:], condT[:], scT[:], ALU.mult, R=[condT, scT], W=[sgT])
        wts = [c.sb(f"mw{i}", [128, 4, 512], F32, ph) for i in range(3)]
        mb = c.sb("mb", [2, 3 * D], F32, ph)
        mo = c.sb("mo", [2, 3 * D], F32, ph)
        nld = 0
        for layer in range(cfg.depth):
            i = layer // 2
            mw = (mod_w_e if layer % 2 == 0 else mod_w_o)[i]
            mbv = (mod_b_e if layer % 2 == 0 else mod_b_o)[i]
            c.dma("sp", "ld0", mb[:], mbv.partition_broadcast(2), W=[mb], R=[])
            for n in range(12):
                pb = pbank()
                for kg in range(4):
                    wt = wts[nld % 3]; nld += 1
                    c.dma("sp", "ldw", wt[:], mw[kg * 512:(kg + 1) * 512, n * 512:(n + 1) * 512].rearrange("(k p) e -> p k e", p=128), W=[wt])
                    for kk in range(4):
                        k = kg * 4 + kk
                        c.mm(pb.ap(2), sgT[:, k, :], wt[:, kk, :], k == 0, k == 15, R=[sgT, wt], W=pb.rs)
                c.tt("dve", mo[:, n * 512:(n + 1) * 512], pb.ap(2), mb[:, n * 512:(n + 1) * 512], ALU.add, R=pb.rs + [mb], W=[mo])
            c.dma("sp", "st0", modv[layer], mo[:], R=[mo], W=[r_modv])
    c.barrier()

    def load_cols(ph, name, vec, nchunk, q="sp"):
        t = c.sb(name, [128, nchunk], F32, ph)
        c.dma(q, "ld0", t[:], vec.rearrange("(k p) -> p k", p=128), W=[t], R=[r_modv], allow_slow_non_contiguous=True)
        return t

    def norm_transpose(ph_tiles, src_first, t0, n, A, B, hT):
        xt, junk, ss = ph_tiles
        for j in range(n // 128):
            xn = xt[j % 2]
            c.dma("sp", "ldx", xn[:], xrows(src_first, t0 + j * 128, 128), R=[c.res(("x", (t0 + j * 128) // 128))], W=[xn])
            c.act(junk[:].rearrange("p a b -> p (a b)")[:, 0:D], xn[:], AF.Square, R=[xn], W=[junk, ss], accum=ss[:, 0:1])
            c.act(ss[:, 1:2], ss[:, 0:1], AF.Ln, R=[ss, epsT], W=[ss], bias=epsT[:, 0:1], scale=1.0 / D)
            c.act(ss[:, 2:3], ss[:, 1:2], AF.Exp, R=[ss], W=[ss], scale=-0.5)
            c.ts("dve", xn[:], xn[:], ss[:, 2:3], ALU.mult, R=[xn, ss], W=[xn])
            for kg in range(4):
                pb = pbank()
                for kk in range(4):
                    k = kg * 4 + kk
                    c.op("pe", lambda e: e.transpose(out=pb.ap(128, 128, kk * 128), in_=xn[:, k * 128:(k + 1) * 128], identity=ident[:]), R=[xn, ident], W=pb.rs)
                for kk in range(4):
                    k = kg * 4 + kk
                    eng = "dve" if kk % 2 == 0 else "pool"
                    if eng == "pool":
                        c.act(hT[:, k, j * 128:(j + 1) * 128], pb.ap(128, 128, kk * 128), AF.Identity, R=pb.rs + [A, B], W=[hT],
                              bias=B[:, k:k + 1], scale=A[:, k:k + 1])
                    else:
                        c.ts("dve", hT[:, k, j * 128:(j + 1) * 128], pb.ap(128, 128, kk * 128), A[:, k:k + 1], ALU.mult, R=pb.rs + [A, B], W=[hT],
                             s2=B[:, k:k + 1], op1=ALU.add)

    def mod_cols(ph, layer, lnw_vec):
        lnw = load_cols(ph, f"lnw{layer}", lnw_vec, 16)
        outs = []
        for ci in range(2):
            sh = load_cols(ph, f"sh{layer}_{ci}", modv[layer, ci, 0:D], 16)
            sc = load_cols(ph, f"sc{layer}_{ci}", modv[layer, ci, D:2 * D], 16)
            A = c.sb(f"A{layer}_{ci}", [128, 16], F32, ph)
            c.stt("dve", A[:], sc[:], 1.0, lnw[:], ALU.add, ALU.mult, R=[sc, lnw], W=[A])
            outs.append((A, sh))
        return outs

    def rstd_from_ps(ps_sum, n, inv_n, out_t, tmp_t):
        c.act(tmp_t[:, 0:n], ps_sum.ap(128, n), AF.Ln, R=ps_sum.rs + [epsT], W=[tmp_t], bias=epsT[:, 0:1], scale=inv_n)
        c.act(out_t[:, 0:n], tmp_t[:, 0:n], AF.Exp, R=[tmp_t], W=[out_t], scale=-0.5)

    def even_E1(layer):
        i = layer // 2
        first = layer == 0
        with ExitStack() as ph:
            mods = mod_cols(ph, layer, ln_e[i])
            qg = load_cols(ph, "qg", q_norm_e[i], 4)
            kg_ = load_cols(ph, "kvg", kv_norm_e[i], 4)
            wuq = c.sb("wuq", [128, 4, 1536], BF16, ph)
            wuqp = c.sb("wuqp", [128, 4, 8, 64], BF16, ph)
            wukk = c.sb("wukk", [128, 4, 8, 128], BF16, ph)
            wukv = c.sb("wukv", [128, 4, 8, 128], BF16, ph)
            c.dma("sp", "ld0", wuq[:], wb_uq[i].rearrange("(k p) e -> p k e", p=128), R=[r_w], W=[wuq])
            uqv = wb_uq[i].rearrange("(k p) (h e) -> p k h e", p=128, e=192)
            for blk in range(2):
                for half in range(2):
                    src = uqv[:, :, :, 128 + blk * 32 + (1 - half) * 16:128 + blk * 32 + (1 - half) * 16 + 16]
                    for k in range(4):
                        c.dma("sp", "ld0", wuqp[:, k, :, blk * 32 + half * 16:blk * 32 + half * 16 + 16], src[:, k], R=[r_w], W=[wuqp],
                              allow_slow_non_contiguous=True)
            ukvv = wb_ukv[i].rearrange("(k p) (h t e) -> p k h t e", p=128, t=2, e=128)
            for k in range(4):
                c.dma("sp", "ld0", wukk[:, k], ukvv[:, k, :, 0, :], R=[r_w], W=[wukk])
                c.dma("sp", "ld0", wukv[:, k], ukvv[:, k, :, 1, :], R=[r_w], W=[wukv])
            TB = 512
            hTs = [c.sb(f"hT{b}", [128, 16, TB], BF16, ph) for b in range(2)]
            xt = [c.sb(f"xt{b}", [128, D], F32, ph) for b in range(2)]
            ss = c.sb("ss", [128, 4], F32, ph)
            wg = [c.sb(f"wg{b}", [128, 16, 512], BF16, ph) for b in range(2)]
            wsm = c.sb("wsm", [128, 16, 128 + 32], BF16, ph)
            raw0 = c.sb("raw0", [128, 4, TB], F32, ph)
            raw = [raw0, raw0]
            sq = c.sb("sq", [128, 4, TB], BF16, ph)
            rstd = c.sb("rstd", [128, TB], F32, ph)
            tmpn = c.sb("tmpn", [128, TB], F32, ph)
            cqn = c.sb("cqn", [128, 4, TB], BF16, ph)
            ckn = c.sb("ckn", [128, 4, TB], BF16, ph)
            ckn32 = raw0
            stq = c.sb("stq", [128, 2, TB], BF16, ph)
            stqr = c.sb("stqr", [64, 2, TB], BF16, ph)
            stk = c.sb("stk", [128, 2, TB], BF16, ph)
            stv = [c.sb(f"stv{b}", [128, 1024], BF16, ph) for b in range(2)]
            stkr = c.sb("stkr", [64, TB], BF16, ph)
            kpe32 = c.sb("kpe32", [64, TB], F32, ph)
            st32 = [c.sb(f"st32_{b}", [128, TB], F32, ph) for b in range(3)]
            stz = [c.sb(f"stz{b}", [128, TB], BF16, ph) for b in range(2)]
            stab = c.sb("stab", [128, 32], F32, ph)
            otok = [c.sb(f"otok{b}", [128, 576], F32, ph) for b in range(2)]
            ropec = c.sb("ropec", [64, TB], F32, ph)
            ropes = c.sb("ropes", [64, TB], F32, ph)
            rt1 = c.sb("rt1", [64, TB], F32, ph)
            rt2 = c.sb("rt2", [64, TB], F32, ph)
            wv = wb_in_e[i].rearrange("(k p) e -> p k e", p=128)
            for k4 in range(4):
                c.dma("sp", "ld0", wsm[:, k4 * 4:k4 * 4 + 4, 0:64], wv[:, k4 * 4:k4 * 4 + 4, 1024:1088], R=[r_w], W=[wsm])
            for blk in range(2):
                for half in range(2):
                    c0 = 1024 + blk * 32 + (1 - half) * 16
                    for k4 in range(4):
                        c.dma("sp", "ld0", wsm[:, k4 * 4:k4 * 4 + 4, 64 + blk * 32 + half * 16:64 + blk * 32 + half * 16 + 16],
                              wv[:, k4 * 4:k4 * 4 + 4, c0:c0 + 16], R=[r_w], W=[wsm], allow_slow_non_contiguous=True)
            for k4 in range(4):
                c.dma("sp", "ld0", wsm[:, k4 * 4:k4 * 4 + 4, 128:160], wv[:, k4 * 4:k4 * 4 + 4, 4160:4192], R=[r_w], W=[wsm])
            groups = [(0, 512, "cq"), (512, 512, "ckv")] + [(1088 + g * 512, 512, "qkv") for g in range(6)] + \
                     [(4192 + g * 512, 512, "z") for g in range(4)]
            nwg = [0]
            nblk = 0
            nst = [0, 0, 0]

            def sumsq_norm(rawt, n, gcol, outb, out32):
                pb = pbank()
                for k in range(4):
                    c.mm(pb.ap(128, n), onesb[:], sq[:, k, 0:n], k == 0, k == 3, R=[onesb, sq], W=pb.rs)
                rstd_from_ps(pb, n, 1.0 / 512, rstd, tmpn)
                for k in range(4):
                    c.stt("dve", outb[:, k, 0:n], rawt[:, k, 0:n], gcol[:, k:k + 1], rstd[:, 0:n], ALU.mult, ALU.mult, R=[rawt, gcol, rstd], W=[outb])
                    if out32 is not None:
                        c.stt("pool", out32[:, k, 0:n], rawt[:, k, 0:n], gcol[:, k:k + 1], rstd[:, 0:n], ALU.mult, ALU.mult, R=[rawt, gcol, rstd, outb], W=[out32])

            def kv_up(src_bf, n, kn_dst, v_dst_rows):
                for h in range(NH):
                    pb = pbank()
                    for k in range(4):
                        c.mm(pb.ap(128, n), wukk[:, k, h, :], src_bf[:, k, 0:n], k == 0, k == 3, R=[wukk, src_bf], W=pb.rs)
                    if h % 2 == 0:
                        c.copy("dve", stk[:, h % 2, 0:n], pb.ap(128, n), R=pb.rs, W=[stk])
                    else:
                        c.copy("act", stk[:, h % 2, 0:n], pb.ap(128, n), R=pb.rs, W=[stk])
                    c.dma("pool", "st1", kn_dst(h), stk[:, h % 2, 0:n], R=[stk], W=[c.res("kn")])
                for j in range(n // 128):
                    sv = stv[nst[0] % 2]; nst[0] += 1
                    for half in range(2):
                        pb = pbank()
                        for k in range(4):
                            c.mm(pb.ap(128, 512), src_bf[:, k, j * 128:(j + 1) * 128], wukv[:, k, half * 4:(half + 1) * 4, :], k == 0, k == 3,
                                 R=[src_bf, wukv], W=pb.rs)
                        if half == 0:
                            c.copy("dve", sv[:, 0:512], pb.ap(128, 512), R=pb.rs, W=[sv])
                        else:
                            c.copy("act", sv[:, 512:1024], pb.ap(128, 512), R=pb.rs, W=[sv])
                    c.dma("pool", "st1", v_dst_rows(j), sv[:], R=[sv], W=[c.res("v")])

            for sq_ in seqs:
                A, Bv = mods[sq_["ci"]]
                is_s = sq_["kind"] == "s"
                L = sq_["L"]
                for b0 in range(0, L, TB):
                    n = min(TB, L - b0)
                    t0 = sq_["t0"] + b0
                    hT = hTs[nblk % 2]; nblk += 1
                    cfg.chk(1 + (10 if is_s else 0))
                    norm_transpose((xt, sq, ss), first, t0, n, A, Bv, hT)
                    cfg.chk(2 + (10 if is_s else 0))
                    if is_s:
                        c.dma("sp", "ld0", ropec[:, 0:n], k_rope[0, :, b0:b0 + n], W=[ropec])
                        c.dma("sp", "ld0", ropes[:, 0:n], k_rope[1, :, b0:b0 + n], W=[ropes])
                    for (c0, ncol, kind) in groups:
                        wt = wg[nwg[0] % 2]; nwg[0] += 1
                        for k4 in range(4):
                            c.dma("sp", "ldw", wt[:, k4 * 4:k4 * 4 + 4, 0:ncol], wv[:, k4 * 4:k4 * 4 + 4, c0:c0 + ncol], R=[r_w], W=[wt])
                        for fc in range(ncol // 128):
                            pb = pbank()
                            for k in range(16):
                                c.mm(pb.ap(128, n), wt[:, k, fc * 128:(fc + 1) * 128], hT[:, k, 0:n], k == 0, k == 15, R=[wt, hT], W=pb.rs)
                            if kind in ("cq", "ckv"):
                                rw = raw[0 if kind == "cq" else 1]
                                c.copy("dve", rw[:, fc, 0:n], pb.ap(128, n), R=pb.rs, W=[rw])
                                c.act(sq[:, fc, 0:n], rw[:, fc, 0:n], AF.Square, R=[rw], W=[sq])
                            elif kind == "qkv":
                                s32 = st32[nst[1] % 3]; nst[1] += 1
                                if fc % 2 == 0:
                                    c.copy("dve", s32[:, 0:n], pb.ap(128, n), R=pb.rs, W=[s32])
                                else:
                                    c.copy("act", s32[:, 0:n], pb.ap(128, n), R=pb.rs, W=[s32])
                                r0 = c0 - 1088 + fc * 128
                                c.dma("pool", "st1", qkv_s[r0:r0 + 128, t0:t0 + n], s32[:, 0:n], R=[s32], W=[c.res("qkv")])
                            else:
                                sz = stz[nst[2] % 2]; nst[2] += 1
                                c.act(sz[:, 0:n], pb.ap(128, n), AF.Silu, R=pb.rs, W=[sz])
                                r0 = c0 - 4192 + fc * 128
                                c.dma("pool", "st1", sz_s[r0:r0 + 128, t0:t0 + n], sz[:, 0:n], R=[sz], W=[c.res("sz")])
                        cfg.chk(3 + (10 if is_s else 0))
                        if kind == "cq":
                            sumsq_norm(raw[0], n, qg, cqn, None)
                            for h in range(NH):
                                pb = pbank()
                                for k in range(4):
                                    c.mm(pb.ap(128, n), wuq[:, k, h * 192:h * 192 + 128], cqn[:, k, 0:n], k == 0, k == 3, R=[wuq, cqn], W=pb.rs)
                                c.copy("act", stq[:, h % 2, 0:n], pb.ap(128, n), R=pb.rs, W=[stq])
                                c.dma("pool", "st1", qn_s[h, :, t0:t0 + n], stq[:, h % 2, 0:n], R=[stq], W=[c.res("qn")])
                                pr = pbank()
                                for k in range(4):
                                    c.mm(pr.ap(64, n), wuq[:, k, h * 192 + 128:h * 192 + 192], cqn[:, k, 0:n], k == 0, k == 3, R=[wuq, cqn], W=pr.rs)
                                if is_s:
                                    pr2 = pbank()
                                    for k in range(4):
                                        c.mm(pr2.ap(64, n), wuqp[:, k, h, :], cqn[:, k, 0:n], k == 0, k == 3, R=[wuqp, cqn], W=pr2.rs)
                                    c.tt("dve", rt1[:, 0:n], pr.ap(64, n), ropec[:, 0:n], ALU.mult, R=pr.rs + [ropec], W=[rt1])
                                    c.tt("dve", rt2[:, 0:n], pr2.ap(64, n), ropes[:, 0:n], ALU.mult, R=pr2.rs + [ropes], W=[rt2])
                                    c.tt("pool", stqr[:, h % 2, 0:n], rt1[:, 0:n], rt2[:, 0:n], ALU.add, R=[rt1, rt2], W=[stqr])
                                else:
                                    c.copy("dve", stqr[:, h % 2, 0:n], pr.ap(64, n), R=pr.rs, W=[stqr])
                                c.dma("pool", "st1", qr_s[h, :, t0:t0 + n], stqr[:, h % 2, 0:n], R=[stqr], W=[c.res("qr")])
                        if kind == "ckv":
                            cfg.chk(4 + (10 if is_s else 0))
                            sumsq_norm(raw[1], n, kg_, ckn, None if is_s else ckn32)
                            if is_s:
                                kv_up(ckn, n, lambda h: kn_x[h, :, LC + b0:LC + b0 + n],
                                      lambda j: v_x[LC + b0 + j * 128:LC + b0 + (j + 1) * 128, :])
                            else:
                                kv_up(ckn, n, lambda h: kn_p[h, :, t0:t0 + n],
                                      lambda j: v_p[t0 + j * 128:t0 + (j + 1) * 128, :])
                    cfg.chk(5 + (10 if is_s else 0))
                    pk = pbank()
                    for k in range(16):
                        c.mm(pk.ap(64, n), wsm[:, k, 0:64], hT[:, k, 0:n], k == 0, k == 15, R=[wsm, hT], W=pk.rs)
                    if is_s:
                        pk2 = pbank()
                        for k in range(16):
                            c.mm(pk2.ap(64, n), wsm[:, k, 64:128], hT[:, k, 0:n], k == 0, k == 15, R=[wsm, hT], W=pk2.rs)
                        c.tt("dve", rt1[:, 0:n], pk.ap(64, n), ropec[:, 0:n], ALU.mult, R=pk.rs + [ropec], W=[rt1])
                        c.tt("dve", rt2[:, 0:n], pk2.ap(64, n), ropes[:, 0:n], ALU.mult, R=pk2.rs + [ropes], W=[rt2])
                        c.tt("pool", stkr[:, 0:n], rt1[:, 0:n], rt2[:, 0:n], ALU.add, R=[rt1, rt2], W=[stkr])
                        c.dma("pool", "st1", kr_x[:, LC + b0:LC + b0 + n], stkr[:, 0:n], R=[stkr], W=[c.res("kr")])
                    else:
                        c.copy("dve", kpe32[:, 0:n], pk.ap(64, n), R=pk.rs, W=[kpe32])
                        c.copy("act", stkr[:, 0:n], kpe32[:, 0:n], R=[kpe32], W=[stkr])
                        c.dma("pool", "st1", kr_p[:, t0:t0 + n], stkr[:, 0:n], R=[stkr], W=[c.res("kr")])
                        for j in range(n // 128):
                            ot = otok[j % 2]
                            pb = pbank()
                            for k in range(4):
                                c.op("pe", lambda e: e.transpose(out=pb.ap(128, 128, k * 128), in_=ckn32[:, k, j * 128:(j + 1) * 128], identity=ident[:]),
                                     R=[ckn32, ident], W=pb.rs)
                            c.copy("dve", ot[:, 0:512], pb.ap(128, 512), R=pb.rs, W=[ot])
                            pb2 = pslot()
                            c.op("pe", lambda e: e.transpose(out=pb2.ap(128, 64), in_=kpe32[:, j * 128:(j + 1) * 128], identity=ident[0:64, 0:64]),
                                 R=[kpe32, ident], W=pb2.rs)
                            c.copy("act", ot[:, 512:576], pb2.ap(128, 64), R=pb2.rs, W=[ot])
                            l0 = b0 + j * 128
                            c.dma("pool", "st2", nckv[sq_["idx"], i, l0:l0 + 128, :], ot[:, 0:512], R=[ot], W=[c.res("nckv")])
                            c.dma("pool", "st2", nkpe[sq_["idx"], i, l0:l0 + 128, :], ot[:, 512:576], R=[ot], W=[c.res("nkpe")])
                    cfg.chk(6 + (10 if is_s else 0))
                    pa = pbank()
                    for k in range(16):
                        c.mm(pa.ap(32, n), wsm[:, k, 128:160], hT[:, k, 0:n], k == 0, k == 15, R=[wsm, hT], W=pa.rs)
                    s32 = st32[nst[1] % 3]; nst[1] += 1
                    c.copy("dve", s32[0:32, 0:n], pa.ap(32, n), R=pa.rs, W=[s32])
                    c.dma("pool", "st1", abT_s[:, t0:t0 + n], s32[0:32, 0:n], R=[s32], W=[c.res("abT")])
                    cfg.chk(8 + (10 if is_s else 0))
            cfg.chk(7)
            ctx32 = c.sb("ctx32", [128, 576], F32, ph)
            for j in range(LC // 128):
                c.dma("sp", "ld0", ctx32[:, 0:512], cckv[i, j * 128:(j + 1) * 128, :], W=[ctx32])
                c.dma("sp", "ld0", ctx32[:, 512:576], ckpe[i, j * 128:(j + 1) * 128, :], W=[ctx32])
                pb = pbank()
                for k in range(4):
                    c.op("pe", lambda e: e.transpose(out=pb.ap(128, 128, k * 128), in_=ctx32[:, k * 128:(k + 1) * 128], identity=ident[:]),
                         R=[ctx32, ident], W=pb.rs)
                for k in range(4):
                    c.copy("dve", ckn[:, k, j * 128:(j + 1) * 128], pb.ap(128, 128, k * 128), R=pb.rs, W=[ckn])
                pb2 = pslot()
                c.op("pe", lambda e: e.transpose(out=pb2.ap(64, 128), in_=ctx32[:, 512:576], identity=ident[:]), R=[ctx32, ident], W=pb2.rs)
                c.copy("act", stkr[:, j * 128:(j + 1) * 128], pb2.ap(64, 128), R=pb2.rs, W=[stkr])
            c.dma("pool", "st1", kr_x[:, 0:LC], stkr[:, 0:LC], R=[stkr], W=[c.res("kr")])
            kv_up(ckn, LC, lambda h: kn_x[h, :, 0:LC], lambda j: v_x[j * 128:(j + 1) * 128, :])
        c.barrier()

    def attn(layer, ph):
        if True:
            LKMAX = LC + LS
            kn_t = [c.sb(f"kn{b}", [128, LKMAX], BF16, ph) for b in range(2)]
            v_t = [c.sb(f"vv{b}", [128, LKMAX // 128, 128], BF16, ph) for b in range(2)]
            kr_t = c.sb("krt", [64, LKMAX], BF16, ph)
            qn_t = [c.sb(f"qnt{b}", [128, 512], BF16, ph) for b in range(2)]
            qr_t = [c.sb(f"qrt{b}", [64, 512], BF16, ph) for b in range(2)]
            pT = [c.sb(f"pT{b}", [128, 512], BF16, ph) for b in range(4)]
            szt = [c.sb(f"szt{b}", [128, 512], BF16, ph) for b in range(2)]
            yst = [c.sb(f"yst{b}", [128, 512], BF16, ph) for b in range(2)]
            rden = c.sb("rden", [128, 512], F32, ph)
            tmpo = c.sb("tmpo", [128, 512], F32, ph)
            scale = 192.0 ** -0.5
            nh = 0; nq = 0; npt_box = [0]
            for sq_ in seqs:
                is_s = sq_["kind"] == "s"
                L = sq_["L"]; t0 = sq_["t0"]
                Lk = LC + L if is_s else L
                nkc = Lk // 128
                QB = min(512, L)
                c.dma("sp", "ld0", kr_t[:, 0:Lk], (kr_x[:, 0:Lk] if is_s else kr_p[:, t0:t0 + Lk]), R=[c.res("kr")], W=[kr_t])
                for h in range(NH):
                    knt = kn_t[nh % 2]; vt = v_t[nh % 2]; nh += 1
                    c.dma("sp", "ldk", knt[:, 0:Lk], (kn_x[h, :, 0:Lk] if is_s else kn_p[h, :, t0:t0 + Lk]), R=[c.res("kn")], W=[knt])
                    vsrc = (v_x[0:Lk, h * 128:(h + 1) * 128] if is_s else v_p[t0:t0 + Lk, h * 128:(h + 1) * 128])
                    c.dma("sp", "ldk", vt[:, 0:nkc, :], vsrc.rearrange("(n p) e -> p n e", p=128), R=[c.res("v")], W=[vt])
                    for q0 in range(0, L, QB):
                        qnt = qn_t[nq % 2]; qrt = qr_t[nq % 2]; szT = szt[nq % 2]; yT = yst[nq % 2]
                        a, b = t0 + q0, t0 + q0 + QB
                        c.dma("sp", "ldq", qnt[:, 0:QB], qn_s[h, :, a:b], R=[c.res("qn")], W=[qnt])
                        c.dma("sp", "ldq", qrt[:, 0:QB], qr_s[h, :, a:b], R=[c.res("qr")], W=[qrt])
                        c.dma("sp", "ldq", szT[:, 0:QB], sz_s[h * 128:(h + 1) * 128, a:b], R=[c.res("sz")], W=[szT])
                        po = pbank(4 + nq % 2); pd = pbank(6 + nq % 2); nq += 1
                        def scores(kc):
                            nonlocal_npt = npt_box[0]; npt_box[0] += 1
                            psb = pbank(nonlocal_npt % 4); p_t = pT[nonlocal_npt % 4]
                            c.mm(psb.ap(128, QB), knt[:, kc * 128:(kc + 1) * 128], qnt[:, 0:QB], True, False, R=[knt, qnt], W=psb.rs)
                            c.mm(psb.ap(128, QB), kr_t[:, kc * 128:(kc + 1) * 128], qrt[:, 0:QB], False, True, R=[kr_t, qrt], W=psb.rs)
                            c.act(p_t[:, 0:QB], psb.ap(128, QB), AF.Exp, R=psb.rs, W=[p_t], scale=scale)
                            return p_t
                        LA = 3
                        pend = [scores(j_) for j_ in range(min(LA, nkc))]
                        for kc in range(nkc):
                            p_t = pend.pop(0)
                            if kc + LA < nkc:
                                pend.append(scores(kc + LA))
                            c.mm(po.ap(128, QB), vt[:, kc, :], p_t[:, 0:QB], kc == 0, kc == nkc - 1, R=[vt, p_t], W=po.rs)
                            c.mm(pd.ap(128, QB), onesb[:], p_t[:, 0:QB], kc == 0, kc == nkc - 1, R=[onesb, p_t], W=pd.rs)
                        c.op("dve", lambda e: e.reciprocal(out=rden[:, 0:QB], in_=pd.ap(128, QB)), R=pd.rs, W=[rden])
                        c.tt("dve", tmpo[:, 0:QB], po.ap(128, QB), rden[:, 0:QB], ALU.mult, R=po.rs + [rden], W=[tmpo])
                        c.tt("pool", yT[:, 0:QB], tmpo[:, 0:QB], szT[:, 0:QB], ALU.mult, R=[tmpo, szT], W=[yT])
                        c.dma("pool", "st1", yT_s[h * 128:(h + 1) * 128, a:b], yT[:, 0:QB], R=[yT], W=[c.res("yT")])
                        yield

    def even_E3(layer, with_pre=False):
        with ExitStack() as ph:
            gens = [attn(layer, ph)] + ([gdn_pre(layer, ph)] if with_pre else [])
            while gens:
                for g_ in list(gens):
                    try:
                        next(g_)
                    except StopIteration:
                        gens.remove(g_)
        c.barrier()

    def out_proj(layer, wsrc):
        first = layer == 0
        last = layer == cfg.depth - 1
        with ExitStack() as ph:
            wo = c.sb("wo", [128, 16, D], BF16, ph)
            wov = wsrc.rearrange("(k p) e -> p k e", p=128)
            for k4 in range(8):
                c.dma("sp", "ld0", wo[:, k4 * 2:k4 * 2 + 2, :], wov[:, k4 * 2:k4 * 2 + 2, :], R=[r_w], W=[wo])
            gts = []
            for ci in range(2):
                g = c.sb(f"gt{ci}", [128, D], F32, ph)
                c.dma("sp", "ld0", g[:], modv[layer, ci, 2 * D:3 * D].partition_broadcast(128), R=[r_modv], W=[g])
                gts.append(g)
            fn = None
            if last:
                fn = c.sb("fnw", [128, D], F32, ph)
                c.dma("sp", "ld0", fn[:], final_norm.partition_broadcast(128), W=[fn])
            yts = [c.sb(f"yt{b}", [128, 16, 512], BF16, ph) for b in range(2)]
            xts = [c.sb(f"xo{b}", [128, D], F32, ph) for b in range(2)]
            tms = [c.sb(f"tm{b}", [128, 512], F32, ph) for b in range(2)]
            junk = c.sb("junkb", [128, D], BF16, ph)
            ss = c.sb("ss4", [128, 4], F32, ph)
            nb = 0; nt = 0; ntm = 0
            for sq_ in seqs:
                L = sq_["L"]; g = gts[sq_["ci"]]
                for b0 in range(0, L, 512):
                    n = min(512, L - b0)
                    t0 = sq_["t0"] + b0
                    yt = yts[nb % 2]; nb += 1
                    for k4 in range(4):
                        c.dma("sp", "ldk", yt[:, k4 * 4:k4 * 4 + 4, 0:n], yT_s.rearrange("(k p) l -> p k l", p=128)[:, k4 * 4:k4 * 4 + 4, t0:t0 + n],
                              R=[c.res("yT")], W=[yt])
                    for j in range(n // 128):
                        xt = xts[nt % 2]; nt += 1
                        tj = t0 + j * 128
                        xres = c.res(("x", tj // 128))
                        c.dma("sp", "ldx", xt[:], xrows(first, tj, 128), R=[xres], W=[xt])
                        for nn in range(4):
                            pb = pbank()
                            for k in range(16):
                                c.mm(pb.ap(128, 512), yt[:, k, j * 128:(j + 1) * 128], wo[:, k, nn * 512:(nn + 1) * 512], k == 0, k == 15, R=[yt, wo], W=pb.rs)
                            tm = tms[ntm % 2]; ntm += 1
                            c.tt("dve", tm[:], pb.ap(128, 512), g[:, nn * 512:(nn + 1) * 512], ALU.mult, R=pb.rs + [g], W=[tm])
                            c.tt("pool", xt[:, nn * 512:(nn + 1) * 512], xt[:, nn * 512:(nn + 1) * 512], tm[:], ALU.add, R=[tm, xt], W=[xt])
                        if last:
                            c.act(junk[:], xt[:], AF.Square, R=[xt], W=[junk, ss], accum=ss[:, 0:1])
                            c.act(ss[:, 1:2], ss[:, 0:1], AF.Ln, R=[ss, epsT], W=[ss], bias=epsT[:, 0:1], scale=1.0 / D)
                            c.act(ss[:, 2:3], ss[:, 1:2], AF.Exp, R=[ss], W=[ss], scale=-0.5)
                            c.stt("dve", xt[:], xt[:], ss[:, 2:3], fn[:], ALU.mult, ALU.mult, R=[xt, ss, fn], W=[xt])
                            dst = yp[tj:tj + 128, :] if tj < NPT else ys[tj - NPT:tj - NPT + 128, :]
                            c.dma("pool", "st2", dst, xt[:], R=[xt], W=[xres])
                        else:
                            dst = xw_p[tj:tj + 128, :] if tj < NPT else xw_s[tj - NPT:tj - NPT + 128, :]
                            c.dma("pool", "st2", dst, xt[:], R=[xt], W=[xres])
        c.barrier()

    def gdn_pre(layer, ph):
        i = layer // 2
        if True:
            cw = c.sb("cw", [128, 3, 24], F32, ph)
            for t in range(3):
                c.dma("sp", "ld0", cw[:, t, :], conv_e[i, t].rearrange("(k p) -> p k", p=128), W=[cw], allow_slow_non_contiguous=True)
            LM = max(s_["L"] for s_ in seqs)
            xr = [c.sb(f"xr{b}", [128, LM + 2], F32, ph) for b in range(2)]
            accs = [c.sb(f"acc{b}", [128, LM], F32, ph) for b in range(2)]
            ysls = [c.sb(f"ysl{b}", [128, LM], F32, ph) for b in range(2)]
            sqb = c.sb("sqb", [128, LM], BF16, ph)
            rinv = c.sb("rinv", [128, LM], F32, ph)
            tln = c.sb("tln", [128, 512], F32, ph)
            outb = [c.sb(f"outb{b}", [128, LM], BF16, ph) for b in range(2)]
            bbc = c.sb("bbc", [128, LM], F32, ph)
            nx = 0; no = 0
            for sq_ in seqs:
                L = sq_["L"]; t0 = sq_["t0"]
                for h in range(NH):
                    kn_keep = None
                    for which in (1, 0, 2):
                        fcx = which * 8 + h
                        x = xr[nx % 2]; acc = accs[nx % 2]; ysl = ysls[nx % 2]; nx += 1
                        c.op("pool", lambda e: e.memset(x[:, 0:1], 0.0), W=[x])
                        c.op("pool", lambda e: e.memset(x[:, L + 1:L + 2], 0.0), W=[x])
                        c.dma("sp", "ldk", x[:, 1:L + 1], qkv_s[fcx * 128:(fcx + 1) * 128, t0:t0 + L], R=[c.res("qkv")], W=[x])
                        c.act(acc[:, 0:L], x[:, 1:L + 1], AF.Identity, R=[x, cw], W=[acc], scale=cw[:, 1, fcx:fcx + 1])
                        c.stt("dve", acc[:, 0:L], x[:, 0:L], cw[:, 0, fcx:fcx + 1], acc[:, 0:L], ALU.mult, ALU.add, R=[x, cw, acc], W=[acc])
                        c.stt("dve", acc[:, 0:L], x[:, 2:L + 2], cw[:, 2, fcx:fcx + 1], acc[:, 0:L], ALU.mult, ALU.add, R=[x, cw, acc], W=[acc])
                        c.act(ysl[:, 0:L], acc[:, 0:L], AF.Silu, R=[acc], W=[ysl])
                        ob = outb[no % 2]; no += 1
                        if which == 2:
                            c.copy("act", ob[:, 0:L], ysl[:, 0:L], R=[ysl], W=[ob])
                            c.dma("pool", "st1", gv_s[h, :, t0:t0 + L], ob[:, 0:L], R=[ob], W=[c.res("gv")])
                            yield
                            continue
                        c.act(sqb[:, 0:L], ysl[:, 0:L], AF.Square, R=[ysl], W=[sqb])
                        for b0 in range(0, L, 512):
                            n = min(512, L - b0)
                            pb = pbank()
                            c.mm(pb.ap(128, n), onesb[:], sqb[:, b0:b0 + n], True, True, R=[onesb, sqb], W=pb.rs)
                            c.act(tln[:, 0:n], pb.ap(128, n), AF.Ln, R=pb.rs + [epsT], W=[tln], bias=epsT[:, 0:1], scale=1.0)
                            c.act(rinv[:, b0:b0 + n], tln[:, 0:n], AF.Exp, R=[tln], W=[rinv], scale=-0.5)
                        if which == 0:
                            c.stt("dve", ob[:, 0:L], ysl[:, 0:L], 128.0 ** -0.5, rinv[:, 0:L], ALU.mult, ALU.mult, R=[ysl, rinv], W=[ob])
                            c.dma("pool", "st1", gq_s[h, :, t0:t0 + L], ob[:, 0:L], R=[ob], W=[c.res("gq")])
                            yield
                        else:
                            c.tt("dve", ysl[:, 0:L], ysl[:, 0:L], rinv[:, 0:L], ALU.mult, R=[ysl, rinv], W=[ysl])
                            c.copy("act", ob[:, 0:L], ysl[:, 0:L], R=[ysl], W=[ob])
                            c.dma("pool", "st1", gk_s[h, :, t0:t0 + L], ob[:, 0:L], R=[ob], W=[c.res("gk")])
                            for d in range(2):
                                c.dma("sp", "ldk", bbc[:, 0:L], abT_s[16 + d * 8 + h, t0:t0 + L].partition_broadcast(128), R=[c.res("abT")], W=[bbc])
                                c.act(bbc[:, 0:L], bbc[:, 0:L], AF.Sigmoid, R=[bbc], W=[bbc])
                                ob2 = outb[no % 2]; no += 1
                                c.tt("dve", ob2[:, 0:L], ysl[:, 0:L], bbc[:, 0:L], ALU.mult, R=[ysl, bbc], W=[ob2])
                                c.dma("pool", "st1", gkb_s[d, h, :, t0:t0 + L], ob2[:, 0:L], R=[ob2], W=[c.res("gkb")])
                            yield

    def even_E2(layer):
        i = layer // 2
        with ExitStack() as ph0:
            for _ in gdn_pre(layer, ph0):
                pass
        c.barrier()
        with ExitStack() as ph:
            msk = c.sb("msk", [128, 9, 128], F32, ph)
            cum = c.sb("cum", [128, 2, 128], F32, ph)
            for m_ in range(9):
                c.dma("sp", "ld0", msk[:, m_, :], k_masks[m_], W=[msk])
            for m_ in range(2):
                c.dma("sp", "ld0", cum[:, m_, :], k_cum[m_], W=[cum])
            dtb = c.sb("dtb", [128, 16], F32, ph)
            nega = c.sb("nega", [128, 16], F32, ph)
            c.dma("sp", "ld0", dtb[:], dt_bias_e[i].partition_broadcast(128), W=[dtb])
            c.dma("sp", "ld0", nega[:], a_log_e[i].partition_broadcast(128), W=[nega])
            c.act(nega[:], nega[:], AF.Exp, R=[nega], W=[nega])
            c.ts("dve", nega[:], nega[:], -1.0, ALU.mult, R=[nega], W=[nega])
            onw = load_cols(ph, "onw", o_norm_e[i], 1)
            NCM = max(max(s_["L"] for s_ in seqs) // 128, 8)
            abT_sb = c.sb("abT_sb", [32, NCM * 128], F32, ph)
            ab_tm = c.sb("ab_tm", [128, NCM, 32], F32, ph)
            T = lambda nm: c.sb(nm, [128, NCM, 16], F32, ph)
            xg, ta, tb_, gg, beta, Gc, expG, kdsc, dec, bexpG = [T(n_) for n_ in ("xg", "ta", "tb_", "gg", "beta", "Gc", "expG", "kdsc", "dec", "bexpG")]
            fl = lambda t_: t_[:].rearrange("p a b -> p (a b)")
            GS = cfg.gs; NG = NH // GS
            G2 = lambda nm, dt: [c.sb(f"{nm}{g}", [128, GS, 128], dt, ph) for g in range(NG)]
            FL = lambda t_: t_[:].rearrange("p a b -> p (a b)")
            ld = [[c.sb(f"ld{nm}{b}", [128, NH, 128], BF16, ph) for b in range(2)] for nm in ("k", "q", "v", "kb")]
            NDT = F32 if cfg.neu32 else BF16
            gam1, gamT, gamTs, egrow, tmp32 = [G2(n_, F32) for n_ in ("gam1", "gamT", "gamTs", "egrow", "tmp32")]
            diag = tmp32
            IT8, QD8 = [G2(n_, BF16) for n_ in ("IT8", "QD8")]
            A8, AT8 = [G2(n_, NDT) for n_ in ("A8", "AT8")]
            Ad8, ATd8, AL8, Td8, Z8 = [G2(n_, NDT) for n_ in ("Ad8", "ATd8", "AL8", "Td8", "Z8")]
            PTf = G2("PTf", BF16)
            PTfb = PTf
            identn = ident if cfg.neu32 else identb
            Xa, Xb, XTa, XTb, PTa, PTb = [G2(n_, NDT) for n_ in ("Xa", "Xb", "XTa", "XTb", "PTa", "PTb")]
            KBG, KD, VB, WT, VN = [G2(n_, BF16) for n_ in ("KBG", "KD", "VB", "WT", "VN")]
            U32, S32, O32 = [G2(n_, F32) for n_ in ("U32", "S32", "O32")]
            ON32 = U32
            Sbf = G2("Sbf", BF16)
            YG = G2("YG", BF16)
            rep = {}
            for nm_, src_ in (("m01f", k_masks[4]), ("m01b", k_masks[5]), ("bd", k_masks[6]), ("ll", k_masks[7]), ("ur", k_masks[8]), ("id", k_ident)):
                t_ = c.sb("rep_" + nm_, [128, GS, 128], F32, ph)
                for hh in range(GS):
                    c.dma("sp", "ld0", t_[:, hh, :], src_, W=[t_])
                rep[nm_] = t_
            OG = c.sb("OG", [128, NH, 128], F32, ph)
            szc = c.sb("szc", [128, NH, 128], BF16, ph)
            ssq = c.sb("ssq", [128, 3, NH], F32, ph)
            junk = c.sb("junk2", [128, 128], BF16, ph)
            nld = 0
            for sq_ in seqs:
                L = sq_["L"]; t0 = sq_["t0"]; is_s = sq_["kind"] == "s"
                nch = L // 128
                NP_ = max(nch, 8)
                c.dma("sp", "ld0", abT_sb[:, 0:L], abT_s[:, t0:t0 + L], R=[c.res("abT")], W=[abT_sb])
                for c4 in range(0, nch, 4):
                    ps4 = pbank()
                    nn_ = min(4, nch - c4)
                    for q_ in range(nn_):
                        c.op("pe", lambda e: e.transpose(out=ps4.ap(128, 32, q_ * 32), in_=abT_sb[:, (c4 + q_) * 128:(c4 + q_ + 1) * 128], identity=ident[0:32, 0:32]),
                             R=[abT_sb, ident], W=ps4.rs)
                    c.copy("dve", ab_tm[:, c4:c4 + nn_, :].rearrange("p a b -> p (a b)"), ps4.ap(128, nn_ * 32), R=ps4.rs, W=[ab_tm])
                cfg.chk(22)
                if NP_ > nch:
                    c.op("pool", lambda e: e.memset(fl(gg), 0.0), W=[gg])
                for j_ in range(16):
                    c.ts("dve", xg[:, 0:nch, j_], ab_tm[:, 0:nch, j_], dtb[:, j_:j_ + 1], ALU.add, R=[ab_tm, dtb], W=[xg])
                xf, taf, tbf = xg[:, 0:nch, :], ta[:, 0:nch, :], tb_[:, 0:nch, :]
                c.ts("dve", taf, xf, -1.0, ALU.mult, R=[xg], W=[ta])
                c.tt("dve", taf, taf, xf, ALU.max, R=[ta, xg], W=[ta])
                c.act(taf, taf, AF.Exp, R=[ta], W=[ta], scale=-1.0)
                c.act(taf, taf, AF.Ln, R=[ta, oneT], W=[ta], bias=oneT[:, 0:1], scale=1.0)
                c.ts("dve", tbf, xf, 0.0, ALU.max, R=[xg], W=[tb_])
                c.tt("dve", taf, taf, tbf, ALU.add, R=[ta, tb_], W=[ta])
                for j_ in range(16):
                    c.ts("dve", gg[:, 0:nch, j_], ta[:, 0:nch, j_], nega[:, j_:j_ + 1], ALU.mult, R=[ta, nega], W=[gg])
                c.act(beta[:, 0:nch, :], ab_tm[:, 0:nch, 16:32], AF.Exp, R=[ab_tm], W=[beta], scale=-1.0)
                c.ts("dve", beta[:, 0:nch, :], beta[:, 0:nch, :], 1.0, ALU.add, R=[beta], W=[beta])
                c.op("dve", lambda e: e.reciprocal(out=beta[:, 0:nch, :], in_=beta[:, 0:nch, :]), R=[beta], W=[beta])
                pF = pbank(); pB = pbank(); pT_ = pbank()
                W_ = NP_ * 16
                c.mm(pF.ap(128, W_), cum[:, 0, :], fl(gg)[:, 0:W_], True, True, R=[cum, gg], W=pF.rs)
                c.mm(pB.ap(128, W_), cum[:, 1, :], fl(gg)[:, 0:W_], True, True, R=[cum, gg], W=pB.rs)
                c.mm(pT_.ap(128, W_), ones32[:], fl(gg)[:, 0:W_], True, True, R=[ones32, gg], W=pT_.rs)
                v3 = lambda ps_: ps_.ap(128, W_).rearrange("p (a b) -> p a b", b=16)
                c.copy("dve", Gc[:, 0:NP_, 0:8], v3(pF)[:, :, 0:8], R=pF.rs, W=[Gc])
                c.copy("dve", Gc[:, 0:NP_, 8:16], v3(pB)[:, :, 8:16], R=pB.rs, W=[Gc])
                c.act(fl(expG)[:, 0:W_], fl(Gc)[:, 0:W_], AF.Exp, R=[Gc], W=[expG])
                c.tt("dve", fl(kdsc)[:, 0:W_], pT_.ap(128, W_), fl(Gc)[:, 0:W_], ALU.subtract, R=pT_.rs + [Gc], W=[kdsc])
                c.act(fl(kdsc)[:, 0:W_], fl(kdsc)[:, 0:W_], AF.Exp, R=[kdsc], W=[kdsc])
                c.act(fl(dec)[:, 0:W_], pT_.ap(128, W_), AF.Exp, R=pT_.rs, W=[dec])
                c.tt("dve", fl(bexpG)[:, 0:W_], fl(beta)[:, 0:W_], fl(expG)[:, 0:W_], ALU.mult, R=[beta, expG], W=[bexpG])
                if sq_ is seqs[0]:
                    for nm_, t_ in (("Gc", Gc), ("expG", expG), ("kdsc", kdsc), ("dec", dec), ("bexpG", bexpG), ("beta", beta), ("gg", gg)):
                        dbg(nm_, t_[:], [128, NCM, 16], R=[t_])
                    dbg("ab_tm", ab_tm[:], [128, NCM, 32], R=[ab_tm])
                cfg.chk(23)
                for d in range(2):
                    mA, mB, m01 = (0, 1, 4) if d == 0 else (2, 3, 5)
                    for gi in range(NG):
                        for hh in range(GS):
                            h = gi * GS + hh
                            if is_s:
                                c.dma("sp", "ld0", S32[gi][:, hh, :], (sfw if d == 0 else sbw)[i, h], W=[S32[gi]])
                        if not is_s:
                            c.op("pool", lambda e: e.memset(FL(S32[gi]), 0.0), W=[S32[gi]])
                        c.copy("act", FL(Sbf[gi]), FL(S32[gi]), R=[S32[gi]], W=[Sbf[gi]])
                    order = range(nch) if d == 0 else range(nch - 1, -1, -1)
                    order_l = list(order)

                    def issue_loads(cj, slot):
                        aa, bb = t0 + cj * 128, t0 + (cj + 1) * 128
                        tk, tq, tv, tkb = [ld[x_][slot % 2] for x_ in range(4)]
                        c.dma("sp", "ldq", tk[:], gk_s.rearrange("h p l -> p h l")[:, :, aa:bb], R=[c.res("gk")], W=[tk])
                        c.dma("sp", "ldq", tq[:], gq_s.rearrange("h p l -> p h l")[:, :, aa:bb], R=[c.res("gq")], W=[tq])
                        c.dma("sp", "ldq", tv[:], gv_s.rearrange("h p l -> p h l")[:, :, aa:bb], R=[c.res("gv")], W=[tv])
                        c.dma("sp", "ldq", tkb[:], gkb_s[d].rearrange("h p l -> p h l")[:, :, aa:bb], R=[c.res("gkb")], W=[tkb])

                    issue_loads(order_l[0], nld)
                    for si, ci in enumerate(order_l):
                        a_, b_ = t0 + ci * 128, t0 + (ci + 1) * 128
                        lk, lq, lv, lkb = [ld[x_][nld % 2] for x_ in range(4)]
                        if si + 1 < len(order_l):
                            issue_loads(order_l[si + 1], nld + 1)
                        nld += 1
                        if d == 1:
                            c.dma("sp", "ldq", OG[:].rearrange("p h e -> p (h e)"), og_s[a_:b_, :], R=[c.res(("og", a_))], W=[OG])
                            c.dma("sp", "ldq", szc[:], sz_s[1024:2048, a_:b_].rearrange("(h p) l -> p h l", p=128), R=[c.res("sz")], W=[szc])
                        col = lambda t_, h: t_[:, ci, d * 8 + h:d * 8 + h + 1]
                        m01r = rep["m01f"] if d == 0 else rep["m01b"]
                        offr = rep["ll"] if d == 0 else rep["ur"]
                        GI = tuple(range(NG))
                        hsl = lambda gi: slice(gi * GS, gi * GS + GS)
                        def mm4(pb, lhs, rhs, R):
                            for hh in range(GS):
                                c.mm(pb.ap(128, 128, hh * 128), lhs(hh), rhs(hh), True, True, R=R, W=pb.rs)
                        pg = {}
                        for gi in GI:
                            for hh in range(GS):
                                c.act(diag[gi][:, hh, :], ident[:], AF.Identity, R=[ident, Gc], W=[diag[gi]], scale=col(Gc, gi * GS + hh))
                            pg[gi] = pbank()
                            mm4(pg[gi], lambda hh: ones32[:], lambda hh: diag[gi][:, hh, :], [ones32, diag[gi]])
                        for gi in GI:
                            for hh in range(GS):
                                h = gi * GS + hh
                                c.stt("dve", gam1[gi][:, hh, :], pg[gi].ap(128, 128, hh * 128), col(Gc, h), msk[:, mA, :], ALU.subtract, ALU.subtract,
                                      R=pg[gi].rs + [Gc, msk], W=[gam1[gi]])
                                c.stt("dve", gamT[gi][:, hh, :], pg[gi].ap(128, 128, hh * 128), col(Gc, h), msk[:, mB, :], ALU.subtract, ALU.add,
                                      R=pg[gi].rs + [Gc, msk], W=[gamT[gi]])
                            c.act(FL(egrow[gi]), pg[gi].ap(128, GS * 128), AF.Exp, R=pg[gi].rs, W=[egrow[gi]])
                            c.act(FL(gam1[gi]), FL(gam1[gi]), AF.Exp, R=[gam1[gi]], W=[gam1[gi]], scale=-1.0)
                            c.act(FL(gamT[gi]), FL(gamT[gi]), AF.Exp, R=[gamT[gi]], W=[gamT[gi]])
                            c.tt("pool", FL(gamTs[gi]), FL(gamT[gi]), FL(m01r), ALU.mult, R=[gamT[gi], m01r], W=[gamTs[gi]])
                        cfg.chk(24)
                        for gi in GI:
                            o4 = gi * GS
                            p1 = pbank(); mm4(p1, lambda hh: lkb[:, o4 + hh, :], lambda hh: lk[:, o4 + hh, :], [lkb, lk])
                            p2 = pbank(); mm4(p2, lambda hh: lk[:, o4 + hh, :], lambda hh: lkb[:, o4 + hh, :], [lkb, lk])
                            p3 = pbank(); mm4(p3, lambda hh: lk[:, o4 + hh, :], lambda hh: lq[:, o4 + hh, :], [lk, lq])
                            c.tt("dve", FL(A8[gi]), p1.ap(128, GS * 128), FL(gam1[gi]), ALU.mult, R=p1.rs + [gam1[gi]], W=[A8[gi]])
                            c.tt("dve", FL(AT8[gi]), p2.ap(128, GS * 128), FL(gamTs[gi]), ALU.mult, R=p2.rs + [gamTs[gi]], W=[AT8[gi]])
                            c.tt("dve", FL(IT8[gi]), p3.ap(128, GS * 128), FL(gamT[gi]), ALU.mult, R=p3.rs + [gamT[gi]], W=[IT8[gi]])
                            c.tt("pool", FL(QD8[gi]), lq[:, hsl(gi), :].rearrange("p a b -> p (a b)"), FL(egrow[gi]), ALU.mult, R=[lq, egrow[gi]], W=[QD8[gi]])
                            c.tt("pool", FL(Ad8[gi]), FL(A8[gi]), FL(rep["bd"]), ALU.mult, R=[A8[gi], rep["bd"]], W=[Ad8[gi]])
                            c.tt("pool", FL(ATd8[gi]), FL(AT8[gi]), FL(rep["bd"]), ALU.mult, R=[AT8[gi], rep["bd"]], W=[ATd8[gi]])
                            c.tt("pool", FL(AL8[gi]), FL(A8[gi]), FL(offr), ALU.mult, R=[A8[gi], offr], W=[AL8[gi]])
                            c.tt("pool", FL(PTa[gi]), FL(rep["id"]), FL(ATd8[gi]), ALU.subtract, R=[rep["id"], ATd8[gi]], W=[PTa[gi]])
                        cfg.chk(25)
                        NL = 5
                        X, XT, PT = Ad8, ATd8, PTa
                        Xn, XTn, PTn = Xa, XTa, PTb
                        for lvl in range(1, NL + 1):
                            sA, sB, sC = {}, {}, {}
                            for gi in GI:
                                sA[gi] = pbank(); mm4(sA[gi], lambda hh: XT[gi][:, hh, :], lambda hh: X[gi][:, hh, :], [XT[gi], X[gi]])
                                if lvl < NL:
                                    sB[gi] = pbank(); mm4(sB[gi], lambda hh: X[gi][:, hh, :], lambda hh: XT[gi][:, hh, :], [XT[gi], X[gi]])
                            for gi in GI:
                                c.copy("act", FL(Xn[gi]), sA[gi].ap(128, GS * 128), R=sA[gi].rs, W=[Xn[gi]])
                                if lvl < NL:
                                    c.copy("act" if lvl % 2 == 0 else "dve", FL(XTn[gi]), sB[gi].ap(128, GS * 128), R=sB[gi].rs, W=[XTn[gi]])
                            for gi in GI:
                                sC[gi] = pbank(); mm4(sC[gi], lambda hh: Xn[gi][:, hh, :], lambda hh: PT[gi][:, hh, :], [Xn[gi], PT[gi]])
                            for gi in GI:
                                c.tt("dve", FL(PTn[gi]), sC[gi].ap(128, GS * 128), FL(PT[gi]), ALU.add, R=sC[gi].rs + [PT[gi]], W=[PTn[gi]])
                            X, XT, PT = Xn, XTn, PTn
                            Xn = Xb if X is Xa else Xa
                            XTn = XTb if XT is XTa else XTa
                            PTn = PTa if PT is PTb else PTb
                        TdT = PT
                        sT, sZ, sR = {}, {}, {}
                        for gi in GI:
                            sT[gi] = pbank(); mm4(sT[gi], lambda hh: TdT[gi][:, hh, :], lambda hh: identn[:], [TdT[gi], identn])
                            sZ[gi] = pbank(); mm4(sZ[gi], lambda hh: AL8[gi][:, hh, :], lambda hh: TdT[gi][:, hh, :], [TdT[gi], AL8[gi]])
                        for gi in GI:
                            c.copy("act", FL(Td8[gi]), sT[gi].ap(128, GS * 128), R=sT[gi].rs, W=[Td8[gi]])
                            c.copy("dve", FL(Z8[gi]), sZ[gi].ap(128, GS * 128), R=sZ[gi].rs, W=[Z8[gi]])
                        for gi in GI:
                            sR[gi] = pbank(); mm4(sR[gi], lambda hh: Td8[gi][:, hh, :], lambda hh: Z8[gi][:, hh, :], [Td8[gi], Z8[gi]])
                        for gi in GI:
                            c.tt("dve", FL(PTf[gi]), FL(TdT[gi]), sR[gi].ap(128, GS * 128), ALU.subtract, R=sR[gi].rs + [TdT[gi]], W=[PTf[gi]])
                        PT = PTfb
                        cfg.chk(26)
                        for gi in GI:
                            o4 = gi * GS
                            p1 = pbank(); mm4(p1, lambda hh: lk[:, o4 + hh, :], lambda hh: identb[:], [lk, identb])
                            p2 = pbank(); mm4(p2, lambda hh: lv[:, o4 + hh, :], lambda hh: identb[:], [lv, identb])
                            for hh in range(GS):
                                c.act(KBG[gi][:, hh, :], p1.ap(128, 128, hh * 128), AF.Identity, R=p1.rs + [bexpG], W=[KBG[gi]], scale=col(bexpG, o4 + hh))
                            for hh in range(GS):
                                c.ts("dve", VB[gi][:, hh, :], p2.ap(128, 128, hh * 128), col(beta, o4 + hh), ALU.mult, R=p2.rs + [beta], W=[VB[gi]])
                            for hh in range(GS):
                                c.ts("dve", KD[gi][:, hh, :], p1.ap(128, 128, hh * 128), col(kdsc, o4 + hh), ALU.mult, R=p1.rs + [kdsc], W=[KD[gi]])
                        for gi in GI:
                            p1 = pbank(); mm4(p1, lambda hh: PT[gi][:, hh, :], lambda hh: VB[gi][:, hh, :], [PT[gi], VB[gi]])
                            p2 = pbank(); mm4(p2, lambda hh: KBG[gi][:, hh, :], lambda hh: PT[gi][:, hh, :], [PT[gi], KBG[gi]])
                            c.copy("act", FL(U32[gi]), p1.ap(128, GS * 128), R=p1.rs, W=[U32[gi]])
                            c.copy("dve", FL(WT[gi]), p2.ap(128, GS * 128), R=p2.rs, W=[WT[gi]])
                        cfg.chk(27)
                        for gi in GI:
                            p1 = pbank(); mm4(p1, lambda hh: WT[gi][:, hh, :], lambda hh: Sbf[gi][:, hh, :], [WT[gi], Sbf[gi]])
                            c.tt("dve", FL(VN[gi]), FL(U32[gi]), p1.ap(128, GS * 128), ALU.subtract, R=p1.rs + [U32[gi]], W=[VN[gi]])
                        for gi in GI:
                            o4 = gi * GS
                            p2 = pbank()
                            for hh in range(GS):
                                c.mm(p2.ap(128, 128, hh * 128), QD8[gi][:, hh, :], Sbf[gi][:, hh, :], True, False, R=[QD8[gi], Sbf[gi]], W=p2.rs)
                                c.mm(p2.ap(128, 128, hh * 128), IT8[gi][:, hh, :], VN[gi][:, hh, :], False, True, R=[IT8[gi], VN[gi]], W=p2.rs)
                            p3 = pbank(); mm4(p3, lambda hh: KD[gi][:, hh, :], lambda hh: VN[gi][:, hh, :], [KD[gi], VN[gi]])
                            for hh in range(GS):
                                c.stt("dve", S32[gi][:, hh, :], S32[gi][:, hh, :], col(dec, o4 + hh), p3.ap(128, 128, hh * 128), ALU.mult, ALU.add,
                                      R=p3.rs + [S32[gi], dec], W=[S32[gi]])
                            c.copy("act", FL(Sbf[gi]), FL(S32[gi]), R=[S32[gi]], W=[Sbf[gi]])
                            if d == 0:
                                c.copy("act", FL(O32[gi]), p2.ap(128, GS * 128), R=p2.rs, W=[O32[gi]])
                                c.dma("pool", "st1", og_s[a_:b_, o4 * 128:(o4 + GS) * 128], FL(O32[gi]), R=[O32[gi]], W=[c.res(("og", a_))])
                            else:
                                c.tt("dve", FL(O32[gi]), p2.ap(128, GS * 128), OG[:, hsl(gi), :].rearrange("p a b -> p (a b)"), ALU.add, R=p2.rs + [OG], W=[O32[gi]])
                                for hh in range(GS):
                                    c.act(junk[:], O32[gi][:, hh, :], AF.Square, R=[O32[gi]], W=[junk, ssq], accum=ssq[:, 0, o4 + hh:o4 + hh + 1])
                        if d == 1:
                            c.act(ssq[:, 1, :], ssq[:, 0, :], AF.Ln, R=[ssq, epsT], W=[ssq], bias=epsT[:, 0:1], scale=1.0 / 128)
                            c.act(ssq[:, 2, :], ssq[:, 1, :], AF.Exp, R=[ssq], W=[ssq], scale=-0.5)
                            for gi in GI:
                                o4 = gi * GS
                                for hh in range(GS):
                                    c.act(ON32[gi][:, hh, :], O32[gi][:, hh, :], AF.Identity, R=[O32[gi], ssq], W=[ON32[gi]], scale=ssq[:, 2, o4 + hh:o4 + hh + 1])
                                p1 = pbank()
                                for hh in range(GS):
                                    c.op("pe", lambda e: e.transpose(out=p1.ap(128, 128, hh * 128), in_=ON32[gi][:, hh, :], identity=ident[:]), R=[ON32[gi], ident], W=p1.rs)
                                c.stt("dve", FL(YG[gi]), p1.ap(128, GS * 128), onw[:, 0:1], szc[:, hsl(gi), :].rearrange("p a b -> p (a b)"), ALU.mult, ALU.mult,
                                      R=p1.rs + [onw, szc], W=[YG[gi]])
                                c.dma("pool", "st1", yT_s[(8 + o4) * 128:(8 + GS + o4) * 128, a_:b_].rearrange("(h p) l -> p h l", p=128), YG[gi][:], R=[YG[gi]], W=[c.res("yT")])
                    if not is_s:
                        for h in range(NH):
                            dst = (nsf if d == 0 else nsb)[sq_["idx"], i, h]
                            c.dma("pool", "st2", dst, S32[h // GS][:, h % GS, :], R=[S32[h // GS]], W=[c.res("ns")])
        c.barrier()

    def odd_O1(layer):
        i = layer // 2
        first = False
        with ExitStack() as ph:
            mods = mod_cols(ph, layer, ln_o[i])
            TB = 512
            hTs = [c.sb(f"hT{b}", [128, 16, TB], BF16, ph) for b in range(2)]
            xt = [c.sb(f"xt{b}", [128, D], F32, ph) for b in range(2)]
            ss = c.sb("ss", [128, 4], F32, ph)
            junk = c.sb("junk3", [128, 4, TB], BF16, ph)
            wg = [c.sb(f"wg{b}", [128, 16, 512], BF16, ph) for b in range(2)]
            st32 = [c.sb(f"st32_{b}", [128, TB], F32, ph) for b in range(3)]
            stz = [c.sb(f"stz{b}", [128, TB], BF16, ph) for b in range(2)]
            wv = wb_in_o[i].rearrange("(k p) e -> p k e", p=128)
            nwg = 0; nblk = 0; n32 = 0; nz = 0
            for sq_ in seqs:
                A, Bv = mods[sq_["ci"]]
                L = sq_["L"]
                for b0 in range(0, L, TB):
                    n = min(TB, L - b0)
                    t0 = sq_["t0"] + b0
                    hT = hTs[nblk % 2]; nblk += 1
                    norm_transpose((xt, junk, ss), first, t0, n, A, Bv, hT)
                    for gidx in range(8):
                        c0 = gidx * 512
                        wt = wg[nwg % 2]; nwg += 1
                        for k4 in range(4):
                            c.dma("sp", "ldw", wt[:, k4 * 4:k4 * 4 + 4, :], wv[:, k4 * 4:k4 * 4 + 4, c0:c0 + 512], R=[r_w], W=[wt])
                        for fc in range(4):
                            pb = pbank()
                            for k in range(16):
                                c.mm(pb.ap(128, n), wt[:, k, fc * 128:(fc + 1) * 128], hT[:, k, 0:n], k == 0, k == 15, R=[wt, hT], W=pb.rs)
                            s32 = st32[n32 % 3]; n32 += 1
                            if gidx < 4:
                                if fc % 2 == 0:
                                    c.copy("dve", s32[:, 0:n], pb.ap(128, n), R=pb.rs, W=[s32])
                                else:
                                    c.copy("act", s32[:, 0:n], pb.ap(128, n), R=pb.rs, W=[s32])
                                r0 = c0 + fc * 128
                                c.dma("pool", "st1", pin_s[r0:r0 + 128, t0:t0 + n], s32[:, 0:n], R=[s32], W=[c.res("pin")])
                            else:
                                sz = stz[nz % 2]; nz += 1
                                c.act(sz[:, 0:n], pb.ap(128, n), AF.Silu, R=pb.rs, W=[sz])
                                r0 = c0 - 2048 + fc * 128
                                c.dma("pool", "st1", sz_s[r0:r0 + 128, t0:t0 + n], sz[:, 0:n], R=[sz], W=[c.res("sz")])
        c.barrier()

    def odd_O2(layer):
        i = layer // 2
        with ExitStack() as ph:
            wp = c.sb("wp", [128, 4, 4, 512], BF16, ph)
            for g in range(4):
                c.dma("sp", "ld0", wp[:, g], wb_pool[i, g].rearrange("(k p) e -> p k e", p=128), R=[r_w], W=[wp])
            psc = load_cols(ph, "psc", pool_scale_o[i], 16)
            TB = 512
            HW = 8
            xr = [c.sb(f"pxr{b}", [128, TB + 2 * HW], F32, ph) for b in range(3)]
            sa = [c.sb(f"psa{b}", [128, TB + 2 * HW], F32, ph) for b in range(2)]
            pooled = [[c.sb(f"ppl{b}_{k}", [128, TB], BF16, ph) for k in range(4)] for b in range(2)]
            inv = [c.sb(f"pinv{b}", [128, TB], F32, ph) for b in range(2)]
            szt = [c.sb(f"pszt{b}", [128, TB], BF16, ph) for b in range(2)]
            yst = [c.sb(f"pyst{b}", [128, TB], BF16, ph) for b in range(2)]
            nx = 0; npl = 0; ni = 0; nz = 0
            for sq_ in seqs:
                L = sq_["L"]; ts0 = sq_["t0"]
                pinv_src = k_pinv_s if sq_["kind"] == "s" else k_pinv_p
                for b0 in range(0, L, TB):
                    n = min(TB, L - b0)
                    for g, w in enumerate((2, 4, 8, 16)):
                        iv = inv[ni % 2]; ni += 1
                        c.dma("sp", "ld0", iv[:, 0:n], pinv_src[g, b0:b0 + n].partition_broadcast(128), W=[iv])
                        pl = pooled[npl % 2]; npl += 1
                        for c4 in range(4):
                            x = xr[nx % 3]; nx += 1
                            lo = max(b0 - HW, 0); hi = min(b0 + n + HW, L)
                            if lo > b0 - HW:
                                c.op("pool", lambda e: e.memset(x[:, 0:HW], 0.0), W=[x])
                            if hi < b0 + n + HW:
                                c.op("pool", lambda e: e.memset(x[:, HW + n:HW + n + HW], 0.0), W=[x])
                            r0 = (g * 4 + c4) * 128
                            c.dma("sp", "ldk", x[:, lo - (b0 - HW):hi - (b0 - HW)], pin_s[r0:r0 + 128, ts0 + lo:ts0 + hi], R=[c.res("pin")], W=[x])
                            W_ = n + 2 * HW
                            cur = x; step = 1; length = W_
                            k_ = 0
                            while step < w:
                                dst = sa[k_ % 2]; k_ += 1
                                length -= step
                                c.tt("pool", dst[:, 0:length], cur[:, 0:length], cur[:, step:step + length], ALU.add, R=[cur], W=[dst])
                                cur = dst; step *= 2
                            o0 = HW - w // 2
                            tmpd = sa[k_ % 2]
                            c.tt("dve", tmpd[:, 0:n], cur[:, o0:o0 + n], iv[:, 0:n], ALU.mult, R=[cur, iv], W=[tmpd])
                            c.tt("dve", pl[c4][:, 0:n], tmpd[:, 0:n], x[:, HW:HW + n], ALU.subtract, R=[tmpd, x], W=[pl[c4]])
                        for e_ in range(4):
                            pb = pbank()
                            for c4 in range(4):
                                c.mm(pb.ap(128, n), wp[:, g, c4, e_ * 128:(e_ + 1) * 128], pl[c4][:, 0:n], c4 == 0, c4 == 3, R=[wp, pl[c4]], W=pb.rs)
                            fcx = g * 4 + e_
                            sz = szt[nz % 2]; ys_ = yst[nz % 2]; nz += 1
                            c.dma("sp", "ldq", sz[:, 0:n], sz_s[fcx * 128:(fcx + 1) * 128, ts0 + b0:ts0 + b0 + n], R=[c.res("sz")], W=[sz])
                            c.stt("dve", ys_[:, 0:n], pb.ap(128, n), psc[:, fcx:fcx + 1], sz[:, 0:n], ALU.mult, ALU.mult, R=pb.rs + [psc, sz], W=[ys_])
                            c.dma("pool", "st1", yT_s[fcx * 128:(fcx + 1) * 128, ts0 + b0:ts0 + b0 + n], ys_[:, 0:n], R=[ys_], W=[c.res("yT")])
        c.barrier()

    PH = {}
    PH["even_E1"] = even_E1
    PH["odd_O1"] = odd_O1
    PH["odd_O2"] = odd_O2
    PH["even_E2"] = even_E2
    PH["out_proj"] = out_proj
    PH["even_E3"] = even_E3
    def run_all():
        for layer in range(cfg.depth):
            i = layer // 2
            layer_now[0] = layer
            if layer % 2 == 0:
                even_E1(layer)
                even_E3(layer)
                even_E2(layer)
                out_proj(layer, wb_out_e[i])
            else:
                odd_O1(layer)
                odd_O2(layer)
                out_proj(layer, wb_out_o[i])
        c.barrier()
        nc.all_engine_barrier()
        c.es.close()

    PH["run_all"] = run_all
    return nc, c, PH, locals()


def make_consts(cfg):
    p = np.arange(128)[:, None]
    f = np.arange(128)[None, :]
    neg = lambda m: np.where(m, 0.0, NEG).astype(np.float32)
    bd = ((p // 64) == (f // 64)).astype(np.float32)
    ll = ((p >= 64) & (f < 64)).astype(np.float32)
    ur = ((p < 64) & (f >= 64)).astype(np.float32)
    masks = np.stack([neg(p > f), neg(f >= p), neg(f > p), neg(p >= f), (f > p).astype(np.float32), (p > f).astype(np.float32), bd, ll, ur])
    cum = np.stack([(p <= f).astype(np.float32), (p >= f).astype(np.float32)])
    LS = cfg.ls
    t = np.arange(LS)
    row = (t // 64).astype(np.float32)
    col = (t % 64).astype(np.float32)
    inv_freq = (10000.0 ** (-np.arange(0, 32, 2, dtype=np.float32) / 32)).astype(np.float32)
    cosT = np.zeros((64, LS), np.float32)
    sinS = np.zeros((64, LS), np.float32)
    for q in range(64):
        blk, half, j = q // 32, (q % 32) // 16, q % 16
        ang = (row if blk == 0 else col) * inv_freq[j]
        cosT[q] = np.cos(ang)
        sinS[q] = np.sin(ang) * (-1.0 if half == 0 else 1.0)

    def pinv(L):
        out = np.zeros((4, L), np.float32)
        for gi, w in enumerate((2, 4, 8, 16)):
            lo = np.clip(t[:L] - w // 2, 0, L) if L <= LS else None
            tt_ = np.arange(L)
            lo = np.clip(tt_ - w // 2, 0, L)
            hi = np.clip(tt_ + (w - w // 2), 0, L)
            out[gi] = 1.0 / (hi - lo).astype(np.float32)
        return out

    return dict(k_ident=np.eye(128, dtype=np.float32), k_masks=masks, k_cum=cum, k_rope=np.stack([cosT, sinS]),
                k_pinv_p=pinv(cfg.lp), k_pinv_s=pinv(cfg.ls))


WNAMES = ["ln_e", "mod_w_e", "mod_b_e", "w_in_e", "q_norm_e", "kv_norm_e", "w_uq_e", "w_ukv_e", "conv_e", "o_norm_e", "w_out_e",
          "ln_o", "mod_w_o", "mod_b_o", "w_in_o", "w_pool_o", "pool_scale_o", "w_out_o", "final_norm"]


def core_inputs(cfg, inp, j, consts):
    f = lambda a: np.ascontiguousarray(np.asarray(a, dtype=np.float32))
    m = dict(consts)
    m["xp"] = f(inp["x_prompt"][j * cfg.nps:(j + 1) * cfg.nps]).reshape(cfg.nps * cfg.lp, D)
    m["xs"] = f(inp["x_sample"][j])
    m["cckv"] = f(inp["cache_ckv"][j]); m["ckpe"] = f(inp["cache_kpe"][j])
    m["sfw"] = f(inp["state_fwd"][j]); m["sbw"] = f(inp["state_bwd"][j])
    m["cond"] = f(np.stack([np.asarray(inp["c_ctx"]), np.asarray(inp["c"][j])]))
    for k in WNAMES:
        m[k] = f(inp[k])
    m["a_log_e"] = f(inp["a_log_e"]).reshape(2, 16)
    m["dt_bias_e"] = f(inp["dt_bias_e"]).reshape(2, 16)
    return m


N_CORES = 8


def kernel(**inputs):
    cfg = Cfg(nps=4, lp=256, ls=4096, lc=256, depth=4)
    nc, c, PH, _ = build(cfg)
    PH["run_all"]()
    consts = make_consts(cfg)
    in_maps = [core_inputs(cfg, inputs, j, consts) for j in range(N_CORES)]
    res = run_bass_kernel_spmd(nc, in_maps, core_ids=list(range(N_CORES)))
    rs = res.results
    f = lambda k, shp: np.concatenate([np.asarray(r[k], dtype=np.float32).reshape(shp) for r in rs], axis=0)
    y_prompt = f("yp", (4, 256, D))
    y_sample = f("ys", (1, 4096, D))
    nckv = f("nckv", (4, 2, 256, 512))
    nkpe = f("nkpe", (4, 2, 256, 64))
    nsf = f("nsf", (4, 2, NH, 128, 128))
    nsb = f("nsb", (4, 2, NH, 128, 128))
    return (y_prompt, y_sample, nckv, nkpe, nsf, nsb)
```
